# Optimizing a Trainium2 kernel written in Bass

```python
import jax, jax.numpy as jnp
from jax import lax
import numpy as np

D_MODEL = 2048
BATCH = 2
SEQ = 4096
DEPTH = 4
DEC_BATCH = 32
DEC_SEQ = 64
PAST_LEN = 4096

CHUNK = 64
N_AB_LAYERS = (DEPTH + 1) // 2
N_C_LAYERS = DEPTH // 2
H_A = 8
HD_A = 128
Q_BLOCK = 128
FORGET_BIAS = 2.0
H_B = 8
DK_B = 128
DV_B = 128
HG_BLOCK = 16
C_DIM = D_MODEL
C_GROUPS = 8
C_LEN = 2 * CHUNK
D_FF = 5632
EPS = 1e-6
MASK_VALUE = -1e30
TINY = 1e-30

W_A = H_A * HD_A
W_B_K = H_B * DK_B
W_B_V = H_B * DV_B
AB_IN = 3 * W_A + H_A + 2 * W_B_K + 2 * W_B_V
AB_OUT = W_A + W_B_V

kernel_name = 'fox_hgrn2_gmlp_macaron_stream_step'

F32 = jnp.float32


def _rmsnorm(x, g):
    xf = x.astype(F32)
    y = xf * lax.rsqrt(jnp.mean(xf * xf, axis=-1, keepdims=True) + EPS)
    return (y * g.astype(F32)).astype(x.dtype)


def _layernorm(x, g, b):
    xf = x.astype(F32)
    mu = jnp.mean(xf, axis=-1, keepdims=True)
    xc = xf - mu
    var = jnp.mean(xc * xc, axis=-1, keepdims=True)
    return (xc * lax.rsqrt(var + EPS) * g.astype(F32) + b.astype(F32)).astype(x.dtype)


def _swiglu(h, wg, wu, wd):
    return (jax.nn.silu(h @ wg) * (h @ wu)) @ wd


def _heads(z, n_heads):
    B, T, _ = z.shape
    return z.reshape(B, T, n_heads, -1).transpose(0, 2, 1, 3)


def _merge(o):
    B, n, T, d = o.shape
    return o.transpose(0, 2, 1, 3).reshape(B, T, n * d)


def _ab_inputs(h, w_in, b_f, lb):
    z = h @ w_in
    o1 = 3 * W_A + H_A
    cuts = [W_A, 2 * W_A, 3 * W_A, o1, o1 + W_B_K, o1 + 2 * W_B_K, o1 + 2 * W_B_K + W_B_V]
    qa, ka, va, fa, qb, fb, ib, gb = jnp.split(z, cuts, axis=-1)
    logf_a = jax.nn.log_sigmoid(fa.astype(F32) + b_f.astype(F32)).transpose(0, 2, 1)
    xf = fb.astype(F32)
    lb = lb.astype(F32)
    f_b = lb + (1.0 - lb) * jax.nn.sigmoid(xf)
    g_b = jnp.log(jnp.maximum(f_b, TINY))
    k_b = (1.0 - lb) * jax.nn.sigmoid(-xf)
    return (_heads(qa, H_A), _heads(ka, H_A), _heads(va, H_A), logf_a,
            _heads(jax.nn.silu(qb), H_B), _heads(k_b, H_B), _heads(g_b, H_B), _heads(ib, H_B), gb)


def _fox_prompt(q, k, v, logf):
    B, H, S, D = q.shape
    scale = D ** -0.5
    L = jnp.cumsum(logf, axis=-1)
    nb = S // Q_BLOCK
    qb = q.reshape(B, H, nb, Q_BLOCK, D).transpose(2, 0, 1, 3, 4)
    Lq = L.reshape(B, H, nb, Q_BLOCK).transpose(2, 0, 1, 3)
    kpos = jnp.arange(S)

    def block(args):
        qi, Li, i = args
        s = jnp.einsum('bhqd,bhkd->bhqk', qi, k).astype(F32) * scale + (Li[..., :, None] - L[..., None, :])
        qpos = i * Q_BLOCK + jnp.arange(Q_BLOCK)
        s = jnp.where(kpos[None, :] <= qpos[:, None], s, MASK_VALUE)
        p = jax.nn.softmax(s, axis=-1).astype(v.dtype)
        return jnp.einsum('bhqk,bhkd->bhqd', p, v)

    o = lax.map(block, (qb, Lq, jnp.arange(nb)))
    return o.transpose(1, 2, 0, 3, 4).reshape(B, H, S, D)


def _fox_sample(q, k, v, logf, ck, cv, clogf):
    B, H, T, D = q.shape
    P = ck.shape[2]
    scale = D ** -0.5
    Lc = jnp.cumsum(clogf.astype(F32), axis=-1)
    Ln = Lc[..., -1:] + jnp.cumsum(logf, axis=-1)
    s_c = jnp.einsum('bhqd,bhkd->bhqk', q, ck).astype(F32) * scale + (Ln[..., :, None] - Lc[..., None, :])
    s_n = jnp.einsum('bhqd,bhkd->bhqk', q, k).astype(F32) * scale + (Ln[..., :, None] - Ln[..., None, :])
    s_n = jnp.where(jnp.tril(jnp.ones((T, T), bool)), s_n, MASK_VALUE)
    p = jax.nn.softmax(jnp.concatenate([s_c, s_n], axis=-1), axis=-1).astype(v.dtype)
    return jnp.einsum('bhqk,bhkd->bhqd', p[..., :P], cv) + jnp.einsum('bhqk,bhkd->bhqd', p[..., P:], v)


def _hgrn2(q, k, g, i, s0):
    B, H, T, _ = q.shape
    q, k, g, i = (a.astype(F32) for a in (q, k, g, i))
    pad = (-T) % HG_BLOCK
    if pad:
        cfg = ((0, 0), (0, 0), (0, pad), (0, 0))
        q, k, g, i = (jnp.pad(a, cfg) for a in (q, k, g, i))
    n = (T + pad) // HG_BLOCK

    def blocks(a):
        return a.reshape(B, H, n, HG_BLOCK, a.shape[-1]).transpose(2, 0, 1, 3, 4)

    tri = jnp.tril(jnp.ones((HG_BLOCK, HG_BLOCK), bool))[:, :, None]

    def step(S, inp):
        qc, kc, gc, ic = inp
        G = jnp.cumsum(gc, axis=2)
        diff = G[:, :, :, None, :] - G[:, :, None, :, :]
        dec = jnp.where(tri, jnp.exp(jnp.where(tri, diff, 0.0)), 0.0)
        A = jnp.einsum('bhtd,bhsd,bhtsd->bhts', qc, kc, dec)
        o = jnp.einsum('bhts,bhsv->bhtv', A, ic) + jnp.einsum('bhtd,bhdv->bhtv', qc * jnp.exp(G), S)
        GL = G[:, :, -1, :]
        S = jnp.exp(GL)[..., None] * S + jnp.einsum('bhsd,bhsv->bhdv', kc * jnp.exp(GL[:, :, None, :] - G), ic)
        return S, o

    S, o = lax.scan(step, s0.astype(F32), tuple(blocks(a) for a in (q, k, g, i)))
    o = o.transpose(1, 2, 0, 3, 4).reshape(B, H, n * HG_BLOCK, DV_B)[:, :, :T]
    return o, S


def _ab_output(o_a, o_b, og, gnorm, w_out):
    ob = _rmsnorm(o_b.astype(og.dtype), gnorm)
    ob = _merge(ob) * jax.nn.silu(og)
    return jnp.concatenate([_merge(o_a).astype(og.dtype), ob], axis=-1) @ w_out


def _c_mix(h, w_in, ln_g, ln_b, w_s, b_s, w_out):
    z = jax.nn.gelu(h @ w_in)
    u, v = jnp.split(z, 2, axis=-1)
    v = _layernorm(v, ln_g, ln_b)
    B, T, _ = v.shape
    Lc = min(T, C_LEN)
    n = T // Lc
    ws = jnp.tril(w_s[:, :Lc, :Lc])
    vc = v.reshape(B, n, Lc, C_GROUPS, C_DIM // C_GROUPS)
    mixed = jnp.einsum('gts,bnsgc->bntgc', ws.astype(v.dtype), vc) + b_s[:, :Lc].T[None, None, :, :, None]
    y = (u * mixed.reshape(B, T, C_DIM).astype(u.dtype)) @ w_out
    return y, v


def setup_inputs(seed: int = 0) -> dict:
    key = jax.random.key(seed)
    ks = jax.random.split(key, 32)
    d = D_MODEL

    def nrm(k, shape, scale):
        return jax.random.normal(k, shape, F32) * scale

    return {
        'x_prompt': nrm(ks[0], (BATCH, SEQ, d), 1.0),
        'x_sample': nrm(ks[1], (DEC_BATCH, DEC_SEQ, d), 1.0),
        'cache_k': nrm(ks[2], (N_AB_LAYERS, DEC_BATCH, H_A, PAST_LEN, HD_A), 1.0),
        'cache_v': nrm(ks[3], (N_AB_LAYERS, DEC_BATCH, H_A, PAST_LEN, HD_A), 1.0),
        'cache_logf': jax.nn.log_sigmoid(FORGET_BIAS + nrm(ks[4], (N_AB_LAYERS, DEC_BATCH, H_A, PAST_LEN), 1.0)),
        'state_hgrn': nrm(ks[5], (N_AB_LAYERS, DEC_BATCH, H_B, DK_B, DV_B), 1.0),
        'norm_ffn1': 1.0 + nrm(ks[6], (DEPTH, d), 0.02),
        'ffn1_gate': nrm(ks[7], (DEPTH, d, D_FF), d ** -0.5),
        'ffn1_up': nrm(ks[8], (DEPTH, d, D_FF), d ** -0.5),
        'ffn1_down': nrm(ks[9], (DEPTH, D_FF, d), D_FF ** -0.5),
        'norm_mix': 1.0 + nrm(ks[10], (DEPTH, d), 0.02),
        'ab_w_in': nrm(ks[11], (N_AB_LAYERS, d, AB_IN), d ** -0.5),
        'ab_b_f': FORGET_BIAS + nrm(ks[12], (N_AB_LAYERS, H_A), 0.1),
        'hgrn_lb': 1.0 + nrm(ks[13], (N_AB_LAYERS, W_B_K), 0.1),
        'hgrn_gnorm': 1.0 + nrm(ks[14], (N_AB_LAYERS, DV_B), 0.02),
        'ab_w_out': nrm(ks[15], (N_AB_LAYERS, AB_OUT, d), AB_OUT ** -0.5),
        'c_w_in': nrm(ks[16], (N_C_LAYERS, d, 2 * C_DIM), d ** -0.5),
        'c_ln_g': 1.0 + nrm(ks[17], (N_C_LAYERS, C_DIM), 0.02),
        'c_ln_b': nrm(ks[18], (N_C_LAYERS, C_DIM), 0.02),
        'c_w_s': nrm(ks[19], (N_C_LAYERS, C_GROUPS, C_LEN, C_LEN), C_LEN ** -0.5),
        'c_b_s': 1.0 + nrm(ks[20], (N_C_LAYERS, C_GROUPS, C_LEN), 0.02),
        'c_w_out': nrm(ks[21], (N_C_LAYERS, C_DIM, d), C_DIM ** -0.5),
        'norm_ffn2': 1.0 + nrm(ks[22], (DEPTH, d), 0.02),
        'ffn2_gate': nrm(ks[23], (DEPTH, d, D_FF), d ** -0.5),
        'ffn2_up': nrm(ks[24], (DEPTH, d, D_FF), d ** -0.5),
        'ffn2_down': nrm(ks[25], (DEPTH, D_FF, d), D_FF ** -0.5),
        'norm_final': 1.0 + nrm(ks[26], (d,), 0.02),
    }


def reference(x_prompt, x_sample, cache_k, cache_v, cache_logf, state_hgrn,
              norm_ffn1, ffn1_gate, ffn1_up, ffn1_down, norm_mix,
              ab_w_in, ab_b_f, hgrn_lb, hgrn_gnorm, ab_w_out,
              c_w_in, c_ln_g, c_ln_b, c_w_s, c_b_s, c_w_out,
              norm_ffn2, ffn2_gate, ffn2_up, ffn2_down, norm_final):
    sm = jax.nn.softmax(hgrn_lb.astype(F32), axis=0)
    lower_bounds = jnp.cumsum(sm, axis=0) - sm[0:1]

    xp, xs = x_prompt, x_sample
    pk, pv, plf, ps = [], [], [], []
    sk, sv, slf, ss, scv = [], [], [], [], []
    for l in range(DEPTH):
        j = l // 2
        xp = xp + 0.5 * _swiglu(_rmsnorm(xp, norm_ffn1[l]), ffn1_gate[l], ffn1_up[l], ffn1_down[l])
        xs = xs + 0.5 * _swiglu(_rmsnorm(xs, norm_ffn1[l]), ffn1_gate[l], ffn1_up[l], ffn1_down[l])
        hp = _rmsnorm(xp, norm_mix[l])
        hs = _rmsnorm(xs, norm_mix[l])
        if l % 2 == 0:
            qa, ka, va, lfa, qb, kb, gb, ib, og = _ab_inputs(hp, ab_w_in[j], ab_b_f[j], lower_bounds[j])
            oa = _fox_prompt(qa, ka, va, lfa)
            s0 = jnp.zeros((xp.shape[0], H_B, DK_B, DV_B), F32)
            ob, s_fin = _hgrn2(qb, kb, gb, ib, s0)
            xp = xp + _ab_output(oa, ob, og, hgrn_gnorm[j], ab_w_out[j])
            pk.append(ka)
            pv.append(va)
            plf.append(lfa)
            ps.append(s_fin)
            qa, ka, va, lfa, qb, kb, gb, ib, og = _ab_inputs(hs, ab_w_in[j], ab_b_f[j], lower_bounds[j])
            oa = _fox_sample(qa, ka, va, lfa, cache_k[j], cache_v[j], cache_logf[j])
            ob, s_fin = _hgrn2(qb, kb, gb, ib, state_hgrn[j])
            xs = xs + _ab_output(oa, ob, og, hgrn_gnorm[j], ab_w_out[j])
            sk.append(ka)
            sv.append(va)
            slf.append(lfa)
            ss.append(s_fin)
        else:
            yc, _ = _c_mix(hp, c_w_in[j], c_ln_g[j], c_ln_b[j], c_w_s[j], c_b_s[j], c_w_out[j])
            xp = xp + yc
            yc, v_rows = _c_mix(hs, c_w_in[j], c_ln_g[j], c_ln_b[j], c_w_s[j], c_b_s[j], c_w_out[j])
            xs = xs + yc
            scv.append(v_rows)
        xp = xp + 0.5 * _swiglu(_rmsnorm(xp, norm_ffn2[l]), ffn2_gate[l], ffn2_up[l], ffn2_down[l])
        xs = xs + 0.5 * _swiglu(_rmsnorm(xs, norm_ffn2[l]), ffn2_gate[l], ffn2_up[l], ffn2_down[l])

    y_prompt = _rmsnorm(xp, norm_final)
    y_sample = _rmsnorm(xs, norm_final)
    new_k_prompt = jnp.stack(pk)
    new_v_prompt = jnp.stack(pv)
    new_logf_prompt = jnp.stack(plf)
    new_hgrn_prompt = jnp.stack(ps)
    new_k_sample = jnp.stack(sk)
    new_v_sample = jnp.stack(sv)
    new_logf_sample = jnp.stack(slf)
    new_hgrn_sample = jnp.stack(ss)
    new_cv_sample = jnp.stack(scv)
    return (y_prompt, y_sample, new_k_prompt, new_v_prompt, new_logf_prompt, new_hgrn_prompt,
            new_k_sample, new_v_sample, new_logf_sample, new_hgrn_sample, new_cv_sample)
```

```python
import numpy as np
from contextlib import ExitStack
import concourse.bass as bass
import concourse.mybir as mybir
from concourse.bass_utils import run_bass_kernel_spmd

F32 = mybir.dt.float32
BF16 = mybir.dt.bfloat16
AF = mybir.ActivationFunctionType
ALU = mybir.AluOpType
AX = mybir.AxisListType

D = 2048
DFF = 5632
NCORE = 8
KC = D // 128
EPS = 1e-6
MAXT = 9
NTMAX = MAXT * 128
SEQ = 4096
NCHUNK = 4
H = 8
HD = 128
AB_IN = 7176
TINY = 1e-30
SCALE = HD ** -0.5


class Chunk:
    def __init__(self, c):
        self.c = c
        self.has_sample = c < 2
        self.ntile = 9 if self.has_sample else 8
        self.ntok = self.ntile * 128
        self.p0 = c * 1024
        self.gt0 = c * 8

    def groups(self):
        g = [(0, 512), (512, 512)]
        if self.has_sample:
            g.append((1024, 128))
        return g


_UID = [0]


def U(name):
    _UID[0] += 1
    return f"{name}_{_UID[0]}"


class Buf:
    __slots__ = ("name", "last_w", "readers", "sem", "dcnt")

    def __init__(self, name):
        self.name = name
        self.last_w = None
        self.readers = {}
        self.sem = None
        self.dcnt = 0


class Stream:
    def __init__(self, eng, name, sem, is_pe=False):
        self.eng = eng
        self.name = name
        self.sem = sem
        self.cnt = 0
        self.waited = {}
        self.is_pe = is_pe


class Prog:
    def __init__(self, nc, es):
        self.nc = nc
        self.es = es
        self.sems = {}
        mk = lambda n: es.enter_context(nc.semaphore(n))
        self.pe = Stream(nc.tensor, "pe", mk("s_pe"), is_pe=True)
        self.act = Stream(nc.scalar, "act", mk("s_act"))
        self.dve = Stream(nc.vector, "dve", mk("s_dve"))
        self.pool = Stream(nc.gpsimd, "pool", mk("s_pool"))
        self.sp = Stream(nc.sync, "sp", mk("s_sp"))
        self.streams = [self.pe, self.act, self.dve, self.pool, self.sp]
        self.semobj = {}
        for s in self.streams:
            self.semobj[id(s.sem)] = s
        self.dma_bufs = []
        self.bufs = {}

    def buf(self, name):
        b = self.bufs.get(name)
        if b is None:
            b = Buf(name)
            self.bufs[name] = b
        return b

    def _collect(self, reads, writes, extra=()):
        evs = {}

        def add(ev):
            if ev is None:
                return
            k = id(ev[0])
            if k not in evs or evs[k][1] < ev[1]:
                evs[k] = ev
        for b in reads:
            add(b.last_w)
        for b in writes:
            add(b.last_w)
            for ev in b.readers.values():
                add(ev)
        for ev in extra:
            add(ev)
        return evs

    def _wait(self, st, evs):
        for k, (sem, val) in evs.items():
            if sem is st.sem:
                if st.is_pe or val > st.cnt:
                    continue
            if st.waited.get(k, 0) >= val:
                continue
            st.eng.wait_ge(sem, val)
            st.waited[k] = val

    def _record(self, ev, reads, writes):
        k = id(ev[0])
        for b in reads:
            old = b.readers.get(k)
            if old is None or old[1] < ev[1]:
                b.readers[k] = ev
        for b in writes:
            b.last_w = ev
            b.readers = {}

    def op(self, st, fn, reads=(), writes=(), signal=True, extra=()):
        self._wait(st, self._collect(reads, writes, extra))
        ins = fn(st.eng)
        if signal:
            st.cnt += 1
            ins.then_inc(st.sem, 1)
            ev = (st.sem, st.cnt)
        else:
            ev = (st.sem, st.cnt + 1)
        self._record(ev, reads, writes)
        return ev

    def dma(self, st, out, in_, chan, reads=(), writes=(), extra=()):
        if chan.sem is None:
            chan.sem = self.es.enter_context(self.nc.semaphore("d_" + chan.name))
            self.dma_bufs.append(chan)
        self._wait(st, self._collect(reads, writes, extra))
        ins = st.eng.dma_start(out=out, in_=in_)
        chan.dcnt += 16
        ins.then_inc(chan.sem, 16)
        ev = (chan.sem, chan.dcnt)
        self._record(ev, reads, writes)
        return ev

    def barrier(self):
        evs = {}
        for s in self.streams:
            if s.cnt:
                evs[id(s.sem)] = (s.sem, s.cnt)
        for b in self.dma_bufs:
            evs[id(b.sem)] = (b.sem, b.dcnt)
        for s in self.streams:
            self._wait(s, dict(evs))

    def finish(self):
        evs = {}
        for b in self.dma_bufs:
            evs[id(b.sem)] = (b.sem, b.dcnt)
        for s in self.streams:
            if s.cnt:
                evs[id(s.sem)] = (s.sem, s.cnt)
        self._wait(self.sp, evs)


class K:
    pass


def setup_persistent(P, k, es, nc):
    sb = lambda name, shape, dt: es.enter_context(nc.sbuf_tensor(U(name), shape, dt))
    k.x = sb("x", [128, MAXT, D], F32)
    k.xb = [P.buf(f"x{t}") for t in range(MAXT)]
    k.hT = sb("hT", [128, KC, NTMAX], BF16)
    k.hTb = [P.buf(f"hT{t}") for t in range(MAXT)]
    k.xnb = P.buf("xn")
    k.ident = sb("ident", [128, 128], F32)
    k.identb = P.buf("ident")
    k.maskU = sb("maskU", [128, 128], F32)
    k.maskUb = P.buf("maskU")
    k.ones_f = sb("ones_f", [128, 128], F32)
    k.ones_b = sb("ones_b", [128, 128], BF16)
    k.BU64 = sb("BU64", [128, 128], F32)
    k.BR64 = sb("BR64", [128, 128], F32)
    k.cstb = P.buf("consts")
    k.Lhist = sb("Lhist", [128, 2, 32, H], F32)
    k.Lcarry = sb("Lcarry", [128, 2, H], F32)
    k.Lhb = [P.buf("Lhist0"), P.buf("Lhist1")]
    k.gam = sb("gam_sb", [128, 13, KC], F32)
    k.gamb = P.buf("gam")
    k.small = sb("small", [128, 64], F32)
    k.ssb = P.buf("ss")
    k.rsb = P.buf("rs")
    k.statb = P.buf("stat")
    k.psum = [es.enter_context(nc.psum_tensor(f"ps{i}", [128, 512], F32)) for i in range(8)]
    k.psb = [P.buf(f"ps{i}") for i in range(8)]


def make_identity(P, k, nc):
    P.op(P.pool, lambda e: e.memset(k.ident[:], 0.0), writes=[k.identb])
    P.op(P.pool, lambda e: e.affine_select(out=k.ident[:], in_=k.ident[:], pattern=[[-1, 128]],
                                           compare_op=ALU.not_equal, fill=1.0, base=0, channel_multiplier=1),
         reads=[k.identb], writes=[k.identb])
    P.op(P.pool, lambda e: e.memset(k.maskU[:], 1.0), writes=[k.maskUb])
    P.op(P.pool, lambda e: e.affine_select(out=k.maskU[:], in_=k.maskU[:], pattern=[[1, 128]],
                                           compare_op=ALU.is_ge, fill=0.0, base=0, channel_multiplier=-1),
         reads=[k.maskUb], writes=[k.maskUb])
    P.op(P.pool, lambda e: e.memset(k.ones_f[:], 1.0), writes=[k.cstb])
    P.op(P.pool, lambda e: e.memset(k.ones_b[:], 1.0), writes=[k.cstb])
    P.op(P.pool, lambda e: e.tensor_copy(out=k.BU64[:], in_=k.maskU[:]), reads=[k.maskUb], writes=[k.cstb])
    P.op(P.pool, lambda e: e.memset(k.BU64[0:64, 64:128], 0.0), writes=[k.cstb])
    P.op(P.pool, lambda e: e.memset(k.BR64[:], 1.0), writes=[k.cstb])
    P.op(P.pool, lambda e: e.affine_select(out=k.BR64[:], in_=k.BR64[:], pattern=[[-1, 128]],
                                           compare_op=ALU.is_gt, fill=0.0, base=0, channel_multiplier=1),
         reads=[k.cstb], writes=[k.cstb])
    P.op(P.pool, lambda e: e.memset(k.BR64[64:128, 0:64], 0.0), writes=[k.cstb])
    P.op(P.pool, lambda e: e.memset(k.Lcarry[:], 0.0), writes=k.Lhb)


def rmsnorm_to_hT(P, k, nc, gidx, ch, xn):
    hT, hTb, xnb, ssb_col = k.hT, k.hTb, k.xnb, 0
    for t in range(ch.ntile):
        ss = k.small[:, ssb_col + 0:ssb_col + 1]
        rs = k.small[:, ssb_col + 1:ssb_col + 2]
        P.op(P.act, lambda e: e.activation(out=xn[:], in_=k.x[:, t, :], func=AF.Square, accum_out=ss),
             reads=[k.xb[t]], writes=[xnb, k.ssb])
        P.op(P.dve, lambda e: e.tensor_scalar(out=rs, in0=ss, scalar1=1.0 / D, scalar2=EPS,
                                               op0=ALU.mult, op1=ALU.add),
             reads=[k.ssb], writes=[k.rsb])
        P.op(P.act, lambda e: e.activation(out=rs, in_=rs, func=AF.Sqrt), reads=[k.rsb], writes=[k.rsb])
        P.op(P.dve, lambda e: e.reciprocal(out=rs, in_=rs), reads=[k.rsb], writes=[k.rsb])
        P.op(P.dve, lambda e: e.tensor_scalar(out=xn[:], in0=k.x[:, t, :], scalar1=rs, scalar2=None,
                                               op0=ALU.mult),
             reads=[k.xb[t], k.rsb], writes=[xnb])
        for g4 in range(KC // 4):
            pi = g4 % 2
            ps = k.psum[pi]
            for j in range(4):
                c = g4 * 4 + j
                P.op(P.pe, lambda e: e.transpose(out=ps[:, j * 128:(j + 1) * 128],
                                                 in_=xn[:, c * 128:(c + 1) * 128], identity=k.ident[:]),
                     reads=[xnb, k.identb], writes=[k.psb[pi]], signal=(j == 3))
            for j in range(4):
                c = g4 * 4 + j
                eng = P.dve if j % 2 == 0 else P.act
                if eng is P.dve:
                    P.op(eng, lambda e: e.tensor_scalar(out=hT[:, c, t * 128:(t + 1) * 128],
                                                        in0=ps[:, j * 128:(j + 1) * 128],
                                                        scalar1=k.gam[:, gidx, c:c + 1], scalar2=None,
                                                        op0=ALU.mult),
                         reads=[k.psb[pi], k.gamb], writes=[hTb[t]])
                else:
                    P.op(eng, lambda e: e.activation(out=hT[:, c, t * 128:(t + 1) * 128],
                                                     in_=ps[:, j * 128:(j + 1) * 128],
                                                     func=AF.Copy, scale=k.gam[:, gidx, c:c + 1]),
                         reads=[k.psb[pi], k.gamb], writes=[hTb[t]])


FB = 256
NB = DFF // FB


def alloc_ffn_bufs(P, k, es, nc):
    sb = lambda name, shape, dt: es.enter_context(nc.sbuf_tensor(U(name), shape, dt))
    b = K()
    b.hT, b.hTb, b.xnb = k.hT, k.hTb, k.xnb
    b.xn = sb("xn", [128, D], F32)
    b.wgu = [sb(f"wgu{i}", [128, KC, FB], BF16) for i in range(4)]
    b.wgub = [P.buf(f"wgu{i}") for i in range(4)]
    b.wd = [sb(f"wd{i}", [128, FB // 128, D], BF16) for i in range(2)]
    b.wdb = [P.buf(f"wd{i}") for i in range(2)]
    b.actT = [sb(f"actT{i}", [128, FB // 128, NTMAX], BF16) for i in range(2)]
    b.actTb = [P.buf(f"actT{i}") for i in range(2)]
    b.stmp = [sb(f"stmp{i}", [128, 512], BF16) for i in range(2)]
    b.stmpb = [P.buf(f"stmp{i}") for i in range(2)]
    return b


def ffn_phase(P, k, nc, wg, wu, wd, gidx, ch):
    with ExitStack() as es:
        b = alloc_ffn_bufs(P, k, es, nc)
        ffn(P, k, nc, wg, wu, wd, gidx, b, ch)
        P.barrier()


def ffn(P, k, nc, wg, wu, wd, gidx, b, ch):
    rmsnorm_to_hT(P, k, nc, gidx, ch, b.xn)
    wg_v = wg.rearrange("(c p) n -> p c n", p=128)
    wu_v = wu.rearrange("(c p) n -> p c n", p=128)
    wd_v = wd.rearrange("(c p) n -> p c n", p=128)

    def load(blk):
        sg = (2 * blk) % 4
        su = (2 * blk + 1) % 4
        sd = blk % 2
        P.dma(P.pool, b.wgu[sg][:], wg_v[:, :, blk * FB:(blk + 1) * FB], b.wgub[sg], writes=[b.wgub[sg]])
        P.dma(P.pool, b.wgu[su][:], wu_v[:, :, blk * FB:(blk + 1) * FB], b.wgub[su], writes=[b.wgub[su]])
        P.dma(P.pool, b.wd[sd][:], wd_v[:, blk * 2:blk * 2 + 2, :], b.wdb[sd], writes=[b.wdb[sd]])

    load(0)
    pcount = 0
    for blk in range(NB):
        if blk + 1 < NB:
            load(blk + 1)
        sg = (2 * blk) % 4
        su = (2 * blk + 1) % 4
        sd = blk % 2
        a = blk % 2
        for m in range(FB // 128):
            for (t0, tn) in ch.groups():
                pg = 0 + (pcount % 2)
                pu = 2 + (pcount % 2)
                st = pcount % 2
                pcount += 1
                tiles = [b.hTb[t] for t in range(t0 // 128, (t0 + tn) // 128)]
                for c in range(KC):
                    P.op(P.pe, lambda e: e.matmul(k.psum[pg][:, 0:tn], lhsT=b.wgu[sg][:, c, m * 128:(m + 1) * 128],
                                                  rhs=b.hT[:, c, t0:t0 + tn], start=(c == 0), stop=(c == KC - 1)),
                         reads=[b.wgub[sg]] + tiles, writes=[k.psb[pg]], signal=(c == KC - 1))
                for c in range(KC):
                    P.op(P.pe, lambda e: e.matmul(k.psum[pu][:, 0:tn], lhsT=b.wgu[su][:, c, m * 128:(m + 1) * 128],
                                                  rhs=b.hT[:, c, t0:t0 + tn], start=(c == 0), stop=(c == KC - 1)),
                         reads=[b.wgub[su]] + tiles, writes=[k.psb[pu]], signal=(c == KC - 1))
                P.op(P.act, lambda e: e.activation(out=b.stmp[st][:, 0:tn], in_=k.psum[pg][:, 0:tn], func=AF.Silu),
                     reads=[k.psb[pg]], writes=[b.stmpb[st]])
                P.op(P.dve, lambda e: e.tensor_tensor(out=b.actT[a][:, m, t0:t0 + tn], in0=k.psum[pu][:, 0:tn],
                                                      in1=b.stmp[st][:, 0:tn], op=ALU.mult),
                     reads=[k.psb[pu], b.stmpb[st]], writes=[b.actTb[a]])
        dcount = 0
        for t in range(ch.ntile):
            for q in range(4):
                pd = 4 + (dcount % 4)
                dcount += 1
                for m in range(FB // 128):
                    P.op(P.pe, lambda e: e.matmul(k.psum[pd][:, :], lhsT=b.actT[a][:, m, t * 128:(t + 1) * 128],
                                                  rhs=b.wd[sd][:, m, q * 512:(q + 1) * 512],
                                                  start=(m == 0), stop=(m == FB // 128 - 1)),
                         reads=[b.actTb[a], b.wdb[sd]], writes=[k.psb[pd]], signal=(m == FB // 128 - 1))
                P.op(P.dve, lambda e: e.scalar_tensor_tensor(out=k.x[:, t, q * 512:(q + 1) * 512], in0=k.psum[pd][:, :],
                                                             scalar=0.5, in1=k.x[:, t, q * 512:(q + 1) * 512],
                                                             op0=ALU.mult, op1=ALU.add),
                     reads=[k.psb[pd], k.xb[t]], writes=[k.xb[t]])


GELU_C = 1.5957691216057308


def gelu_from_psum(P, ps, psb, n, dst, dstb, tmp, tmpb):
    P.op(P.act, lambda e: e.activation(out=tmp[:, :n], in_=ps[:, :n], func=AF.Square), reads=[psb], writes=[tmpb])
    P.op(P.dve, lambda e: e.tensor_scalar(out=tmp[:, :n], in0=tmp[:, :n], scalar1=0.044715, scalar2=1.0,
                                           op0=ALU.mult, op1=ALU.add), reads=[tmpb], writes=[tmpb])
    P.op(P.dve, lambda e: e.tensor_tensor(out=tmp[:, :n], in0=ps[:, :n], in1=tmp[:, :n], op=ALU.mult),
         reads=[psb, tmpb], writes=[tmpb])
    P.op(P.act, lambda e: e.activation(out=tmp[:, :n], in_=tmp[:, :n], func=AF.Sigmoid, scale=GELU_C),
         reads=[tmpb], writes=[tmpb])
    P.op(P.dve, lambda e: e.tensor_tensor(out=dst, in0=ps[:, :n], in1=tmp[:, :n], op=ALU.mult),
         reads=[psb, tmpb], writes=[dstb])


def cmix(P, k, nc, w_in, ln_g, ln_b, w_s, b_s, w_out, cv_out, gidx, ch):
    NTILE = ch.ntile
    xnb = k.xnb
    is_s = lambda t: ch.has_sample and t == 8
    with ExitStack() as es:
        sb = lambda name, shape, dt: es.enter_context(nc.sbuf_tensor(U(name), shape, dt))
        xn = sb("c_xn", [128, D], F32)
        rmsnorm_to_hT(P, k, nc, gidx, ch, xn)
        P.barrier()
        vn = sb("c_vn", [128, MAXT, D], BF16)
        vnb = [P.buf(f"c_vn{t}") for t in range(MAXT)]
        wsT = [sb("c_wsTp", [128, 8, 128], BF16), sb("c_wsTs", [128, 8, 128], BF16)]
        wsTb = P.buf("c_wsT")
        bsb = [sb("c_bsp", [128, 8, 128], F32), sb("c_bss", [128, 8, 128], F32)]
        bsbb = P.buf("c_bs")
        gtmp = [xn[:, 0:512], xn[:, 512:1024]]
        gtmpb = [P.buf(f"c_gtmp{i}") for i in range(2)]
        stat = k.small
        with ExitStack() as es1:
            stg = [es1.enter_context(nc.sbuf_tensor(U(f"c_stg{i}"), [128, 128], F32)) for i in range(2)]
            stgb = [P.buf(f"c_stg{i}") for i in range(2)]
            P.op(P.pool, lambda e: e.memset(stg[1][:], 0.0), writes=[stgb[1]])
            for g in range(8):
                P.dma(P.sp, stg[0][:], w_s[g, :, :], stgb[0], writes=[stgb[0]])
                P.dma(P.sp, stg[1][0:64, 0:64], w_s[g, 0:64, 0:64], stgb[1], writes=[stgb[1]])
                P.dma(P.sp, stg[1][64:128, 64:128], w_s[g, 0:64, 0:64], stgb[1], writes=[stgb[1]])
                for v in range(2):
                    P.op(P.pe, lambda e: e.transpose(out=k.psum[v][:, 0:128], in_=stg[v][:], identity=k.ident[:]),
                         reads=[stgb[v], k.identb], writes=[k.psb[v]])
                    P.op(P.dve, lambda e: e.tensor_tensor(out=wsT[v][:, g, :], in0=k.psum[v][:, 0:128], in1=k.maskU[:],
                                                          op=ALU.mult),
                         reads=[k.psb[v], k.maskUb], writes=[wsTb])
            P.dma(P.sp, bsb[0][:], b_s.partition_broadcast(128), bsbb, writes=[bsbb])
            for hh in range(2):
                P.dma(P.sp, bsb[1][:, :, hh * 64:(hh + 1) * 64], b_s[:, 0:64].partition_broadcast(128), bsbb, writes=[bsbb])
            P.barrier()
        with ExitStack() as es2:
            sb2 = lambda name, shape, dt: es2.enter_context(nc.sbuf_tensor(U(name), shape, dt))
            ring = [sb2(f"c_ring{i}", [128, KC, 256], BF16) for i in range(2)]
            ringb = [P.buf(f"c_ring{i}") for i in range(2)]
            HD2 = D // 2
            lng = sb2("c_lng", [128, HD2], F32)
            lnb = sb2("c_lnb", [128, HD2], F32)
            lnbuf = P.buf("c_ln")
            w_v = w_in.rearrange("(c p) n -> p c n", p=128)
            NVB = D // 256
            P.dma(P.pool, ring[0][:], w_v[:, :, D:D + 256], ringb[0], writes=[ringb[0]])
            cnt = 0
            for cb in range(NVB):
                if cb + 1 < NVB:
                    s1 = (cb + 1) % 2
                    P.dma(P.pool, ring[s1][:], w_v[:, :, D + (cb + 1) * 256:D + (cb + 2) * 256], ringb[s1], writes=[ringb[s1]])
                s = cb % 2
                for t in range(NTILE):
                    pi = cnt % 4
                    gi = cnt % 2
                    cnt += 1
                    for c in range(KC):
                        P.op(P.pe, lambda e: e.matmul(k.psum[pi][:, 0:256], lhsT=k.hT[:, c, t * 128:(t + 1) * 128],
                                                      rhs=ring[s][:, c, :], start=(c == 0), stop=(c == KC - 1)),
                             reads=[k.hTb[t], ringb[s]], writes=[k.psb[pi]], signal=(c == KC - 1))
                    gelu_from_psum(P, k.psum[pi], k.psb[pi], 256, vn[:, t, cb * 256:(cb + 1) * 256], vnb[t],
                                   gtmp[gi], gtmpb[gi])
            P.barrier()
            stat = k.small
            for t in range(NTILE):
                s1 = stat[:, 8:9]
                s2 = stat[:, 9:10]
                msq = stat[:, 11:12]
                mean = stat[:, 16 + t:17 + t]
                rstd = stat[:, 32 + t:33 + t]
                P.op(P.act, lambda e: e.activation(out=xn[:], in_=vn[:, t, :], func=AF.Copy, accum_out=s1),
                     reads=[vnb[t]], writes=[xnb, k.statb])
                P.op(P.act, lambda e: e.activation(out=xn[:], in_=vn[:, t, :], func=AF.Square, accum_out=s2),
                     reads=[vnb[t]], writes=[xnb, k.statb])
                P.op(P.dve, lambda e: e.tensor_scalar(out=mean, in0=s1, scalar1=1.0 / D, scalar2=None, op0=ALU.mult),
                     reads=[k.statb], writes=[k.statb])
                P.op(P.dve, lambda e: e.tensor_tensor(out=msq, in0=mean, in1=mean, op=ALU.mult),
                     reads=[k.statb], writes=[k.statb])
                P.op(P.dve, lambda e: e.scalar_tensor_tensor(out=rstd, in0=s2, scalar=1.0 / D, in1=msq,
                                                             op0=ALU.mult, op1=ALU.subtract),
                     reads=[k.statb], writes=[k.statb])
                P.op(P.dve, lambda e: e.tensor_scalar(out=rstd, in0=rstd, scalar1=EPS, scalar2=None, op0=ALU.add),
                     reads=[k.statb], writes=[k.statb])
                P.op(P.act, lambda e: e.activation(out=rstd, in_=rstd, func=AF.Sqrt), reads=[k.statb], writes=[k.statb])
                P.op(P.dve, lambda e: e.reciprocal(out=rstd, in_=rstd), reads=[k.statb], writes=[k.statb])
            for half in range(2):
                hs = slice(half * HD2, (half + 1) * HD2)
                P.dma(P.sp, lng[:], ln_g[hs].partition_broadcast(128), lnbuf, writes=[lnbuf])
                P.dma(P.sp, lnb[:], ln_b[hs].partition_broadcast(128), lnbuf, writes=[lnbuf])
                for t in range(NTILE):
                    mean = stat[:, 16 + t:17 + t]
                    rstd = stat[:, 32 + t:33 + t]
                    xh = xn[:, 0:HD2]
                    P.op(P.dve, lambda e: e.tensor_scalar(out=xh, in0=vn[:, t, hs], scalar1=mean, scalar2=rstd,
                                                          op0=ALU.subtract, op1=ALU.mult),
                         reads=[vnb[t], k.statb], writes=[xnb])
                    P.op(P.dve, lambda e: e.tensor_tensor(out=xh, in0=xh, in1=lng[:], op=ALU.mult),
                         reads=[xnb, lnbuf], writes=[xnb])
                    if not is_s(t):
                        P.op(P.dve, lambda e: e.tensor_tensor(out=vn[:, t, hs], in0=xh, in1=lnb[:], op=ALU.add),
                             reads=[xnb, lnbuf], writes=[vnb[t]])
                    else:
                        P.op(P.dve, lambda e: e.tensor_tensor(out=xh, in0=xh, in1=lnb[:], op=ALU.add),
                             reads=[xnb, lnbuf], writes=[xnb])
                        ts_ = ch.c
                        P.dma(P.sp, cv_out[ts_ * 128:(ts_ + 1) * 128, hs], xh, xnb, reads=[xnb])
                        P.op(P.act, lambda e: e.activation(out=vn[:, t, hs], in_=xh, func=AF.Copy),
                             reads=[xnb], writes=[vnb[t]])
            P.barrier()
        with ExitStack() as es3:
            sb3 = lambda name, shape, dt: es3.enter_context(nc.sbuf_tensor(U(name), shape, dt))
            ring = [sb3(f"c_uring{i}", [128, KC, 128], BF16) for i in range(2)]
            ringb = [P.buf(f"c_uring{i}") for i in range(2)]
            wo = [sb3(f"c_wo{i}", [128, D], BF16) for i in range(2)]
            wob = [P.buf(f"c_wo{i}") for i in range(2)]
            uT = [sb3(f"c_uT{i}", [128, NTMAX], BF16) for i in range(2)]
            uTb = [P.buf(f"c_uT{i}") for i in range(2)]
            pT, pTb = uT, uTb
            mt = [sb3(f"c_mt{i}", [128, 128], F32) for i in range(2)]
            mtb = [P.buf(f"c_mt{i}") for i in range(2)]
            w_v = w_in.rearrange("(c p) n -> p c n", p=128)

            def load(cb):
                s = cb % 2
                P.dma(P.pool, ring[s][:], w_v[:, :, cb * 128:(cb + 1) * 128], ringb[s], writes=[ringb[s]])
                P.dma(P.pool, wo[s][:], w_out[cb * 128:(cb + 1) * 128, :], wob[s], writes=[wob[s]])
            load(0)
            cnt = 0
            mcnt = 0
            dcnt = 0
            for cb in range(KC):
                if cb + 1 < KC:
                    load(cb + 1)
                s = cb % 2
                g = cb // 2
                for (t0, tn) in ch.groups():
                    pi = cnt % 2
                    cnt += 1
                    tiles = [k.hTb[t] for t in range(t0 // 128, (t0 + tn) // 128)]
                    for c in range(KC):
                        P.op(P.pe, lambda e: e.matmul(k.psum[pi][:, 0:tn], lhsT=ring[s][:, c, :], rhs=k.hT[:, c, t0:t0 + tn],
                                                      start=(c == 0), stop=(c == KC - 1)),
                             reads=[ringb[s]] + tiles, writes=[k.psb[pi]], signal=(c == KC - 1))
                    gelu_from_psum(P, k.psum[pi], k.psb[pi], tn, uT[s][:, t0:t0 + tn], uTb[s], gtmp[pi], gtmpb[pi])
                for t in range(NTILE):
                    v = 1 if is_s(t) else 0
                    pm = 2 + (mcnt % 2)
                    mi = mcnt % 2
                    mcnt += 1
                    P.op(P.pe, lambda e: e.matmul(k.psum[pm][:, 0:128], lhsT=vn[:, t, cb * 128:(cb + 1) * 128],
                                                  rhs=wsT[v][:, g, :], start=True, stop=True),
                         reads=[vnb[t], wsTb], writes=[k.psb[pm]])
                    P.op(P.dve, lambda e: e.tensor_tensor(out=mt[mi][:], in0=k.psum[pm][:, 0:128], in1=bsb[v][:, g, :],
                                                          op=ALU.add),
                         reads=[k.psb[pm], bsbb], writes=[mtb[mi]])
                    P.op(P.dve, lambda e: e.tensor_tensor(out=pT[s][:, t * 128:(t + 1) * 128], in0=mt[mi][:],
                                                          in1=uT[s][:, t * 128:(t + 1) * 128], op=ALU.mult),
                         reads=[mtb[mi], uTb[s]], writes=[pTb[s]])
                for t in range(NTILE):
                    for q in range(4):
                        pd = 4 + (dcnt % 4)
                        dcnt += 1
                        P.op(P.pe, lambda e: e.matmul(k.psum[pd][:, :], lhsT=pT[s][:, t * 128:(t + 1) * 128],
                                                      rhs=wo[s][:, q * 512:(q + 1) * 512], start=True, stop=True),
                             reads=[pTb[s], wob[s]], writes=[k.psb[pd]])
                        P.op(P.dve, lambda e: e.tensor_tensor(out=k.x[:, t, q * 512:(q + 1) * 512], in0=k.psum[pd][:, :],
                                                              in1=k.x[:, t, q * 512:(q + 1) * 512], op=ALU.add),
                             reads=[k.psb[pd], k.xb[t]], writes=[k.xb[t]])
            P.barrier()


DBG = dict(fox=True, samp=True, hgrn=True, outp=True, heads=8, nproj=7, store=True)


def abmix(P, k, nc, j, ch, io, gidx):
    c, p0, gt0, NTL = ch.c, ch.p0, ch.gt0, ch.ntile
    HS = ch.has_sample
    w_in = io.ab_w_in[j].rearrange("(c p) n -> p c n", p=128)
    w_out = io.ab_w_out[j]
    with ExitStack() as es0:
        xn = es0.enter_context(nc.sbuf_tensor(U("a_xn"), [128, D], F32))
        rmsnorm_to_hT(P, k, nc, gidx, ch, xn)
        P.barrier()
    with ExitStack() as es:
        def sbt(name, shape, dt):
            return es.enter_context(nc.sbuf_tensor(U(name), shape, dt)), P.buf(name)
        NR = 5
        kTall, kTallb = sbt("a_kTall", [128, SEQ], BF16)
        wr = [sbt(f"a_wr{i}", [128, KC, 128], BF16) for i in range(NR)]
        wfa, wfab = sbt("a_wfa", [128, KC, H], BF16)
        wo, wob = sbt("a_wo", [128, 2, D], BF16)
        qT, qTb = sbt("a_qT", [128, NTMAX], BF16)
        kT32, kT32b = sbt("a_kT32", [128, 512], F32)
        vall, vallb = sbt("a_vall", [128, 32, 128], BF16)
        kTs, kTsb = sbt("a_kTs", [128, 128], BF16)
        vs, vsb = sbt("a_vs", [128, 128], BF16)
        kc32 = [sbt(f"a_kc32_{i}", [128, 4, 128], F32) for i in range(2)]
        lf, lfb = sbt("a_lf", [128, MAXT, H], F32)
        lfT = [sbt(f"a_lfT{i}", [H, 128], F32) for i in range(2)]
        bfb, bfbb = sbt("a_bfb", [128, H], F32)
        Lref, Lrefb = sbt("a_Lref", [128, 8, H], F32)
        Bn, Bnb = sbt("a_Bn", [128, H], F32)
        Bq, Bqb = sbt("a_Bq", [128, 32], F32)
        pT = [sbt(f"a_pT{i}", [128, 128], BF16) for i in range(3)]
        rec, recb = sbt("a_rec", [128, 128], F32)
        oaT, oaTb = sbt("a_oaT", [128, NTMAX], BF16)
        obT, obTb = sbt("a_obT", [128, NTMAX], BF16)
        st32 = [sbt(f"a_st{i}", [128, 128], F32) for i in range(4)]
        lfc_r, lfc_rb = sbt("a_lfcr", [32, 128], F32)
        lfc, lfcb = sbt("a_lfc", [128, 4, 32], F32)
        qbs, qbsb = sbt("a_qbs", [128, NTMAX], BF16)
        g_tm, g_tmb = sbt("a_gtm", [128, MAXT, 128], F32)
        k_tm, k_tmb = sbt("a_ktm", [128, MAXT, 128], F32)
        i_bf, i_bfb = sbt("a_ibf", [128, MAXT, 128], BF16)
        gate, gateb = sbt("a_gate", [128, MAXT, 128], BF16)
        lbt, lbtb = sbt("a_lbt", [128, 2, 128], F32)
        lbo, lbob = sbt("a_lbo", [128, 2, 128], F32)
        gnb, gnbb = sbt("a_gnb", [128, 128], F32)
        tE, tEb = sbt("a_tE", [128, 128], F32)
        tK, tKb = sbt("a_tK", [128, 128], F32)
        tG, tGb = sbt("a_tG", [128, 128], F32)
        tR, tRb = sbt("a_tR", [128, 128], F32)
        rcol, rcolb = sbt("a_rcol", [128, 4], F32)
        qp, qpb = sbt("a_qp", [128, 128], BF16)
        kp, kpb = sbt("a_kp", [128, 128], BF16)
        qpp, qppb = sbt("a_qpp", [128, 128], BF16)
        kpp, kppb = sbt("a_kpp", [128, 128], BF16)
        ATm, ATmb = sbt("a_ATm", [128, 128], BF16)
        S32 = [sbt(f"a_S32_{i}", [128, 128], F32) for i in range(2)]
        Sbf = [sbt(f"a_Sbf_{i}", [128, 128], BF16) for i in range(2)]
        on, onb = sbt("a_on", [128, 128], F32)
        PS, PB = k.psum, k.psb
        kthb, vthb, sthb = P.buf(f"kth{j}"), P.buf(f"vth{j}"), P.buf(f"sth{j}")
        stat = k.small

        P.dma(P.pool, wfa[:], w_in[:, :, 3072:3080], wfab, writes=[wfab])
        P.dma(P.sp, bfb[:], io.ab_b_f[j].partition_broadcast(128), bfbb, writes=[bfbb])
        P.dma(P.sp, gnb[:], io.hgrn_gnorm[j].partition_broadcast(128), gnbb, writes=[gnbb])
        for t in range(NTL):
            for kc in range(KC):
                P.op(P.pe, lambda e: e.matmul(PS[0][:, t * H:(t + 1) * H], lhsT=k.hT[:, kc, t * 128:(t + 1) * 128],
                                              rhs=wfa[:, kc, :], start=(kc == 0), stop=(kc == KC - 1)),
                     reads=[k.hTb[t], wfab], writes=[PB[0]], signal=(kc == KC - 1))
            P.op(P.dve, lambda e: e.tensor_tensor(out=lf[:, t, :], in0=PS[0][:, t * H:(t + 1) * H], in1=bfb[:], op=ALU.add),
                 reads=[PB[0], bfbb], writes=[lfb])
        lfl = lf[:, 0:NTL, :]
        P.op(P.act, lambda e: e.activation(out=lfl, in_=lfl, func=AF.Exp, scale=-1.0), reads=[lfb], writes=[lfb])
        P.op(P.dve, lambda e: e.tensor_scalar(out=lfl, in0=lfl, scalar1=1.0, scalar2=None, op0=ALU.add), reads=[lfb], writes=[lfb])
        P.op(P.act, lambda e: e.activation(out=lfl, in_=lfl, func=AF.Ln), reads=[lfb], writes=[lfb])
        P.op(P.dve, lambda e: e.tensor_scalar(out=lfl, in0=lfl, scalar1=-1.0, scalar2=None, op0=ALU.mult), reads=[lfb], writes=[lfb])
        for t in range(NTL):
            lt, ltb = lfT[t % 2]
            P.op(P.pe, lambda e: e.transpose(out=PS[1][0:H, 0:128], in_=lf[:, t, :], identity=k.ident[:]),
                 reads=[lfb, k.identb], writes=[PB[1]])
            P.op(P.act, lambda e: e.activation(out=lt[:], in_=PS[1][0:H, 0:128], func=AF.Copy),
                 reads=[PB[1]], writes=[ltb])
            if HS and t == 8:
                for s_ in range(2):
                    P.dma(P.sp, io.logf_s[j, 2 * c + s_, :, :], lt[:, 64 * s_:64 * s_ + 64], ltb, reads=[ltb])
            else:
                P.dma(P.sp, io.logf_p[j, :, p0 + t * 128:p0 + (t + 1) * 128], lt[:], ltb, reads=[ltb])
        lfp = lf[:, 0:8, :]
        P.op(P.pe, lambda e: e.matmul(PS[2][:, 0:64], lhsT=k.maskU[:], rhs=lfp, start=True, stop=True),
             reads=[lfb, k.maskUb], writes=[PB[2]])
        P.op(P.pe, lambda e: e.matmul(PS[3][:, 0:64], lhsT=k.ones_f[:], rhs=lfp, start=True, stop=True),
             reads=[lfb, k.cstb], writes=[PB[3]])
        for t in range(8):
            prev = k.Lcarry[:, j, :] if t == 0 else Lref[:, t - 1, :]
            P.op(P.dve, lambda e: e.tensor_tensor(out=k.Lhist[:, j, gt0 + t, :], in0=PS[2][:, t * H:(t + 1) * H], in1=prev, op=ALU.add),
                 reads=[PB[2], k.Lhb[j], Lrefb], writes=[k.Lhb[j]])
            P.op(P.dve, lambda e: e.tensor_tensor(out=Lref[:, t, :], in0=PS[3][:, t * H:(t + 1) * H], in1=prev, op=ALU.add),
                 reads=[PB[3], k.Lhb[j], Lrefb], writes=[Lrefb])
        P.op(P.dve, lambda e: e.tensor_copy(out=k.Lcarry[:, j, :], in_=Lref[:, 7, :]), reads=[Lrefb], writes=[k.Lhb[j]])
        if HS:
            P.op(P.pe, lambda e: e.matmul(PS[2][:, 64:64 + H], lhsT=k.BU64[:], rhs=lf[:, 8, :], start=True, stop=True),
                 reads=[lfb, k.cstb], writes=[PB[2]])
            P.op(P.dve, lambda e: e.tensor_scalar(out=Bn[:], in0=PS[2][:, 64:64 + H], scalar1=-1.0, scalar2=None, op0=ALU.mult),
                 reads=[PB[2]], writes=[Bnb])

        COLS = [0, 1024, 2048, 3080, 4104, 5128, 6152]
        nload = [0]

        def load_next():
            n = nload[0]
            if n >= 7 * H:
                return
            nload[0] += 1
            hh, bi = divmod(n, 7)
            c0 = COLS[bi] + hh * 128
            if DBG.get('dbgcol') and bi == 6:
                c0 = COLS[5]
            if True:
                for hf in range(2):
                    P.dma(P.pool, wr[n % NR][0][:, hf * 8:(hf + 1) * 8, :], w_in[:, hf * 8:(hf + 1) * 8, c0:c0 + 128], wr[n % NR][1], writes=[wr[n % NR][1]])
                return
            P.dma(P.pool, wr[n % NR][0][:], w_in[:, :, c0:c0 + 128], wr[n % NR][1], writes=[wr[n % NR][1]])

        def slot_of(h, bi):
            return wr[(h * 7 + bi) % NR]
        cntB = [0]
        cntA = [0]

        def projT(w, wb, evac):
            for (t0, tn) in ch.groups():
                pb = cntB[0] % 2
                cntB[0] += 1
                tiles = [k.hTb[t] for t in range(t0 // 128, (t0 + tn) // 128)]
                for kc in range(KC):
                    P.op(P.pe, lambda e: e.matmul(PS[pb][:, 0:tn], lhsT=w[:, kc, :], rhs=k.hT[:, kc, t0:t0 + tn],
                                                  start=(kc == 0), stop=(kc == KC - 1)),
                         reads=[wb] + tiles, writes=[PB[pb]], signal=(kc == KC - 1))
                evac(PS[pb], PB[pb], t0, tn)
            if not DBG.get('noload') and nload[0] < DBG.get('maxload', 999):
                load_next()

        def projA(w, wb, evac):
            for t in range(NTL):
                pb = 2 + cntA[0] % 2
                cntA[0] += 1
                for kc in range(KC):
                    P.op(P.pe, lambda e: e.matmul(PS[pb][:, 0:128], lhsT=k.hT[:, kc, t * 128:(t + 1) * 128], rhs=w[:, kc, :],
                                                  start=(kc == 0), stop=(kc == KC - 1)),
                         reads=[wb, k.hTb[t]], writes=[PB[pb]], signal=(kc == KC - 1))
                evac(PS[pb], PB[pb], t)
            load_next()

        for _ in range(NR):
            load_next()
        for h in range(DBG['heads']):
            P.dma(P.sp, lbt[:, 0, :], io.hgrn_lb[0, h * 128:(h + 1) * 128].partition_broadcast(128), lbtb, writes=[lbtb])
            P.dma(P.sp, lbt[:, 1, :], io.hgrn_lb[1, h * 128:(h + 1) * 128].partition_broadcast(128), lbtb, writes=[lbtb])
            P.op(P.dve, lambda e: e.tensor_tensor(out=lbo[:, 1, :], in0=lbt[:, 1, :], in1=lbt[:, 0, :], op=ALU.subtract),
                 reads=[lbtb], writes=[lbob])
            P.op(P.act, lambda e: e.activation(out=lbo[:, 1, :], in_=lbo[:, 1, :], func=AF.Sigmoid), reads=[lbob], writes=[lbob])
            if j == 0:
                P.op(P.dve, lambda e: e.tensor_scalar(out=lbo[:, 0, :], in0=lbo[:, 1, :], scalar1=0.0, scalar2=None, op0=ALU.mult),
                     reads=[lbob], writes=[lbob])
            else:
                P.op(P.dve, lambda e: e.tensor_copy(out=lbo[:, 0, :], in_=lbo[:, 1, :]), reads=[lbob], writes=[lbob])
            P.op(P.dve, lambda e: e.tensor_scalar(out=lbo[:, 1, :], in0=lbo[:, 0, :], scalar1=-1.0, scalar2=1.0,
                                                  op0=ALU.mult, op1=ALU.add), reads=[lbob], writes=[lbob])
            if c > 0:
                P.dma(P.sp, kTall[:, 0:p0], io.kth[j, h, :, 0:p0], kTallb, reads=[kthb], writes=[kTallb])
                P.dma(P.sp, vall[:, 0:gt0, :], io.vth[j, h, 0:p0, :].rearrange("(t p) d -> p t d", p=128), vallb,
                      reads=[vthb], writes=[vallb])
            def out_rows(dst_p, dst_s, src, srcb, t):
                if HS and t == 8:
                    for s_ in range(2):
                        P.dma(P.sp, dst_s[j, 2 * c + s_, h, :, :], src[64 * s_:64 * s_ + 64, :], srcb, reads=[srcb])
                else:
                    P.dma(P.sp, dst_p[j, h, p0 + t * 128:p0 + (t + 1) * 128, :], src[:], srcb, reads=[srcb])

            wq, wqb = slot_of(h, 0)

            def ev_q(ps, psb, t0, tn):
                P.op(P.act, lambda e: e.activation(out=qT[:, t0:t0 + tn], in_=ps[:, 0:tn], func=AF.Copy, scale=SCALE),
                     reads=[psb], writes=[qTb])
            if DBG['nproj'] > 0 and not DBG.get('skipq'):
                projT(wq, wqb, ev_q)
            wk, wkb = slot_of(h, 1)

            def ev_k(ps, psb, t0, tn):
                P.op(P.act, lambda e: e.activation(out=kT32[:, 0:tn], in_=ps[:, 0:tn], func=AF.Copy),
                     reads=[psb], writes=[kT32b])
                for tt in range(tn // 128 if DBG.get('kout', True) else 0):
                    t = t0 // 128 + tt
                    pb2 = 2 + cntA[0] % 2
                    cntA[0] += 1
                    sk, skb = st32[2 + t % 2]
                    P.op(P.pe, lambda e: e.transpose(out=PS[pb2][:, 0:128], in_=kT32[:, tt * 128:(tt + 1) * 128], identity=k.ident[:]),
                         reads=[kT32b, k.identb], writes=[PB[pb2]])
                    P.op(P.act, lambda e: e.activation(out=sk[:], in_=PS[pb2][:, 0:128], func=AF.Copy), reads=[PB[pb2]], writes=[skb])
                    out_rows(io.k_p, io.k_s, sk, skb, t)
                if not DBG.get('kdve', True):
                    pass
                elif t0 < 1024:
                    P.op(P.act, lambda e: e.activation(out=(oaT[:, t0:t0 + tn] if DBG.get('dst') else kTall[:, p0 + t0:p0 + t0 + tn]), in_=ps[:, 0:tn], func=AF.Copy),
                         reads=[psb], writes=[kTallb])
                else:
                    P.op(P.act, lambda e: e.activation(out=kTs[:], in_=ps[:, 0:tn], func=AF.Copy), reads=[psb], writes=[kTsb])
            if DBG['nproj'] > 1:
                projT(wk, wkb, ev_k)
            wv, wvb = slot_of(h, 2)

            def ev_v(ps, psb, t):
                sv, svb = st32[t % 2]
                P.op(P.act, lambda e: e.activation(out=sv[:], in_=ps[:, 0:128], func=AF.Copy), reads=[psb], writes=[svb])
                if HS and t == 8:
                    P.op(P.act, lambda e: e.activation(out=vs[:], in_=ps[:, 0:128], func=AF.Copy), reads=[psb], writes=[vsb])
                else:
                    P.op(P.act, lambda e: e.activation(out=vall[:, gt0 + t, :], in_=ps[:, 0:128], func=AF.Copy), reads=[psb], writes=[vallb])
                out_rows(io.v_p, io.v_s, sv, svb, t)
            if DBG['nproj'] > 2:
                projA(wv, wvb, ev_v)
            wqb_, wqbb = slot_of(h, 3)

            def ev_qb(ps, psb, t0, tn):
                P.op(P.act, lambda e: e.activation(out=qbs[:, t0:t0 + tn], in_=ps[:, 0:tn], func=AF.Silu), reads=[psb], writes=[qbsb])
            if DBG['nproj'] > 3:
                projT(wqb_, wqbb, ev_qb)
            wf, wfb_ = slot_of(h, 4)

            def ev_fb(ps, psb, t):
                P.op(P.act, lambda e: e.activation(out=tE[:], in_=ps[:, 0:128], func=AF.Sigmoid), reads=[psb], writes=[tEb])
                P.op(P.dve, lambda e: e.tensor_tensor(out=tK[:], in0=tE[:], in1=lbo[:, 1, :], op=ALU.mult),
                     reads=[tEb, lbob], writes=[tKb])
                P.op(P.dve, lambda e: e.tensor_tensor(out=k_tm[:, t, :], in0=lbo[:, 1, :], in1=tK[:], op=ALU.subtract),
                     reads=[tKb, lbob], writes=[k_tmb])
                P.op(P.dve, lambda e: e.tensor_tensor(out=tE[:], in0=tK[:], in1=lbo[:, 0, :], op=ALU.add),
                     reads=[tKb, lbob], writes=[tEb])
                P.op(P.dve, lambda e: e.tensor_scalar(out=tE[:], in0=tE[:], scalar1=TINY, scalar2=None, op0=ALU.max),
                     reads=[tEb], writes=[tEb])
                P.op(P.act, lambda e: e.activation(out=g_tm[:, t, :], in_=tE[:], func=AF.Ln), reads=[tEb], writes=[g_tmb])
            if DBG['nproj'] > 4:
                projA(wf, wfb_, ev_fb)
            wi, wib = slot_of(h, 5)

            def ev_ib(ps, psb, t):
                P.op(P.act, lambda e: e.activation(out=i_bf[:, t, :], in_=ps[:, 0:128], func=AF.Copy), reads=[psb], writes=[i_bfb])
            if DBG['nproj'] > 5:
                projA(wi, wib, ev_ib)
            wg_, wgb_ = slot_of(h, 6)

            def ev_gb(ps, psb, t):
                P.op(P.act, lambda e: e.activation(out=gate[:, t, :], in_=ps[:, 0:128], func=AF.Silu), reads=[psb], writes=[gateb])
            if DBG['nproj'] > 6:
                projA(wg_, wgb_, ev_gb)
            P.dma(P.pool, wo[:, 0, :], w_out[h * 128:(h + 1) * 128, :], wob, writes=[wob])
            P.dma(P.pool, wo[:, 1, :], w_out[1024 + h * 128:1024 + (h + 1) * 128, :], wob, writes=[wob])
            if c < NCHUNK - 1 and DBG['store']:
                P.dma(P.sp, io.kth[j, h, :, p0:p0 + 1024], kTall[:, p0:p0 + 1024], kthb, reads=[kTallb], writes=[kthb])
                P.dma(P.sp, io.vth[j, h, p0:p0 + 1024, :].rearrange("(t p) d -> p t d", p=128), vall[:, gt0:gt0 + 8, :], vthb,
                      reads=[vallb], writes=[vthb])

            pcnt = 0
            for iq in range(8 if DBG['fox'] else 0):
                gi = gt0 + iq
                P.op(P.dve, lambda e: e.tensor_scalar(out=Bq[:, 0:gi + 1], in0=k.Lhist[:, j, 0:gi + 1, h], scalar1=Lref[:, iq, h:h + 1],
                                                      scalar2=-1.0, op0=ALU.subtract, op1=ALU.mult),
                     reads=[k.Lhb[j], Lrefb], writes=[Bqb])
                for jj in range(gi + 1):
                    pb = 4 + pcnt % 2
                    pt, ptb = pT[pcnt % 3]
                    pcnt += 1
                    P.op(P.pe, lambda e: e.matmul(PS[pb][:, 0:128], lhsT=kTall[:, jj * 128:(jj + 1) * 128],
                                                  rhs=qT[:, iq * 128:(iq + 1) * 128], start=True, stop=True),
                         reads=[kTallb, qTb], writes=[PB[pb]])
                    P.op(P.act, lambda e: e.activation(out=pt[:], in_=PS[pb][:, 0:128], func=AF.Exp, bias=Bq[:, jj:jj + 1]),
                         reads=[PB[pb], Bqb], writes=[ptb])
                    if jj == gi:
                        P.op(P.dve, lambda e: e.tensor_tensor(out=pt[:], in0=pt[:], in1=k.maskU[:], op=ALU.mult),
                             reads=[ptb, k.maskUb], writes=[ptb])
                    P.op(P.pe, lambda e: e.matmul(PS[6][:, 0:128], lhsT=vall[:, jj, :], rhs=pt[:], start=(jj == 0), stop=(jj == gi)),
                         reads=[vallb, ptb], writes=[PB[6]], signal=False)
                    P.op(P.pe, lambda e: e.matmul(PS[7][:, 0:128], lhsT=k.ones_b[:], rhs=pt[:], start=(jj == 0), stop=(jj == gi)),
                         reads=[k.cstb, ptb], writes=[PB[7]])
                P.op(P.dve, lambda e: e.reciprocal(out=rec[:], in_=PS[7][:, 0:128]), reads=[PB[7]], writes=[recb])
                P.op(P.dve, lambda e: e.tensor_tensor(out=oaT[:, iq * 128:(iq + 1) * 128], in0=PS[6][:, 0:128], in1=rec[:], op=ALU.mult),
                     reads=[PB[6], recb], writes=[oaTb])

            if HS and DBG['samp']:
                for s_ in range(2):
                    b = 2 * c + s_
                    qs = qT[:, 1024 + 64 * s_:1024 + 64 * s_ + 64]
                    P.dma(P.pool, vall[:], io.cache_v[j, b, h, :, :].rearrange("(t p) d -> p t d", p=128), vallb, writes=[vallb])
                    for r in range(8):
                        kc_, kcb = kc32[r % 2]
                        P.dma(P.sp, kc_[:], io.cache_k[j, b, h, r * 512:(r + 1) * 512, :].rearrange("(t p) d -> p t d", p=128),
                              kcb, writes=[kcb])
                        pb = r % 2
                        for q4 in range(4):
                            P.op(P.pe, lambda e: e.transpose(out=PS[pb][:, q4 * 128:(q4 + 1) * 128], in_=kc_[:, q4, :], identity=k.ident[:]),
                                 reads=[kcb, k.identb], writes=[PB[pb]], signal=(q4 == 3))
                        P.op(P.act, lambda e: e.activation(out=kTall[:, r * 512:(r + 1) * 512], in_=PS[pb][:, :], func=AF.Copy),
                             reads=[PB[pb]], writes=[kTallb])
                    P.dma(P.sp, lfc_r[:], io.cache_logf[j, b, h, :].rearrange("(t p) -> t p", p=128), lfc_rb, writes=[lfc_rb])
                    P.op(P.pe, lambda e: e.transpose(out=PS[2][:, 0:32], in_=lfc_r[:], identity=k.ident[0:32, 0:32]),
                         reads=[lfc_rb, k.identb], writes=[PB[2]])
                    P.op(P.act, lambda e: e.activation(out=lfc[:, 3, :], in_=PS[2][:, 0:32], func=AF.Copy), reads=[PB[2]], writes=[lfcb])
                    P.op(P.pe, lambda e: e.matmul(PS[2][:, 0:32], lhsT=k.maskU[:], rhs=lfc[:, 3, :], start=True, stop=True),
                         reads=[lfcb, k.maskUb], writes=[PB[2]])
                    P.op(P.pe, lambda e: e.matmul(PS[3][:, 0:32], lhsT=k.ones_f[:], rhs=lfc[:, 3, :], start=True, stop=True),
                         reads=[lfcb, k.cstb], writes=[PB[3]])
                    P.op(P.act, lambda e: e.activation(out=lfc[:, 1, :], in_=PS[3][:, 0:32], func=AF.Copy), reads=[PB[3]], writes=[lfcb])
                    P.op(P.dve, lambda e: e.tensor_tensor_scan(out=lfc[:, 2, :], data0=k.ones_f[:, 0:32], data1=lfc[:, 1, :], initial=0.0,
                                                               op0=ALU.mult, op1=ALU.add),
                         reads=[lfcb, k.cstb], writes=[lfcb])
                    P.op(P.dve, lambda e: e.tensor_tensor(out=lfc[:, 0, :], in0=lfc[:, 2, :], in1=lfc[:, 1, :], op=ALU.subtract),
                         reads=[lfcb], writes=[lfcb])
                    P.op(P.dve, lambda e: e.tensor_tensor(out=lfc[:, 0, :], in0=PS[2][:, 0:32], in1=lfc[:, 0, :], op=ALU.add),
                         reads=[lfcb, PB[2]], writes=[lfcb])
                    P.op(P.dve, lambda e: e.tensor_scalar(out=lfc[:, 3, :], in0=lfc[:, 0, :], scalar1=lfc[:, 2, 31:32], scalar2=-1.0,
                                                          op0=ALU.subtract, op1=ALU.mult),
                         reads=[lfcb], writes=[lfcb])
                    for jj in range(33):
                        pb = 4 + pcnt % 2
                        pt, ptb = pT[pcnt % 3]
                        pcnt += 1
                        if jj < 32:
                            P.op(P.pe, lambda e: e.matmul(PS[pb][:, 0:64], lhsT=kTall[:, jj * 128:(jj + 1) * 128], rhs=qs, start=True, stop=True),
                                 reads=[kTallb, qTb], writes=[PB[pb]])
                            P.op(P.act, lambda e: e.activation(out=pt[:, 0:64], in_=PS[pb][:, 0:64], func=AF.Exp, bias=lfc[:, 3, jj:jj + 1]),
                                 reads=[PB[pb], lfcb], writes=[ptb])
                            P.op(P.pe, lambda e: e.matmul(PS[6][:, 0:64], lhsT=vall[:, jj, :], rhs=pt[:, 0:64], start=(jj == 0), stop=False),
                                 reads=[vallb, ptb], writes=[PB[6]], signal=False)
                            P.op(P.pe, lambda e: e.matmul(PS[7][:, 0:64], lhsT=k.ones_b[:], rhs=pt[:, 0:64], start=(jj == 0), stop=False),
                                 reads=[k.cstb, ptb], writes=[PB[7]])
                        else:
                            r0, r1 = 64 * s_, 64 * s_ + 64
                            P.op(P.pe, lambda e: e.matmul(PS[pb][r0:r1, 0:64], lhsT=kTs[:, r0:r1], rhs=qs, start=True, stop=True),
                                 reads=[kTsb, qTb], writes=[PB[pb]])
                            P.op(P.act, lambda e: e.activation(out=pt[r0:r1, 0:64], in_=PS[pb][r0:r1, 0:64], func=AF.Exp, bias=Bn[r0:r1, h:h + 1]),
                                 reads=[PB[pb], Bnb], writes=[ptb])
                            P.op(P.dve, lambda e: e.tensor_tensor(out=pt[r0:r1, 0:64], in0=pt[r0:r1, 0:64], in1=k.maskU[r0:r1, r0:r1], op=ALU.mult),
                                 reads=[ptb, k.maskUb], writes=[ptb])
                            P.op(P.pe, lambda e: e.matmul(PS[6][:, 0:64], lhsT=vs[r0:r1, :], rhs=pt[r0:r1, 0:64], start=False, stop=True),
                                 reads=[vsb, ptb], writes=[PB[6]], signal=False)
                            P.op(P.pe, lambda e: e.matmul(PS[7][:, 0:64], lhsT=k.ones_b[r0:r1, :], rhs=pt[r0:r1, 0:64], start=False, stop=True),
                                 reads=[k.cstb, ptb], writes=[PB[7]])
                    P.op(P.dve, lambda e: e.reciprocal(out=rec[:, 0:64], in_=PS[7][:, 0:64]), reads=[PB[7]], writes=[recb])
                    P.op(P.dve, lambda e: e.tensor_tensor(out=oaT[:, 1024 + 64 * s_:1024 + 64 * s_ + 64], in0=PS[6][:, 0:64], in1=rec[:, 0:64], op=ALU.mult),
                         reads=[PB[6], recb], writes=[oaTb])

            (S0, S0b), (S1, S1b) = S32
            (Sb0, Sb0b), (Sb1, Sb1b) = Sbf
            if c == 0:
                P.op(P.pool, lambda e: e.memset(S0[:], 0.0), writes=[S0b])
            else:
                P.dma(P.sp, S0[:], io.sth[j, h, :, :], S0b, reads=[sthb], writes=[S0b])
            P.op(P.act, lambda e: e.activation(out=Sb0[:], in_=S0[:], func=AF.Copy), reads=[S0b], writes=[Sb0b])
            for t in range(NTL if DBG['hgrn'] else 0):
                smp = HS and t == 8
                tc_ = slice(t * 128, (t + 1) * 128)
                if smp:
                    for u, (Su, Sub, Sbu, Sbub) in enumerate(((S0, S0b, Sb0, Sb0b), (S1, S1b, Sb1, Sb1b))):
                        P.dma(P.sp, Su[:], io.state_hgrn[j, 2 * c + u, h, :, :], Sub, writes=[Sub])
                        P.op(P.act, lambda e: e.activation(out=Sbu[:], in_=Su[:], func=AF.Copy), reads=[Sub], writes=[Sbub])
                P.op(P.pe, lambda e: e.matmul(PS[0][:, 0:128], lhsT=g_tm[:, t, :], rhs=k.BU64[:], start=True, stop=True),
                     reads=[g_tmb, k.cstb], writes=[PB[0]])
                P.op(P.pe, lambda e: e.matmul(PS[1][:, 0:128], lhsT=k.BR64[:], rhs=g_tm[:, t, :], start=True, stop=True),
                     reads=[g_tmb, k.cstb], writes=[PB[1]])
                P.op(P.pe, lambda e: e.transpose(out=PS[2][:, 0:128], in_=k_tm[:, t, :], identity=k.ident[:]),
                     reads=[k_tmb, k.identb], writes=[PB[2]])
                for u in range(2):
                    cu = slice(64 * u, 64 * u + 64)
                    mid = 64 * u + 31
                    P.op(P.dve, lambda e: e.tensor_scalar(out=rcol[:, 2 * u:2 * u + 1], in0=PS[0][:, mid:mid + 1], scalar1=1.0, scalar2=None, op0=ALU.mult), reads=[PB[0]], writes=[rcolb])
                    P.op(P.dve, lambda e: e.tensor_scalar(out=rcol[:, 2 * u + 1:2 * u + 2], in0=PS[0][:, mid:mid + 1], scalar1=-1.0, scalar2=None,
                                                          op0=ALU.mult), reads=[PB[0]], writes=[rcolb])
                    P.op(P.act, lambda e: e.activation(out=tE[:, cu], in_=PS[0][:, cu], func=AF.Exp, bias=rcol[:, 2 * u + 1:2 * u + 2]),
                         reads=[PB[0], rcolb], writes=[tEb])
                    P.op(P.act, lambda e: e.activation(out=tK[:, cu], in_=PS[0][:, cu], func=AF.Exp, bias=rcol[:, 2 * u:2 * u + 1], scale=-1.0),
                         reads=[PB[0], rcolb], writes=[tKb])
                P.op(P.act, lambda e: e.activation(out=tG[:], in_=PS[0][:, 0:128], func=AF.Exp), reads=[PB[0]], writes=[tGb])
                P.op(P.act, lambda e: e.activation(out=tR[:], in_=PS[1][:, 0:128], func=AF.Exp), reads=[PB[1]], writes=[tRb])
                P.op(P.dve, lambda e: e.tensor_tensor(out=qp[:], in0=qbs[:, tc_], in1=tE[:], op=ALU.mult), reads=[qbsb, tEb], writes=[qpb])
                P.op(P.dve, lambda e: e.tensor_tensor(out=kp[:], in0=PS[2][:, 0:128], in1=tK[:], op=ALU.mult), reads=[PB[2], tKb], writes=[kpb])
                P.op(P.dve, lambda e: e.tensor_tensor(out=qpp[:], in0=qbs[:, tc_], in1=tG[:], op=ALU.mult), reads=[qbsb, tGb], writes=[qppb])
                P.op(P.dve, lambda e: e.tensor_tensor(out=kpp[:], in0=k_tm[:, t, :], in1=tR[:], op=ALU.mult), reads=[k_tmb, tRb], writes=[kppb])
                P.op(P.pe, lambda e: e.matmul(PS[3][:, 0:128], lhsT=kp[:], rhs=qp[:], start=True, stop=True),
                     reads=[kpb, qpb], writes=[PB[3]])
                P.op(P.dve, lambda e: e.tensor_tensor(out=ATm[:], in0=PS[3][:, 0:128], in1=k.BU64[:], op=ALU.mult),
                     reads=[PB[3], k.cstb], writes=[ATmb])
                P.op(P.pe, lambda e: e.matmul(PS[4][:, 0:128], lhsT=ATm[:], rhs=i_bf[:, t, :], start=True, stop=False),
                     reads=[ATmb, i_bfb], writes=[PB[4]], signal=False)
                for u in range(2):
                    r0, r1 = 64 * u, 64 * u + 64
                    if smp:
                        Su, Sub, Sbu, Sbub = ((S0, S0b, Sb0, Sb0b), (S1, S1b, Sb1, Sb1b))[u]
                    else:
                        Su, Sub, Sbu, Sbub = S0, S0b, Sb0, Sb0b
                    P.op(P.pe, lambda e: e.matmul(PS[4][r0:r1, 0:128], lhsT=qpp[:, r0:r1], rhs=Sbu[:], start=False, stop=(u == 1)),
                         reads=[qppb, Sbub], writes=[PB[4]], signal=(u == 1))
                    P.op(P.pe, lambda e: e.matmul(PS[5][:, 0:128], lhsT=kpp[r0:r1, :], rhs=i_bf[r0:r1, t, :], start=True, stop=True),
                         reads=[kppb, i_bfb], writes=[PB[5]])
                    P.op(P.dve, lambda e: e.scalar_tensor_tensor(out=Su[:], in0=Su[:], scalar=tG[:, r1 - 1:r1], in1=PS[5][:, 0:128],
                                                                 op0=ALU.mult, op1=ALU.add),
                         reads=[Sub, tGb, PB[5]], writes=[Sub])
                    if smp:
                        P.dma(P.sp, io.hgrn_s[j, 2 * c + u, h, :, :], Su[:], Sub, reads=[Sub])
                    else:
                        P.op(P.act, lambda e: e.activation(out=Sbu[:], in_=Su[:], func=AF.Copy), reads=[Sub], writes=[Sbub])
                ss = stat[:, 4:5]
                rs = stat[:, 5:6]
                P.op(P.act, lambda e: e.activation(out=on[:], in_=PS[4][:, 0:128], func=AF.Square, accum_out=ss),
                     reads=[PB[4]], writes=[onb, k.statb])
                P.op(P.dve, lambda e: e.tensor_scalar(out=rs, in0=ss, scalar1=1.0 / HD, scalar2=EPS, op0=ALU.mult, op1=ALU.add),
                     reads=[k.statb], writes=[k.statb])
                P.op(P.act, lambda e: e.activation(out=rs, in_=rs, func=AF.Sqrt), reads=[k.statb], writes=[k.statb])
                P.op(P.dve, lambda e: e.reciprocal(out=rs, in_=rs), reads=[k.statb], writes=[k.statb])
                P.op(P.dve, lambda e: e.scalar_tensor_tensor(out=on[:], in0=PS[4][:, 0:128], scalar=rs, in1=gnb[:], op0=ALU.mult, op1=ALU.mult),
                     reads=[PB[4], k.statb, gnbb], writes=[onb])
                P.op(P.dve, lambda e: e.tensor_tensor(out=on[:], in0=on[:], in1=gate[:, t, :], op=ALU.mult), reads=[onb, gateb], writes=[onb])
                P.op(P.pe, lambda e: e.transpose(out=PS[6][:, 0:128], in_=on[:], identity=k.ident[:]), reads=[onb, k.identb], writes=[PB[6]])
                P.op(P.act, lambda e: e.activation(out=obT[:, tc_], in_=PS[6][:, 0:128], func=AF.Copy), reads=[PB[6]], writes=[obTb])
                if t == 7:
                    if c < NCHUNK - 1:
                        P.dma(P.sp, io.sth[j, h, :, :], S0[:], sthb, reads=[S0b], writes=[sthb])
                    else:
                        P.dma(P.sp, io.hgrn_p[j, h, :, :], S0[:], S0b, reads=[S0b])
            dcnt = 0
            for t in range(NTL if DBG['outp'] else 0):
                for q in range(4):
                    pd = dcnt % 4
                    dcnt += 1
                    P.op(P.pe, lambda e: e.matmul(PS[pd][:, :], lhsT=oaT[:, t * 128:(t + 1) * 128], rhs=wo[:, 0, q * 512:(q + 1) * 512],
                                                  start=True, stop=False),
                         reads=[oaTb, wob], writes=[PB[pd]], signal=False)
                    P.op(P.pe, lambda e: e.matmul(PS[pd][:, :], lhsT=obT[:, t * 128:(t + 1) * 128], rhs=wo[:, 1, q * 512:(q + 1) * 512],
                                                  start=False, stop=True),
                         reads=[obTb, wob], writes=[PB[pd]])
                    P.op(P.dve, lambda e: e.tensor_tensor(out=k.x[:, t, q * 512:(q + 1) * 512], in0=PS[pd][:, :],
                                                          in1=k.x[:, t, q * 512:(q + 1) * 512], op=ALU.add),
                         reads=[PB[pd], k.xb[t]], writes=[k.xb[t]])
        P.barrier()


def final_norm_out(P, k, nc, gam_final, y_p, y_s, ch):
    with ExitStack() as es:
        gb = es.enter_context(nc.sbuf_tensor(U("f_gb"), [128, D], F32))
        gbb = P.buf("f_gb")
        yt = [es.enter_context(nc.sbuf_tensor(U(f"f_y{i}"), [128, D], F32)) for i in range(2)]
        ytb = [P.buf(f"f_y{i}") for i in range(2)]
        P.dma(P.sp, gb[:], gam_final.partition_broadcast(128), gbb, writes=[gbb])
        for t in range(ch.ntile):
            i = t % 2
            ss = k.small[:, 0:1]
            rs = k.small[:, 1:2]
            P.op(P.act, lambda e: e.activation(out=yt[i][:], in_=k.x[:, t, :], func=AF.Square, accum_out=ss),
                 reads=[k.xb[t]], writes=[ytb[i], k.ssb])
            P.op(P.dve, lambda e: e.tensor_scalar(out=rs, in0=ss, scalar1=1.0 / D, scalar2=EPS, op0=ALU.mult, op1=ALU.add),
                 reads=[k.ssb], writes=[k.rsb])
            P.op(P.act, lambda e: e.activation(out=rs, in_=rs, func=AF.Sqrt), reads=[k.rsb], writes=[k.rsb])
            P.op(P.dve, lambda e: e.reciprocal(out=rs, in_=rs), reads=[k.rsb], writes=[k.rsb])
            P.op(P.dve, lambda e: e.scalar_tensor_tensor(out=yt[i][:], in0=k.x[:, t, :], scalar=rs, in1=gb[:],
                                                         op0=ALU.mult, op1=ALU.mult),
                 reads=[k.xb[t], k.rsb, gbb], writes=[ytb[i]])
            if ch.has_sample and t == 8:
                dst = y_s[ch.c * 128:(ch.c + 1) * 128, :]
            else:
                dst = y_p[ch.p0 + t * 128:ch.p0 + (t + 1) * 128, :]
            P.dma(P.sp, dst, yt[i][:], ytb[i], reads=[ytb[i]])
        P.barrier()


def load_x(P, k, nc, x_p, x_s, ch):
    for t in range(ch.ntile):
        if ch.has_sample and t == 8:
            src = x_s[ch.c * 128:(ch.c + 1) * 128, :]
        else:
            src = x_p[ch.p0 + t * 128:ch.p0 + (t + 1) * 128, :]
        P.dma(P.sp, k.x[:, t, :], src, k.xb[t], writes=[k.xb[t]])


def load_consts(P, k, nc, gam):
    make_identity(P, k, nc)
    with nc.allow_non_contiguous_dma(reason="tiny gamma load"):
        P.dma(P.sp, k.gam[:], gam.rearrange("g (c p) -> p g c", p=128), k.gamb, writes=[k.gamb])


IN_SPECS = [
    ("x_p", [SEQ, D]), ("x_s", [256, D]),
    ("cache_k", [2, 4, H, SEQ, HD]), ("cache_v", [2, 4, H, SEQ, HD]), ("cache_logf", [2, 4, H, SEQ]),
    ("state_hgrn", [2, 4, H, HD, HD]),
    ("gam", [13, D]),
    ("ffn1_gate", [4, D, DFF]), ("ffn1_up", [4, D, DFF]), ("ffn1_down", [4, DFF, D]),
    ("ffn2_gate", [4, D, DFF]), ("ffn2_up", [4, D, DFF]), ("ffn2_down", [4, DFF, D]),
    ("ab_w_in", [2, D, AB_IN]), ("ab_b_f", [2, H]), ("hgrn_lb", [2, 1024]), ("hgrn_gnorm", [2, HD]),
    ("ab_w_out", [2, D, D]),
    ("c_w_in", [2, D, 2 * D]), ("c_ln_g", [2, D]), ("c_ln_b", [2, D]), ("c_w_s", [2, 8, 128, 128]),
    ("c_b_s", [2, 8, 128]), ("c_w_out", [2, D, D]),
]
OUT_SPECS = [
    ("y_p", [SEQ, D]), ("y_s", [256, D]),
    ("k_p", [2, H, SEQ, HD]), ("v_p", [2, H, SEQ, HD]), ("logf_p", [2, H, SEQ]), ("hgrn_p", [2, H, HD, HD]),
    ("k_s", [2, 4, H, 64, HD]), ("v_s", [2, 4, H, 64, HD]), ("logf_s", [2, 4, H, 64]), ("hgrn_s", [2, 4, H, HD, HD]),
    ("cv_s", [2, 256, D]),
]


def build(layers=(0, 1, 2, 3), do_ffn=True, chunks=(0, 1, 2, 3)):
    nc = bass.Bass("TRN2", target_bir_lowering=False)
    io = K()
    for n, shp in IN_SPECS:
        setattr(io, n, nc.dram_tensor(n, shp, F32, kind="ExternalInput").ap())
    for n, shp in OUT_SPECS:
        setattr(io, n, nc.dram_tensor(n, shp, F32, kind="ExternalOutput").ap())
    io.kth = nc.dram_tensor("kth", [2, H, HD, SEQ], BF16).ap()
    io.vth = nc.dram_tensor("vth", [2, H, SEQ, HD], BF16).ap()
    io.sth = nc.dram_tensor("sth", [2, H, HD, HD], F32).ap()
    with ExitStack() as es:
        P = Prog(nc, es)
        k = K()
        setup_persistent(P, k, es, nc)
        load_consts(P, k, nc, io.gam)
        for c in chunks:
            ch = Chunk(c)
            load_x(P, k, nc, io.x_p, io.x_s, ch)
            for l in layers:
                j = l // 2
                if do_ffn:
                    ffn_phase(P, k, nc, io.ffn1_gate[l], io.ffn1_up[l], io.ffn1_down[l], l, ch)
                if l % 2 == 0:
                    abmix(P, k, nc, j, ch, io, 4 + l)
                else:
                    cmix(P, k, nc, io.c_w_in[j], io.c_ln_g[j], io.c_ln_b[j], io.c_w_s[j], io.c_b_s[j], io.c_w_out[j],
                         io.cv_s[j], 4 + l, ch)
                if do_ffn:
                    ffn_phase(P, k, nc, io.ffn2_gate[l], io.ffn2_up[l], io.ffn2_down[l], 8 + l, ch)
            final_norm_out(P, k, nc, io.gam[12, :], io.y_p, io.y_s, ch)
        P.finish()
    return nc


def make_in_maps(inp):
    f = lambda a: np.ascontiguousarray(a, dtype=np.float32)
    gam = np.concatenate([inp["norm_ffn1"], inp["norm_mix"], inp["norm_ffn2"], inp["norm_final"][None, :]], axis=0)
    shared = {n: f(inp[n]) for n, _ in IN_SPECS if n in inp}
    shared["gam"] = f(gam)
    maps = []
    for core in range(NCORE):
        m = dict(shared)
        m["x_p"] = f(inp["x_prompt"][core % 2])
        sl = slice(4 * core, 4 * core + 4)
        m["x_s"] = f(inp["x_sample"][sl].reshape(256, D))
        m["cache_k"] = f(inp["cache_k"][:, sl])
        m["cache_v"] = f(inp["cache_v"][:, sl])
        m["cache_logf"] = f(inp["cache_logf"][:, sl])
        m["state_hgrn"] = f(inp["state_hgrn"][:, sl])
        maps.append(m)
    return maps


def gather_outputs(res):
    r = res
    cat = lambda n, ax: np.concatenate([r[c][n] for c in range(NCORE)], axis=ax)
    y_prompt = np.stack([r[0]["y_p"], r[1]["y_p"]])
    y_sample = np.concatenate([r[c]["y_s"].reshape(4, 64, D) for c in range(NCORE)], axis=0)
    stk = lambda n: np.stack([r[0][n], r[1][n]], axis=1)
    return (y_prompt, y_sample, stk("k_p"), stk("v_p"), stk("logf_p"), stk("hgrn_p"),
            cat("k_s", 1), cat("v_s", 1), cat("logf_s", 1), cat("hgrn_s", 1),
            np.concatenate([r[c]["cv_s"].reshape(2, 4, 64, D) for c in range(NCORE)], axis=1))


def kernel(**inputs):
    inp = {k_: np.asarray(v) for k_, v in inputs.items()}
    nc = build()
    maps = make_in_maps(inp)
    res = run_bass_kernel_spmd(nc, maps, core_ids=list(range(NCORE)))
    outs = gather_outputs(res.results)
    return tuple(np.ascontiguousarray(o, dtype=np.float32) for o in outs)
```

```python
import numpy as np
from contextlib import ExitStack
import concourse.bass as bass
import concourse.mybir as mybir
from concourse.bass_utils import run_bass_kernel_spmd

F32 = mybir.dt.float32
BF16 = mybir.dt.bfloat16
AF = mybir.ActivationFunctionType
ALU = mybir.AluOpType
AX = mybir.AxisListType

D = 2048
DFF = 5632
NCORE = 8
KC = D // 128
EPS = 1e-6
MAXT = 9
NTMAX = MAXT * 128
SEQ = 4096
NCHUNK = 4
H = 8
HD = 128
AB_IN = 7176
TINY = 1e-30
SCALE = HD ** -0.5


class Chunk:
    def __init__(self, c):
        self.c = c
        self.has_sample = c < 2
        self.ntile = 9 if self.has_sample else 8
        self.ntok = self.ntile * 128
        self.p0 = c * 1024
        self.gt0 = c * 8

    def groups(self):
        g = [(0, 512), (512, 512)]
        if self.has_sample:
            g.append((1024, 128))
        return g


_UID = [0]


def U(name):
    _UID[0] += 1
    return f"{name}_{_UID[0]}"


def sbuf_guard(nc):
    with nc.sbuf_tensor(U("guard"), [128, 4096 + 8], F32):
        pass


class Buf:
    __slots__ = ("name", "last_w", "readers", "sem", "dcnt")

    def __init__(self, name):
        self.name = name
        self.last_w = None
        self.readers = {}
        self.sem = None
        self.dcnt = 0


class Stream:
    def __init__(self, eng, name, sem, is_pe=False):
        self.eng = eng
        self.name = name
        self.sem = sem
        self.cnt = 0
        self.waited = {}
        self.is_pe = is_pe


class Prog:
    def __init__(self, nc, es):
        self.nc = nc
        self.es = es
        self.sems = {}
        mk = lambda n: es.enter_context(nc.semaphore(n))
        self.pe = Stream(nc.tensor, "pe", mk("s_pe"), is_pe=True)
        self.act = Stream(nc.scalar, "act", mk("s_act"))
        self.dve = Stream(nc.vector, "dve", mk("s_dve"))
        self.pool = Stream(nc.gpsimd, "pool", mk("s_pool"))
        self.sp = Stream(nc.sync, "sp", mk("s_sp"))
        self.streams = [self.pe, self.act, self.dve, self.pool, self.sp]
        self.semobj = {}
        for s in self.streams:
            self.semobj[id(s.sem)] = s
        self.dma_bufs = []
        self.bufs = {}

    def buf(self, name):
        b = self.bufs.get(name)
        if b is None:
            b = Buf(name)
            self.bufs[name] = b
        return b

    def _collect(self, reads, writes, extra=()):
        evs = {}

        def add(ev):
            if ev is None:
                return
            k = id(ev[0])
            if k not in evs or evs[k][1] < ev[1]:
                evs[k] = ev
        for b in reads:
            add(b.last_w)
        for b in writes:
            add(b.last_w)
            for ev in b.readers.values():
                add(ev)
        for ev in extra:
            add(ev)
        return evs

    def _wait(self, st, evs):
        for k, (sem, val) in evs.items():
            if sem is st.sem:
                if st.is_pe or val > st.cnt:
                    continue
            if st.waited.get(k, 0) >= val:
                continue
            st.eng.wait_ge(sem, val)
            st.waited[k] = val

    def _record(self, ev, reads, writes):
        k = id(ev[0])
        for b in reads:
            old = b.readers.get(k)
            if old is None or old[1] < ev[1]:
                b.readers[k] = ev
        for b in writes:
            b.last_w = ev
            b.readers = {}

    def op(self, st, fn, reads=(), writes=(), signal=True, extra=()):
        self._wait(st, self._collect(reads, writes, extra))
        ins = fn(st.eng)
        if signal:
            st.cnt += 1
            ins.then_inc(st.sem, 1)
            ev = (st.sem, st.cnt)
        else:
            ev = (st.sem, st.cnt + 1)
        self._record(ev, reads, writes)
        return ev

    def dma(self, st, out, in_, chan, reads=(), writes=(), extra=()):
        if chan.sem is None:
            chan.sem = self.es.enter_context(self.nc.semaphore("d_" + chan.name))
            self.dma_bufs.append(chan)
        self._wait(st, self._collect(reads, writes, extra))
        ins = st.eng.dma_start(out=out, in_=in_)
        chan.dcnt += 16
        ins.then_inc(chan.sem, 16)
        ev = (chan.sem, chan.dcnt)
        self._record(ev, reads, writes)
        return ev

    def barrier(self):
        evs = {}
        for s in self.streams:
            if s.cnt:
                evs[id(s.sem)] = (s.sem, s.cnt)
        for b in self.dma_bufs:
            evs[id(b.sem)] = (b.sem, b.dcnt)
        for s in self.streams:
            self._wait(s, dict(evs))

    def finish(self):
        evs = {}
        for b in self.dma_bufs:
            evs[id(b.sem)] = (b.sem, b.dcnt)
        for s in self.streams:
            if s.cnt:
                evs[id(s.sem)] = (s.sem, s.cnt)
        self._wait(self.sp, evs)


class K:
    pass


def setup_persistent(P, k, es, nc):
    sb = lambda name, shape, dt: es.enter_context(nc.sbuf_tensor(U(name), shape, dt))
    k.x = sb("x", [128, MAXT, D], F32)
    k.xb = [P.buf(f"x{t}") for t in range(MAXT)]
    k.hT = sb("hT", [128, KC, NTMAX], BF16)
    k.hTb = [P.buf(f"hT{t}") for t in range(MAXT)]
    k.xnb = P.buf("xn")
    k.ident = sb("ident", [128, 128], F32)
    k.identb = P.buf("ident")
    k.maskU = sb("maskU", [128, 128], F32)
    k.maskUb = P.buf("maskU")
    k.ones_f = sb("ones_f", [128, 128], F32)
    k.ones_b = sb("ones_b", [128, 128], BF16)
    k.BU64 = sb("BU64", [128, 128], F32)
    k.BR64 = sb("BR64", [128, 128], F32)
    k.cstb = P.buf("consts")
    k.Lhist = sb("Lhist", [128, 2, 32, H], F32)
    k.Lcarry = sb("Lcarry", [128, 2, H], F32)
    k.Lhb = [P.buf("Lhist0"), P.buf("Lhist1")]
    k.gam = sb("gam_sb", [128, 13, KC], F32)
    k.gamb = P.buf("gam")
    k.small = sb("small", [128, 64], F32)
    k.ssb = P.buf("ss")
    k.rsb = P.buf("rs")
    k.statb = P.buf("stat")
    k.psum = [es.enter_context(nc.psum_tensor(f"ps{i}", [128, 512], F32)) for i in range(8)]
    k.psb = [P.buf(f"ps{i}") for i in range(8)]


def make_identity(P, k, nc):
    P.op(P.pool, lambda e: e.memset(k.ident[:], 0.0), writes=[k.identb])
    P.op(P.pool, lambda e: e.affine_select(out=k.ident[:], in_=k.ident[:], pattern=[[-1, 128]],
                                           compare_op=ALU.not_equal, fill=1.0, base=0, channel_multiplier=1),
         reads=[k.identb], writes=[k.identb])
    P.op(P.pool, lambda e: e.memset(k.maskU[:], 1.0), writes=[k.maskUb])
    P.op(P.pool, lambda e: e.affine_select(out=k.maskU[:], in_=k.maskU[:], pattern=[[1, 128]],
                                           compare_op=ALU.is_ge, fill=0.0, base=0, channel_multiplier=-1),
         reads=[k.maskUb], writes=[k.maskUb])
    P.op(P.pool, lambda e: e.memset(k.ones_f[:], 1.0), writes=[k.cstb])
    P.op(P.pool, lambda e: e.memset(k.ones_b[:], 1.0), writes=[k.cstb])
    P.op(P.pool, lambda e: e.tensor_copy(out=k.BU64[:], in_=k.maskU[:]), reads=[k.maskUb], writes=[k.cstb])
    P.op(P.pool, lambda e: e.memset(k.BU64[0:64, 64:128], 0.0), writes=[k.cstb])
    P.op(P.pool, lambda e: e.memset(k.BR64[:], 1.0), writes=[k.cstb])
    P.op(P.pool, lambda e: e.affine_select(out=k.BR64[:], in_=k.BR64[:], pattern=[[-1, 128]],
                                           compare_op=ALU.is_gt, fill=0.0, base=0, channel_multiplier=1),
         reads=[k.cstb], writes=[k.cstb])
    P.op(P.pool, lambda e: e.memset(k.BR64[64:128, 0:64], 0.0), writes=[k.cstb])
    P.op(P.pool, lambda e: e.memset(k.Lcarry[:], 0.0), writes=k.Lhb)


def rmsnorm_to_hT(P, k, nc, gidx, ch, xn):
    hT, hTb, xnb, ssb_col = k.hT, k.hTb, k.xnb, 0
    for t in range(ch.ntile):
        ss = k.small[:, ssb_col + 0:ssb_col + 1]
        rs = k.small[:, ssb_col + 1:ssb_col + 2]
        P.op(P.act, lambda e: e.activation(out=xn[:], in_=k.x[:, t, :], func=AF.Square, accum_out=ss),
             reads=[k.xb[t]], writes=[xnb, k.ssb])
        P.op(P.dve, lambda e: e.tensor_scalar(out=rs, in0=ss, scalar1=1.0 / D, scalar2=EPS,
                                               op0=ALU.mult, op1=ALU.add),
             reads=[k.ssb], writes=[k.rsb])
        P.op(P.act, lambda e: e.activation(out=rs, in_=rs, func=AF.Sqrt), reads=[k.rsb], writes=[k.rsb])
        P.op(P.dve, lambda e: e.reciprocal(out=rs, in_=rs), reads=[k.rsb], writes=[k.rsb])
        P.op(P.dve, lambda e: e.tensor_scalar(out=xn[:], in0=k.x[:, t, :], scalar1=rs, scalar2=None,
                                               op0=ALU.mult),
             reads=[k.xb[t], k.rsb], writes=[xnb])
        for g4 in range(KC // 4):
            pi = g4 % 2
            ps = k.psum[pi]
            for j in range(4):
                c = g4 * 4 + j
                P.op(P.pe, lambda e: e.transpose(out=ps[:, j * 128:(j + 1) * 128],
                                                 in_=xn[:, c * 128:(c + 1) * 128], identity=k.ident[:]),
                     reads=[xnb, k.identb], writes=[k.psb[pi]], signal=(j == 3))
            for j in range(4):
                c = g4 * 4 + j
                eng = P.dve if j % 2 == 0 else P.act
                if eng is P.dve:
                    P.op(eng, lambda e: e.tensor_scalar(out=hT[:, c, t * 128:(t + 1) * 128],
                                                        in0=ps[:, j * 128:(j + 1) * 128],
                                                        scalar1=k.gam[:, gidx, c:c + 1], scalar2=None,
                                                        op0=ALU.mult),
                         reads=[k.psb[pi], k.gamb], writes=[hTb[t]])
                else:
                    P.op(eng, lambda e: e.activation(out=hT[:, c, t * 128:(t + 1) * 128],
                                                     in_=ps[:, j * 128:(j + 1) * 128],
                                                     func=AF.Copy, scale=k.gam[:, gidx, c:c + 1]),
                         reads=[k.psb[pi], k.gamb], writes=[hTb[t]])


FB = 256
NB = DFF // FB


def alloc_ffn_bufs(P, k, es, nc):
    sb = lambda name, shape, dt: es.enter_context(nc.sbuf_tensor(U(name), shape, dt))
    b = K()
    b.hT, b.hTb, b.xnb = k.hT, k.hTb, k.xnb
    b.xn = sb("xn", [128, D], F32)
    b.wgu = [sb(f"wgu{i}", [128, KC, FB], BF16) for i in range(4)]
    b.wgub = [P.buf(f"wgu{i}") for i in range(4)]
    b.wd = [sb(f"wd{i}", [128, FB // 128, D], BF16) for i in range(2)]
    b.wdb = [P.buf(f"wd{i}") for i in range(2)]
    b.actT = [sb(f"actT{i}", [128, FB // 128, NTMAX], BF16) for i in range(2)]
    b.actTb = [P.buf(f"actT{i}") for i in range(2)]
    b.stmp = [sb(f"stmp{i}", [128, 512], BF16) for i in range(2)]
    b.stmpb = [P.buf(f"stmp{i}") for i in range(2)]
    return b


def ffn_phase(P, k, nc, wg, wu, wd, gidx, ch):
    with ExitStack() as es:
        b = alloc_ffn_bufs(P, k, es, nc)
        sbuf_guard(nc)
        ffn(P, k, nc, wg, wu, wd, gidx, b, ch)
        P.barrier()


def ffn(P, k, nc, wg, wu, wd, gidx, b, ch):
    rmsnorm_to_hT(P, k, nc, gidx, ch, b.xn)
    wg_v = wg.rearrange("(c p) n -> p c n", p=128)
    wu_v = wu.rearrange("(c p) n -> p c n", p=128)
    wd_v = wd.rearrange("(c p) n -> p c n", p=128)

    def load(blk):
        sg = (2 * blk) % 4
        su = (2 * blk + 1) % 4
        sd = blk % 2
        P.dma(P.pool, b.wgu[sg][:], wg_v[:, :, blk * FB:(blk + 1) * FB], b.wgub[sg], writes=[b.wgub[sg]])
        P.dma(P.pool, b.wgu[su][:], wu_v[:, :, blk * FB:(blk + 1) * FB], b.wgub[su], writes=[b.wgub[su]])
        P.dma(P.pool, b.wd[sd][:], wd_v[:, blk * 2:blk * 2 + 2, :], b.wdb[sd], writes=[b.wdb[sd]])

    load(0)
    pcount = 0
    for blk in range(NB):
        if blk + 1 < NB:
            load(blk + 1)
        sg = (2 * blk) % 4
        su = (2 * blk + 1) % 4
        sd = blk % 2
        a = blk % 2
        for m in range(FB // 128):
            for (t0, tn) in ch.groups():
                pg = 0 + (pcount % 2)
                pu = 2 + (pcount % 2)
                st = pcount % 2
                pcount += 1
                tiles = [b.hTb[t] for t in range(t0 // 128, (t0 + tn) // 128)]
                for c in range(KC):
                    P.op(P.pe, lambda e: e.matmul(k.psum[pg][:, 0:tn], lhsT=b.wgu[sg][:, c, m * 128:(m + 1) * 128],
                                                  rhs=b.hT[:, c, t0:t0 + tn], start=(c == 0), stop=(c == KC - 1)),
                         reads=[b.wgub[sg]] + tiles, writes=[k.psb[pg]], signal=(c == KC - 1))
                for c in range(KC):
                    P.op(P.pe, lambda e: e.matmul(k.psum[pu][:, 0:tn], lhsT=b.wgu[su][:, c, m * 128:(m + 1) * 128],
                                                  rhs=b.hT[:, c, t0:t0 + tn], start=(c == 0), stop=(c == KC - 1)),
                         reads=[b.wgub[su]] + tiles, writes=[k.psb[pu]], signal=(c == KC - 1))
                P.op(P.act, lambda e: e.activation(out=b.stmp[st][:, 0:tn], in_=k.psum[pg][:, 0:tn], func=AF.Silu),
                     reads=[k.psb[pg]], writes=[b.stmpb[st]])
                P.op(P.dve, lambda e: e.tensor_tensor(out=b.actT[a][:, m, t0:t0 + tn], in0=k.psum[pu][:, 0:tn],
                                                      in1=b.stmp[st][:, 0:tn], op=ALU.mult),
                     reads=[k.psb[pu], b.stmpb[st]], writes=[b.actTb[a]])
        dcount = 0
        for t in range(ch.ntile):
            for q in range(4):
                pd = 4 + (dcount % 4)
                dcount += 1
                for m in range(FB // 128):
                    P.op(P.pe, lambda e: e.matmul(k.psum[pd][:, :], lhsT=b.actT[a][:, m, t * 128:(t + 1) * 128],
                                                  rhs=b.wd[sd][:, m, q * 512:(q + 1) * 512],
                                                  start=(m == 0), stop=(m == FB // 128 - 1)),
                         reads=[b.actTb[a], b.wdb[sd]], writes=[k.psb[pd]], signal=(m == FB // 128 - 1))
                P.op(P.dve, lambda e: e.scalar_tensor_tensor(out=k.x[:, t, q * 512:(q + 1) * 512], in0=k.psum[pd][:, :],
                                                             scalar=0.5, in1=k.x[:, t, q * 512:(q + 1) * 512],
                                                             op0=ALU.mult, op1=ALU.add),
                     reads=[k.psb[pd], k.xb[t]], writes=[k.xb[t]])


GELU_C = 1.5957691216057308


def gelu_from_psum(P, ps, psb, n, dst, dstb, tmp, tmpb):
    P.op(P.act, lambda e: e.activation(out=tmp[:, :n], in_=ps[:, :n], func=AF.Square), reads=[psb], writes=[tmpb])
    P.op(P.dve, lambda e: e.tensor_scalar(out=tmp[:, :n], in0=tmp[:, :n], scalar1=0.044715, scalar2=1.0,
                                           op0=ALU.mult, op1=ALU.add), reads=[tmpb], writes=[tmpb])
    P.op(P.dve, lambda e: e.tensor_tensor(out=tmp[:, :n], in0=ps[:, :n], in1=tmp[:, :n], op=ALU.mult),
         reads=[psb, tmpb], writes=[tmpb])
    P.op(P.act, lambda e: e.activation(out=tmp[:, :n], in_=tmp[:, :n], func=AF.Sigmoid, scale=GELU_C),
         reads=[tmpb], writes=[tmpb])
    P.op(P.dve, lambda e: e.tensor_tensor(out=dst, in0=ps[:, :n], in1=tmp[:, :n], op=ALU.mult),
         reads=[psb, tmpb], writes=[dstb])


def cmix(P, k, nc, w_in, ln_g, ln_b, w_s, b_s, w_out, cv_out, gidx, ch):
    NTILE = ch.ntile
    xnb = k.xnb
    is_s = lambda t: ch.has_sample and t == 8
    with ExitStack() as es:
        sb = lambda name, shape, dt: es.enter_context(nc.sbuf_tensor(U(name), shape, dt))
        xn = sb("c_xn", [128, D], F32)
        rmsnorm_to_hT(P, k, nc, gidx, ch, xn)
        P.barrier()
        vn = sb("c_vn", [128, MAXT, D], BF16)
        vnb = [P.buf(f"c_vn{t}") for t in range(MAXT)]
        wsT = [sb("c_wsTp", [128, 8, 128], BF16), sb("c_wsTs", [128, 8, 128], BF16)]
        wsTb = P.buf("c_wsT")
        bsb = [sb("c_bsp", [128, 8, 128], F32), sb("c_bss", [128, 8, 128], F32)]
        bsbb = P.buf("c_bs")
        gtmp = [xn[:, 0:512], xn[:, 512:1024]]
        gtmpb = [P.buf(f"c_gtmp{i}") for i in range(2)]
        stat = k.small
        with ExitStack() as es1:
            stg = [es1.enter_context(nc.sbuf_tensor(U(f"c_stg{i}"), [128, 128], F32)) for i in range(2)]
            stgb = [P.buf(f"c_stg{i}") for i in range(2)]
            P.op(P.pool, lambda e: e.memset(stg[1][:], 0.0), writes=[stgb[1]])
            for g in range(8):
                P.dma(P.sp, stg[0][:], w_s[g, :, :], stgb[0], writes=[stgb[0]])
                P.dma(P.sp, stg[1][0:64, 0:64], w_s[g, 0:64, 0:64], stgb[1], writes=[stgb[1]])
                P.dma(P.sp, stg[1][64:128, 64:128], w_s[g, 0:64, 0:64], stgb[1], writes=[stgb[1]])
                for v in range(2):
                    P.op(P.pe, lambda e: e.transpose(out=k.psum[v][:, 0:128], in_=stg[v][:], identity=k.ident[:]),
                         reads=[stgb[v], k.identb], writes=[k.psb[v]])
                    P.op(P.dve, lambda e: e.tensor_tensor(out=wsT[v][:, g, :], in0=k.psum[v][:, 0:128], in1=k.maskU[:],
                                                          op=ALU.mult),
                         reads=[k.psb[v], k.maskUb], writes=[wsTb])
            P.dma(P.sp, bsb[0][:], b_s.partition_broadcast(128), bsbb, writes=[bsbb])
            for hh in range(2):
                P.dma(P.sp, bsb[1][:, :, hh * 64:(hh + 1) * 64], b_s[:, 0:64].partition_broadcast(128), bsbb, writes=[bsbb])
            P.barrier()
        with ExitStack() as es2:
            sb2 = lambda name, shape, dt: es2.enter_context(nc.sbuf_tensor(U(name), shape, dt))
            ring = [sb2(f"c_ring{i}", [128, KC, 256], BF16) for i in range(2)]
            ringb = [P.buf(f"c_ring{i}") for i in range(2)]
            HD2 = D // 4
            lng = sb2("c_lng", [128, HD2], F32)
            lnb = sb2("c_lnb", [128, HD2], F32)
            lnbuf = P.buf("c_ln")
            sbuf_guard(nc)
            w_v = w_in.rearrange("(c p) n -> p c n", p=128)
            NVB = D // 256
            P.dma(P.pool, ring[0][:], w_v[:, :, D:D + 256], ringb[0], writes=[ringb[0]])
            cnt = 0
            for cb in range(NVB):
                if cb + 1 < NVB:
                    s1 = (cb + 1) % 2
                    P.dma(P.pool, ring[s1][:], w_v[:, :, D + (cb + 1) * 256:D + (cb + 2) * 256], ringb[s1], writes=[ringb[s1]])
                s = cb % 2
                for t in range(NTILE):
                    pi = cnt % 4
                    gi = cnt % 2
                    cnt += 1
                    for c in range(KC):
                        P.op(P.pe, lambda e: e.matmul(k.psum[pi][:, 0:256], lhsT=k.hT[:, c, t * 128:(t + 1) * 128],
                                                      rhs=ring[s][:, c, :], start=(c == 0), stop=(c == KC - 1)),
                             reads=[k.hTb[t], ringb[s]], writes=[k.psb[pi]], signal=(c == KC - 1))
                    gelu_from_psum(P, k.psum[pi], k.psb[pi], 256, vn[:, t, cb * 256:(cb + 1) * 256], vnb[t],
                                   gtmp[gi], gtmpb[gi])
            P.barrier()
            stat = k.small
            for t in range(NTILE):
                s1 = stat[:, 8:9]
                s2 = stat[:, 9:10]
                msq = stat[:, 11:12]
                mean = stat[:, 16 + t:17 + t]
                rstd = stat[:, 32 + t:33 + t]
                P.op(P.act, lambda e: e.activation(out=xn[:], in_=vn[:, t, :], func=AF.Copy, accum_out=s1),
                     reads=[vnb[t]], writes=[xnb, k.statb])
                P.op(P.act, lambda e: e.activation(out=xn[:], in_=vn[:, t, :], func=AF.Square, accum_out=s2),
                     reads=[vnb[t]], writes=[xnb, k.statb])
                P.op(P.dve, lambda e: e.tensor_scalar(out=mean, in0=s1, scalar1=1.0 / D, scalar2=None, op0=ALU.mult),
                     reads=[k.statb], writes=[k.statb])
                P.op(P.dve, lambda e: e.tensor_tensor(out=msq, in0=mean, in1=mean, op=ALU.mult),
                     reads=[k.statb], writes=[k.statb])
                P.op(P.dve, lambda e: e.scalar_tensor_tensor(out=rstd, in0=s2, scalar=1.0 / D, in1=msq,
                                                             op0=ALU.mult, op1=ALU.subtract),
                     reads=[k.statb], writes=[k.statb])
                P.op(P.dve, lambda e: e.tensor_scalar(out=rstd, in0=rstd, scalar1=EPS, scalar2=None, op0=ALU.add),
                     reads=[k.statb], writes=[k.statb])
                P.op(P.act, lambda e: e.activation(out=rstd, in_=rstd, func=AF.Sqrt), reads=[k.statb], writes=[k.statb])
                P.op(P.dve, lambda e: e.reciprocal(out=rstd, in_=rstd), reads=[k.statb], writes=[k.statb])
            for half in range(4):
                hs = slice(half * HD2, (half + 1) * HD2)
                P.dma(P.sp, lng[:], ln_g[hs].partition_broadcast(128), lnbuf, writes=[lnbuf])
                P.dma(P.sp, lnb[:], ln_b[hs].partition_broadcast(128), lnbuf, writes=[lnbuf])
                for t in range(NTILE):
                    mean = stat[:, 16 + t:17 + t]
                    rstd = stat[:, 32 + t:33 + t]
                    xh = xn[:, 0:HD2]
                    P.op(P.dve, lambda e: e.tensor_scalar(out=xh, in0=vn[:, t, hs], scalar1=mean, scalar2=rstd,
                                                          op0=ALU.subtract, op1=ALU.mult),
                         reads=[vnb[t], k.statb], writes=[xnb])
                    P.op(P.dve, lambda e: e.tensor_tensor(out=xh, in0=xh, in1=lng[:], op=ALU.mult),
                         reads=[xnb, lnbuf], writes=[xnb])
                    if not is_s(t):
                        P.op(P.dve, lambda e: e.tensor_tensor(out=vn[:, t, hs], in0=xh, in1=lnb[:], op=ALU.add),
                             reads=[xnb, lnbuf], writes=[vnb[t]])
                    else:
                        P.op(P.dve, lambda e: e.tensor_tensor(out=xh, in0=xh, in1=lnb[:], op=ALU.add),
                             reads=[xnb, lnbuf], writes=[xnb])
                        ts_ = ch.c
                        P.dma(P.sp, cv_out[ts_ * 128:(ts_ + 1) * 128, hs], xh, xnb, reads=[xnb])
                        P.op(P.act, lambda e: e.activation(out=vn[:, t, hs], in_=xh, func=AF.Copy),
                             reads=[xnb], writes=[vnb[t]])
            P.barrier()
        with ExitStack() as es3:
            sb3 = lambda name, shape, dt: es3.enter_context(nc.sbuf_tensor(U(name), shape, dt))
            ring = [sb3(f"c_uring{i}", [128, KC, 128], BF16) for i in range(2)]
            ringb = [P.buf(f"c_uring{i}") for i in range(2)]
            wo = [sb3(f"c_wo{i}", [128, D], BF16) for i in range(2)]
            wob = [P.buf(f"c_wo{i}") for i in range(2)]
            uT = [sb3(f"c_uT{i}", [128, NTMAX], BF16) for i in range(2)]
            uTb = [P.buf(f"c_uT{i}") for i in range(2)]
            pT, pTb = uT, uTb
            mt = [sb3(f"c_mt{i}", [128, 128], F32) for i in range(2)]
            mtb = [P.buf(f"c_mt{i}") for i in range(2)]
            sbuf_guard(nc)
            w_v = w_in.rearrange("(c p) n -> p c n", p=128)

            def load(cb):
                s = cb % 2
                P.dma(P.pool, ring[s][:], w_v[:, :, cb * 128:(cb + 1) * 128], ringb[s], writes=[ringb[s]])
                P.dma(P.pool, wo[s][:], w_out[cb * 128:(cb + 1) * 128, :], wob[s], writes=[wob[s]])
            load(0)
            cnt = 0
            mcnt = 0
            dcnt = 0
            for cb in range(KC):
                if cb + 1 < KC:
                    load(cb + 1)
                s = cb % 2
                g = cb // 2
                for (t0, tn) in ch.groups():
                    pi = cnt % 2
                    cnt += 1
                    tiles = [k.hTb[t] for t in range(t0 // 128, (t0 + tn) // 128)]
                    for c in range(KC):
                        P.op(P.pe, lambda e: e.matmul(k.psum[pi][:, 0:tn], lhsT=ring[s][:, c, :], rhs=k.hT[:, c, t0:t0 + tn],
                                                      start=(c == 0), stop=(c == KC - 1)),
                             reads=[ringb[s]] + tiles, writes=[k.psb[pi]], signal=(c == KC - 1))
                    gelu_from_psum(P, k.psum[pi], k.psb[pi], tn, uT[s][:, t0:t0 + tn], uTb[s], gtmp[pi], gtmpb[pi])
                for t in range(NTILE):
                    v = 1 if is_s(t) else 0
                    pm = 2 + (mcnt % 2)
                    mi = mcnt % 2
                    mcnt += 1
                    P.op(P.pe, lambda e: e.matmul(k.psum[pm][:, 0:128], lhsT=vn[:, t, cb * 128:(cb + 1) * 128],
                                                  rhs=wsT[v][:, g, :], start=True, stop=True),
                         reads=[vnb[t], wsTb], writes=[k.psb[pm]])
                    P.op(P.dve, lambda e: e.tensor_tensor(out=mt[mi][:], in0=k.psum[pm][:, 0:128], in1=bsb[v][:, g, :],
                                                          op=ALU.add),
                         reads=[k.psb[pm], bsbb], writes=[mtb[mi]])
                    P.op(P.dve, lambda e: e.tensor_tensor(out=pT[s][:, t * 128:(t + 1) * 128], in0=mt[mi][:],
                                                          in1=uT[s][:, t * 128:(t + 1) * 128], op=ALU.mult),
                         reads=[mtb[mi], uTb[s]], writes=[pTb[s]])
                for t in range(NTILE):
                    for q in range(4):
                        pd = 4 + (dcnt % 4)
                        dcnt += 1
                        P.op(P.pe, lambda e: e.matmul(k.psum[pd][:, :], lhsT=pT[s][:, t * 128:(t + 1) * 128],
                                                      rhs=wo[s][:, q * 512:(q + 1) * 512], start=True, stop=True),
                             reads=[pTb[s], wob[s]], writes=[k.psb[pd]])
                        P.op(P.dve, lambda e: e.tensor_tensor(out=k.x[:, t, q * 512:(q + 1) * 512], in0=k.psum[pd][:, :],
                                                              in1=k.x[:, t, q * 512:(q + 1) * 512], op=ALU.add),
                             reads=[k.psb[pd], k.xb[t]], writes=[k.xb[t]])
            P.barrier()


DBG = dict(fox=True, samp=True, hgrn=True, outp=True, heads=8, nproj=7, store=True)


def abmix(P, k, nc, j, ch, io, gidx):
    c, p0, gt0, NTL = ch.c, ch.p0, ch.gt0, ch.ntile
    HS = ch.has_sample
    w_in = io.ab_w_in[j].rearrange("(c p) n -> p c n", p=128)
    w_out = io.ab_w_out[j]
    with ExitStack() as es0:
        xn = es0.enter_context(nc.sbuf_tensor(U("a_xn"), [128, D], F32))
        rmsnorm_to_hT(P, k, nc, gidx, ch, xn)
        P.barrier()
    with ExitStack() as es:
        def sbt(name, shape, dt):
            return es.enter_context(nc.sbuf_tensor(U(name), shape, dt)), P.buf(name)
        NR = 3
        kTall, kTallb = sbt("a_kTall", [128, SEQ], BF16)
        wr = [sbt(f"a_wr{i}", [128, KC, 128], BF16) for i in range(NR)]
        wfa, wfab = sbt("a_wfa", [128, KC, H], BF16)
        wo, wob = sbt("a_wo", [128, 2, D // 2], BF16)
        qT, qTb = sbt("a_qT", [128, NTMAX], BF16)
        kT32, kT32b = sbt("a_kT32", [128, 512], F32)
        vall, vallb = sbt("a_vall", [128, 32, 128], BF16)
        kTs, kTsb = sbt("a_kTs", [128, 128], BF16)
        vs, vsb = sbt("a_vs", [128, 128], BF16)
        kc32 = [sbt(f"a_kc32_{i}", [128, 4, 128], F32) for i in range(1)]
        lf, lfb = sbt("a_lf", [128, MAXT, H], F32)
        lfT = [sbt(f"a_lfT{i}", [H, 128], F32) for i in range(2)]
        bfb, bfbb = sbt("a_bfb", [128, H], F32)
        Lref, Lrefb = sbt("a_Lref", [128, 8, H], F32)
        Bn, Bnb = sbt("a_Bn", [128, H], F32)
        Bq, Bqb = sbt("a_Bq", [128, 8, 32], F32)
        pT = [sbt(f"a_pT{i}", [128, 128], BF16) for i in range(4)]
        recs = [sbt(f"a_rec{i}", [128, 128], F32) for i in range(2)]
        rec, recb = recs[0]
        acc6 = [P.buf("acc6_0"), P.buf("acc6_1")]
        acc7 = [P.buf("acc7_0"), P.buf("acc7_1")]
        oaT, oaTb = sbt("a_oaT", [128, NTMAX], BF16)
        obT, obTb = sbt("a_obT", [128, NTMAX], BF16)
        st32 = [sbt(f"a_st{i}", [128, 128], F32) for i in range(2)]
        lfc_r, lfc_rb = sbt("a_lfcr", [32, 128], F32)
        lfc, lfcb = sbt("a_lfc", [128, 4, 32], F32)
        qbs, qbsb = sbt("a_qbs", [128, NTMAX], BF16)
        g_tm, g_tmb = sbt("a_gtm", [128, MAXT, 128], F32)
        k_tm, k_tmb = sbt("a_ktm", [128, MAXT, 128], BF16)
        i_bf, i_bfb = sbt("a_ibf", [128, MAXT, 128], BF16)
        gate, gateb = sbt("a_gate", [128, MAXT, 128], BF16)
        lbt, lbtb = sbt("a_lbt", [128, 2, 128], F32)
        lbo, lbob = sbt("a_lbo", [128, 2, 128], F32)
        gnb, gnbb = sbt("a_gnb", [128, 128], F32)
        tE, tEb = sbt("a_tE", [128, 128], F32)
        tK, tKb = sbt("a_tK", [128, 128], F32)
        tG, tGb = sbt("a_tG", [128, 128], F32)
        tR, tRb = sbt("a_tR", [128, 128], F32)
        rcol, rcolb = sbt("a_rcol", [128, 4], F32)
        qp, qpb = sbt("a_qp", [128, 128], BF16)
        kp, kpb = sbt("a_kp", [128, 128], BF16)
        qpp, qppb = sbt("a_qpp", [128, 128], BF16)
        kpp, kppb = sbt("a_kpp", [128, 128], BF16)
        ATm, ATmb = sbt("a_ATm", [128, 128], BF16)
        S32 = [sbt(f"a_S32_{i}", [128, 128], F32) for i in range(2)]
        Sbf = [sbt(f"a_Sbf_{i}", [128, 128], BF16) for i in range(2)]
        on, onb = sbt("a_on", [128, 128], F32)
        sbuf_guard(nc)
        PS, PB = k.psum, k.psb
        kthb, vthb, sthb = P.buf(f"kth{j}"), P.buf(f"vth{j}"), P.buf(f"sth{j}")
        stat = k.small

        P.dma(P.pool, wfa[:], w_in[:, :, 3072:3080], wfab, writes=[wfab])
        P.dma(P.sp, bfb[:], io.ab_b_f[j].partition_broadcast(128), bfbb, writes=[bfbb])
        P.dma(P.sp, gnb[:], io.hgrn_gnorm[j].partition_broadcast(128), gnbb, writes=[gnbb])
        for t in range(NTL):
            for kc in range(KC):
                P.op(P.pe, lambda e: e.matmul(PS[0][:, t * H:(t + 1) * H], lhsT=k.hT[:, kc, t * 128:(t + 1) * 128],
                                              rhs=wfa[:, kc, :], start=(kc == 0), stop=(kc == KC - 1)),
                     reads=[k.hTb[t], wfab], writes=[PB[0]], signal=(kc == KC - 1))
            P.op(P.dve, lambda e: e.tensor_tensor(out=lf[:, t, :], in0=PS[0][:, t * H:(t + 1) * H], in1=bfb[:], op=ALU.add),
                 reads=[PB[0], bfbb], writes=[lfb])
        lfl = lf[:, 0:NTL, :]
        P.op(P.act, lambda e: e.activation(out=lfl, in_=lfl, func=AF.Exp, scale=-1.0), reads=[lfb], writes=[lfb])
        P.op(P.dve, lambda e: e.tensor_scalar(out=lfl, in0=lfl, scalar1=1.0, scalar2=None, op0=ALU.add), reads=[lfb], writes=[lfb])
        P.op(P.act, lambda e: e.activation(out=lfl, in_=lfl, func=AF.Ln), reads=[lfb], writes=[lfb])
        P.op(P.dve, lambda e: e.tensor_scalar(out=lfl, in0=lfl, scalar1=-1.0, scalar2=None, op0=ALU.mult), reads=[lfb], writes=[lfb])
        for t in range(NTL):
            lt, ltb = lfT[t % 2]
            P.op(P.pe, lambda e: e.transpose(out=PS[1][0:H, 0:128], in_=lf[:, t, :], identity=k.ident[:]),
                 reads=[lfb, k.identb], writes=[PB[1]])
            P.op(P.act, lambda e: e.activation(out=lt[:], in_=PS[1][0:H, 0:128], func=AF.Copy),
                 reads=[PB[1]], writes=[ltb])
            if HS and t == 8:
                for s_ in range(2):
                    P.dma(P.sp, io.logf_s[j, 2 * c + s_, :, :], lt[:, 64 * s_:64 * s_ + 64], ltb, reads=[ltb])
            else:
                P.dma(P.sp, io.logf_p[j, :, p0 + t * 128:p0 + (t + 1) * 128], lt[:], ltb, reads=[ltb])
        lfp = lf[:, 0:8, :]
        P.op(P.pe, lambda e: e.matmul(PS[2][:, 0:64], lhsT=k.maskU[:], rhs=lfp, start=True, stop=True),
             reads=[lfb, k.maskUb], writes=[PB[2]])
        P.op(P.pe, lambda e: e.matmul(PS[3][:, 0:64], lhsT=k.ones_f[:], rhs=lfp, start=True, stop=True),
             reads=[lfb, k.cstb], writes=[PB[3]])
        for t in range(8):
            prev = k.Lcarry[:, j, :] if t == 0 else Lref[:, t - 1, :]
            P.op(P.dve, lambda e: e.tensor_tensor(out=k.Lhist[:, j, gt0 + t, :], in0=PS[2][:, t * H:(t + 1) * H], in1=prev, op=ALU.add),
                 reads=[PB[2], k.Lhb[j], Lrefb], writes=[k.Lhb[j]])
            P.op(P.dve, lambda e: e.tensor_tensor(out=Lref[:, t, :], in0=PS[3][:, t * H:(t + 1) * H], in1=prev, op=ALU.add),
                 reads=[PB[3], k.Lhb[j], Lrefb], writes=[Lrefb])
        P.op(P.dve, lambda e: e.tensor_copy(out=k.Lcarry[:, j, :], in_=Lref[:, 7, :]), reads=[Lrefb], writes=[k.Lhb[j]])
        if HS:
            P.op(P.pe, lambda e: e.matmul(PS[2][:, 64:64 + H], lhsT=k.BU64[:], rhs=lf[:, 8, :], start=True, stop=True),
                 reads=[lfb, k.cstb], writes=[PB[2]])
            P.op(P.dve, lambda e: e.tensor_scalar(out=Bn[:], in0=PS[2][:, 64:64 + H], scalar1=-1.0, scalar2=None, op0=ALU.mult),
                 reads=[PB[2]], writes=[Bnb])

        COLS = [0, 1024, 2048, 3080, 4104, 5128, 6152]
        nload = [0]

        def load_next():
            n = nload[0]
            if n >= 7 * H:
                return
            nload[0] += 1
            hh, bi = divmod(n, 7)
            c0 = COLS[bi] + hh * 128
            if DBG.get('dbgcol') and bi == 6:
                c0 = COLS[5]
            if True:
                for hf in range(2):
                    P.dma(P.pool, wr[n % NR][0][:, hf * 8:(hf + 1) * 8, :], w_in[:, hf * 8:(hf + 1) * 8, c0:c0 + 128], wr[n % NR][1], writes=[wr[n % NR][1]])
                return
            P.dma(P.pool, wr[n % NR][0][:], w_in[:, :, c0:c0 + 128], wr[n % NR][1], writes=[wr[n % NR][1]])

        def slot_of(h, bi):
            return wr[(h * 7 + bi) % NR]
        cntB = [0]
        cntA = [0]

        def projT(w, wb, evac):
            for (t0, tn) in ch.groups():
                pb = cntB[0] % 2
                cntB[0] += 1
                tiles = [k.hTb[t] for t in range(t0 // 128, (t0 + tn) // 128)]
                for kc in range(KC):
                    P.op(P.pe, lambda e: e.matmul(PS[pb][:, 0:tn], lhsT=w[:, kc, :], rhs=k.hT[:, kc, t0:t0 + tn],
                                                  start=(kc == 0), stop=(kc == KC - 1)),
                         reads=[wb] + tiles, writes=[PB[pb]], signal=(kc == KC - 1))
                evac(PS[pb], PB[pb], t0, tn)
            if not DBG.get('noload') and nload[0] < DBG.get('maxload', 999):
                load_next()

        def projA(w, wb, evac):
            for t in range(NTL):
                pb = 2 + cntA[0] % 2
                cntA[0] += 1
                for kc in range(KC):
                    P.op(P.pe, lambda e: e.matmul(PS[pb][:, 0:128], lhsT=k.hT[:, kc, t * 128:(t + 1) * 128], rhs=w[:, kc, :],
                                                  start=(kc == 0), stop=(kc == KC - 1)),
                         reads=[wb, k.hTb[t]], writes=[PB[pb]], signal=(kc == KC - 1))
                evac(PS[pb], PB[pb], t)
            load_next()

        for _ in range(NR):
            load_next()
        for h in range(DBG['heads']):
            P.dma(P.sp, lbt[:, 0, :], io.hgrn_lb[0, h * 128:(h + 1) * 128].partition_broadcast(128), lbtb, writes=[lbtb])
            P.dma(P.sp, lbt[:, 1, :], io.hgrn_lb[1, h * 128:(h + 1) * 128].partition_broadcast(128), lbtb, writes=[lbtb])
            P.op(P.dve, lambda e: e.tensor_tensor(out=lbo[:, 1, :], in0=lbt[:, 1, :], in1=lbt[:, 0, :], op=ALU.subtract),
                 reads=[lbtb], writes=[lbob])
            P.op(P.act, lambda e: e.activation(out=lbo[:, 1, :], in_=lbo[:, 1, :], func=AF.Sigmoid), reads=[lbob], writes=[lbob])
            if j == 0:
                P.op(P.dve, lambda e: e.tensor_scalar(out=lbo[:, 0, :], in0=lbo[:, 1, :], scalar1=0.0, scalar2=None, op0=ALU.mult),
                     reads=[lbob], writes=[lbob])
            else:
                P.op(P.dve, lambda e: e.tensor_copy(out=lbo[:, 0, :], in_=lbo[:, 1, :]), reads=[lbob], writes=[lbob])
            P.op(P.dve, lambda e: e.tensor_scalar(out=lbo[:, 1, :], in0=lbo[:, 0, :], scalar1=-1.0, scalar2=1.0,
                                                  op0=ALU.mult, op1=ALU.add), reads=[lbob], writes=[lbob])
            if c > 0:
                P.dma(P.sp, kTall[:, 0:p0], io.kth[j, h, :, 0:p0], kTallb, reads=[kthb], writes=[kTallb])
                P.dma(P.sp, vall[:, 0:gt0, :], io.vth[j, h, 0:p0, :].rearrange("(t p) d -> p t d", p=128), vallb,
                      reads=[vthb], writes=[vallb])
            def out_rows(dst_p, dst_s, src, srcb, t):
                if HS and t == 8:
                    for s_ in range(2):
                        P.dma(P.sp, dst_s[j, 2 * c + s_, h, :, :], src[64 * s_:64 * s_ + 64, :], srcb, reads=[srcb])
                else:
                    P.dma(P.sp, dst_p[j, h, p0 + t * 128:p0 + (t + 1) * 128, :], src[:], srcb, reads=[srcb])

            wq, wqb = slot_of(h, 0)

            def ev_q(ps, psb, t0, tn):
                P.op(P.act, lambda e: e.activation(out=qT[:, t0:t0 + tn], in_=ps[:, 0:tn], func=AF.Copy, scale=SCALE),
                     reads=[psb], writes=[qTb])
            if DBG['nproj'] > 0 and not DBG.get('skipq'):
                projT(wq, wqb, ev_q)
            wk, wkb = slot_of(h, 1)

            def ev_k(ps, psb, t0, tn):
                P.op(P.act, lambda e: e.activation(out=kT32[:, 0:tn], in_=ps[:, 0:tn], func=AF.Copy),
                     reads=[psb], writes=[kT32b])
                for tt in range(tn // 128 if DBG.get('kout', True) else 0):
                    t = t0 // 128 + tt
                    pb2 = 2 + cntA[0] % 2
                    cntA[0] += 1
                    sk, skb = st32[1]
                    P.op(P.pe, lambda e: e.transpose(out=PS[pb2][:, 0:128], in_=kT32[:, tt * 128:(tt + 1) * 128], identity=k.ident[:]),
                         reads=[kT32b, k.identb], writes=[PB[pb2]])
                    P.op(P.act, lambda e: e.activation(out=sk[:], in_=PS[pb2][:, 0:128], func=AF.Copy), reads=[PB[pb2]], writes=[skb])
                    out_rows(io.k_p, io.k_s, sk, skb, t)
                if not DBG.get('kdve', True):
                    pass
                elif t0 < 1024:
                    P.op(P.act, lambda e: e.activation(out=(oaT[:, t0:t0 + tn] if DBG.get('dst') else kTall[:, p0 + t0:p0 + t0 + tn]), in_=ps[:, 0:tn], func=AF.Copy),
                         reads=[psb], writes=[kTallb])
                else:
                    P.op(P.act, lambda e: e.activation(out=kTs[:], in_=ps[:, 0:tn], func=AF.Copy), reads=[psb], writes=[kTsb])
            if DBG['nproj'] > 1:
                projT(wk, wkb, ev_k)
            wv, wvb = slot_of(h, 2)

            def ev_v(ps, psb, t):
                sv, svb = st32[0]
                P.op(P.act, lambda e: e.activation(out=sv[:], in_=ps[:, 0:128], func=AF.Copy), reads=[psb], writes=[svb])
                if HS and t == 8:
                    P.op(P.act, lambda e: e.activation(out=vs[:], in_=ps[:, 0:128], func=AF.Copy), reads=[psb], writes=[vsb])
                else:
                    P.op(P.act, lambda e: e.activation(out=vall[:, gt0 + t, :], in_=ps[:, 0:128], func=AF.Copy), reads=[psb], writes=[vallb])
                out_rows(io.v_p, io.v_s, sv, svb, t)
            if DBG['nproj'] > 2:
                projA(wv, wvb, ev_v)
            wqb_, wqbb = slot_of(h, 3)

            def ev_qb(ps, psb, t0, tn):
                P.op(P.act, lambda e: e.activation(out=qbs[:, t0:t0 + tn], in_=ps[:, 0:tn], func=AF.Silu), reads=[psb], writes=[qbsb])
            if DBG['nproj'] > 3:
                projT(wqb_, wqbb, ev_qb)
            wf, wfb_ = slot_of(h, 4)

            def ev_fb(ps, psb, t):
                P.op(P.act, lambda e: e.activation(out=tE[:], in_=ps[:, 0:128], func=AF.Sigmoid), reads=[psb], writes=[tEb])
                P.op(P.dve, lambda e: e.tensor_tensor(out=tK[:], in0=tE[:], in1=lbo[:, 1, :], op=ALU.mult),
                     reads=[tEb, lbob], writes=[tKb])
                P.op(P.dve, lambda e: e.tensor_tensor(out=k_tm[:, t, :], in0=lbo[:, 1, :], in1=tK[:], op=ALU.subtract),
                     reads=[tKb, lbob], writes=[k_tmb])
                P.op(P.dve, lambda e: e.tensor_tensor(out=tE[:], in0=tK[:], in1=lbo[:, 0, :], op=ALU.add),
                     reads=[tKb, lbob], writes=[tEb])
                P.op(P.dve, lambda e: e.tensor_scalar(out=tE[:], in0=tE[:], scalar1=TINY, scalar2=None, op0=ALU.max),
                     reads=[tEb], writes=[tEb])
                P.op(P.act, lambda e: e.activation(out=g_tm[:, t, :], in_=tE[:], func=AF.Ln), reads=[tEb], writes=[g_tmb])
            if DBG['nproj'] > 4:
                projA(wf, wfb_, ev_fb)
            wi, wib = slot_of(h, 5)

            def ev_ib(ps, psb, t):
                P.op(P.act, lambda e: e.activation(out=i_bf[:, t, :], in_=ps[:, 0:128], func=AF.Copy), reads=[psb], writes=[i_bfb])
            if DBG['nproj'] > 5:
                projA(wi, wib, ev_ib)
            wg_, wgb_ = slot_of(h, 6)

            def ev_gb(ps, psb, t):
                P.op(P.act, lambda e: e.activation(out=gate[:, t, :], in_=ps[:, 0:128], func=AF.Silu), reads=[psb], writes=[gateb])
            if DBG['nproj'] > 6:
                projA(wg_, wgb_, ev_gb)
            if c < NCHUNK - 1 and DBG['store']:
                P.dma(P.sp, io.kth[j, h, :, p0:p0 + 1024], kTall[:, p0:p0 + 1024], kthb, reads=[kTallb], writes=[kthb])
                P.dma(P.sp, io.vth[j, h, p0:p0 + 1024, :].rearrange("(t p) d -> p t d", p=128), vall[:, gt0:gt0 + 8, :], vthb,
                      reads=[vallb], writes=[vthb])

            SBK = [0, 1, 4, 5] if DBG.get('sbk4', True) else [4, 5, 4, 5]
            DEPTH = DBG.get('depth', 3)
            nq = 8 if DBG['fox'] else 0
            for iq in range(nq):
                gi = gt0 + iq
                P.op(P.dve, lambda e: e.tensor_scalar(out=Bq[:, iq, 0:gi + 1], in0=k.Lhist[:, j, 0:gi + 1, h], scalar1=Lref[:, iq, h:h + 1],
                                                      scalar2=-1.0, op0=ALU.subtract, op1=ALU.mult),
                     reads=[k.Lhb[j], Lrefb], writes=[Bqb])
            pairs = [(iq, jj) for iq in range(nq) for jj in range(gt0 + iq + 1)]

            def st1(n):
                iq, jj = pairs[n]
                pb = SBK[n % 4]
                pt, ptb = pT[n % 4]
                P.op(P.pe, lambda e: e.matmul(PS[pb][:, 0:128], lhsT=kTall[:, jj * 128:(jj + 1) * 128],
                                              rhs=qT[:, iq * 128:(iq + 1) * 128], start=True, stop=True),
                     reads=[kTallb, qTb], writes=[PB[pb]])
                P.op(P.act, lambda e: e.activation(out=pt[:], in_=PS[pb][:, 0:128], func=AF.Exp, bias=Bq[:, iq, jj:jj + 1]),
                     reads=[PB[pb], Bqb], writes=[ptb])
                if jj == gt0 + iq:
                    P.op(P.dve, lambda e: e.tensor_tensor(out=pt[:], in0=pt[:], in1=k.maskU[:], op=ALU.mult),
                         reads=[ptb, k.maskUb], writes=[ptb])

            def st2(n):
                iq, jj = pairs[n]
                gi = gt0 + iq
                a_ = (iq % 2) if DBG.get('halves', False) else 0
                ca = slice(a_ * 128, (a_ + 1) * 128)
                pt, ptb = pT[n % 4]
                P.op(P.pe, lambda e: e.matmul(PS[6][:, ca], lhsT=vall[:, jj, :], rhs=pt[:], start=(jj == 0), stop=(jj == gi)),
                     reads=[vallb, ptb], writes=[acc6[a_], PB[6]], signal=False)
                P.op(P.pe, lambda e: e.matmul(PS[7][:, ca], lhsT=k.ones_b[:], rhs=pt[:], start=(jj == 0), stop=(jj == gi)),
                     reads=[k.cstb, ptb], writes=[acc7[a_], PB[7]])
                if jj == gi:
                    rc, rcb = recs[a_]
                    last = [PB[6], PB[7]] if iq == nq - 1 else []
                    P.op(P.dve, lambda e: e.reciprocal(out=rc[:], in_=PS[7][:, ca]), reads=[acc7[a_]] + last, writes=[rcb])
                    P.op(P.dve, lambda e: e.tensor_tensor(out=oaT[:, iq * 128:(iq + 1) * 128], in0=PS[6][:, ca], in1=rc[:], op=ALU.mult),
                         reads=[acc6[a_], rcb] + last, writes=[oaTb])
            for n in range(len(pairs) + DEPTH):
                if n < len(pairs):
                    st1(n)
                if n >= DEPTH:
                    st2(n - DEPTH)

            if HS and DBG['samp']:
                for s_ in range(2):
                    b = 2 * c + s_
                    qs = qT[:, 1024 + 64 * s_:1024 + 64 * s_ + 64]
                    P.dma(P.pool, vall[:], io.cache_v[j, b, h, :, :].rearrange("(t p) d -> p t d", p=128), vallb, writes=[vallb])
                    for r in range(8):
                        kc_, kcb = kc32[0]
                        P.dma(P.sp, kc_[:], io.cache_k[j, b, h, r * 512:(r + 1) * 512, :].rearrange("(t p) d -> p t d", p=128),
                              kcb, writes=[kcb])
                        pb = r % 2
                        for q4 in range(4):
                            P.op(P.pe, lambda e: e.transpose(out=PS[pb][:, q4 * 128:(q4 + 1) * 128], in_=kc_[:, q4, :], identity=k.ident[:]),
                                 reads=[kcb, k.identb], writes=[PB[pb]], signal=(q4 == 3))
                        P.op(P.act, lambda e: e.activation(out=kTall[:, r * 512:(r + 1) * 512], in_=PS[pb][:, :], func=AF.Copy),
                             reads=[PB[pb]], writes=[kTallb])
                    P.dma(P.sp, lfc_r[:], io.cache_logf[j, b, h, :].rearrange("(t p) -> t p", p=128), lfc_rb, writes=[lfc_rb])
                    P.op(P.pe, lambda e: e.transpose(out=PS[2][:, 0:32], in_=lfc_r[:], identity=k.ident[0:32, 0:32]),
                         reads=[lfc_rb, k.identb], writes=[PB[2]])
                    P.op(P.act, lambda e: e.activation(out=lfc[:, 3, :], in_=PS[2][:, 0:32], func=AF.Copy), reads=[PB[2]], writes=[lfcb])
                    P.op(P.pe, lambda e: e.matmul(PS[2][:, 0:32], lhsT=k.maskU[:], rhs=lfc[:, 3, :], start=True, stop=True),
                         reads=[lfcb, k.maskUb], writes=[PB[2]])
                    P.op(P.pe, lambda e: e.matmul(PS[3][:, 0:32], lhsT=k.ones_f[:], rhs=lfc[:, 3, :], start=True, stop=True),
                         reads=[lfcb, k.cstb], writes=[PB[3]])
                    P.op(P.act, lambda e: e.activation(out=lfc[:, 1, :], in_=PS[3][:, 0:32], func=AF.Copy), reads=[PB[3]], writes=[lfcb])
                    P.op(P.dve, lambda e: e.tensor_tensor_scan(out=lfc[:, 2, :], data0=k.ones_f[:, 0:32], data1=lfc[:, 1, :], initial=0.0,
                                                               op0=ALU.mult, op1=ALU.add),
                         reads=[lfcb, k.cstb], writes=[lfcb])
                    P.op(P.dve, lambda e: e.tensor_tensor(out=lfc[:, 0, :], in0=lfc[:, 2, :], in1=lfc[:, 1, :], op=ALU.subtract),
                         reads=[lfcb], writes=[lfcb])
                    P.op(P.dve, lambda e: e.tensor_tensor(out=lfc[:, 0, :], in0=PS[2][:, 0:32], in1=lfc[:, 0, :], op=ALU.add),
                         reads=[lfcb, PB[2]], writes=[lfcb])
                    P.op(P.dve, lambda e: e.tensor_scalar(out=lfc[:, 3, :], in0=lfc[:, 0, :], scalar1=lfc[:, 2, 31:32], scalar2=-1.0,
                                                          op0=ALU.subtract, op1=ALU.mult),
                         reads=[lfcb], writes=[lfcb])
                    r0, r1 = 64 * s_, 64 * s_ + 64

                    def sm1(jj):
                        pb = SBK[jj % 4]
                        pt, ptb = pT[jj % 4]
                        if jj < 32:
                            P.op(P.pe, lambda e: e.matmul(PS[pb][:, 0:64], lhsT=kTall[:, jj * 128:(jj + 1) * 128], rhs=qs, start=True, stop=True),
                                 reads=[kTallb, qTb], writes=[PB[pb]])
                            P.op(P.act, lambda e: e.activation(out=pt[:, 0:64], in_=PS[pb][:, 0:64], func=AF.Exp, bias=lfc[:, 3, jj:jj + 1]),
                                 reads=[PB[pb], lfcb], writes=[ptb])
                        else:
                            P.op(P.pe, lambda e: e.matmul(PS[pb][r0:r1, 0:64], lhsT=kTs[:, r0:r1], rhs=qs, start=True, stop=True),
                                 reads=[kTsb, qTb], writes=[PB[pb]])
                            P.op(P.act, lambda e: e.activation(out=pt[r0:r1, 0:64], in_=PS[pb][r0:r1, 0:64], func=AF.Exp, bias=Bn[r0:r1, h:h + 1]),
                                 reads=[PB[pb], Bnb], writes=[ptb])
                            P.op(P.dve, lambda e: e.tensor_tensor(out=pt[r0:r1, 0:64], in0=pt[r0:r1, 0:64], in1=k.maskU[r0:r1, r0:r1], op=ALU.mult),
                                 reads=[ptb, k.maskUb], writes=[ptb])

                    def sm2(jj):
                        pt, ptb = pT[jj % 4]
                        if jj < 32:
                            P.op(P.pe, lambda e: e.matmul(PS[6][:, 0:64], lhsT=vall[:, jj, :], rhs=pt[:, 0:64], start=(jj == 0), stop=False),
                                 reads=[vallb, ptb], writes=[PB[6]], signal=False)
                            P.op(P.pe, lambda e: e.matmul(PS[7][:, 0:64], lhsT=k.ones_b[:], rhs=pt[:, 0:64], start=(jj == 0), stop=False),
                                 reads=[k.cstb, ptb], writes=[PB[7]])
                        else:
                            P.op(P.pe, lambda e: e.matmul(PS[6][:, 0:64], lhsT=vs[r0:r1, :], rhs=pt[r0:r1, 0:64], start=False, stop=True),
                                 reads=[vsb, ptb], writes=[PB[6]], signal=False)
                            P.op(P.pe, lambda e: e.matmul(PS[7][:, 0:64], lhsT=k.ones_b[r0:r1, :], rhs=pt[r0:r1, 0:64], start=False, stop=True),
                                 reads=[k.cstb, ptb], writes=[PB[7]])
                    for n in range(33 + DEPTH):
                        if n < 33:
                            sm1(n)
                        if n >= DEPTH:
                            sm2(n - DEPTH)
                    P.op(P.dve, lambda e: e.reciprocal(out=rec[:, 0:64], in_=PS[7][:, 0:64]), reads=[PB[7]], writes=[recb])
                    P.op(P.dve, lambda e: e.tensor_tensor(out=oaT[:, 1024 + 64 * s_:1024 + 64 * s_ + 64], in0=PS[6][:, 0:64], in1=rec[:, 0:64], op=ALU.mult),
                         reads=[PB[6], recb], writes=[oaTb])

            (S0, S0b), (S1, S1b) = S32
            (Sb0, Sb0b), (Sb1, Sb1b) = Sbf
            if c == 0:
                P.op(P.pool, lambda e: e.memset(S0[:], 0.0), writes=[S0b])
            else:
                P.dma(P.sp, S0[:], io.sth[j, h, :, :], S0b, reads=[sthb], writes=[S0b])
            P.op(P.act, lambda e: e.activation(out=Sb0[:], in_=S0[:], func=AF.Copy), reads=[S0b], writes=[Sb0b])
            for t in range(NTL if DBG['hgrn'] else 0):
                smp = HS and t == 8
                tc_ = slice(t * 128, (t + 1) * 128)
                if smp:
                    for u, (Su, Sub, Sbu, Sbub) in enumerate(((S0, S0b, Sb0, Sb0b), (S1, S1b, Sb1, Sb1b))):
                        P.dma(P.sp, Su[:], io.state_hgrn[j, 2 * c + u, h, :, :], Sub, writes=[Sub])
                        P.op(P.act, lambda e: e.activation(out=Sbu[:], in_=Su[:], func=AF.Copy), reads=[Sub], writes=[Sbub])
                P.op(P.pe, lambda e: e.matmul(PS[0][:, 0:128], lhsT=g_tm[:, t, :], rhs=k.BU64[:], start=True, stop=True),
                     reads=[g_tmb, k.cstb], writes=[PB[0]])
                P.op(P.pe, lambda e: e.matmul(PS[1][:, 0:128], lhsT=k.BR64[:], rhs=g_tm[:, t, :], start=True, stop=True),
                     reads=[g_tmb, k.cstb], writes=[PB[1]])
                P.op(P.dve, lambda e: e.tensor_copy(out=tR[:], in_=k_tm[:, t, :]), reads=[k_tmb], writes=[tRb])
                P.op(P.pe, lambda e: e.transpose(out=PS[2][:, 0:128], in_=tR[:], identity=k.ident[:]),
                     reads=[tRb, k.identb], writes=[PB[2]])
                for u in range(2):
                    cu = slice(64 * u, 64 * u + 64)
                    mid = 64 * u + 31
                    P.op(P.dve, lambda e: e.tensor_scalar(out=rcol[:, 2 * u:2 * u + 1], in0=PS[0][:, mid:mid + 1], scalar1=1.0, scalar2=None, op0=ALU.mult), reads=[PB[0]], writes=[rcolb])
                    P.op(P.dve, lambda e: e.tensor_scalar(out=rcol[:, 2 * u + 1:2 * u + 2], in0=PS[0][:, mid:mid + 1], scalar1=-1.0, scalar2=None,
                                                          op0=ALU.mult), reads=[PB[0]], writes=[rcolb])
                    P.op(P.act, lambda e: e.activation(out=tE[:, cu], in_=PS[0][:, cu], func=AF.Exp, bias=rcol[:, 2 * u + 1:2 * u + 2]),
                         reads=[PB[0], rcolb], writes=[tEb])
                    P.op(P.act, lambda e: e.activation(out=tK[:, cu], in_=PS[0][:, cu], func=AF.Exp, bias=rcol[:, 2 * u:2 * u + 1], scale=-1.0),
                         reads=[PB[0], rcolb], writes=[tKb])
                P.op(P.act, lambda e: e.activation(out=tG[:], in_=PS[0][:, 0:128], func=AF.Exp), reads=[PB[0]], writes=[tGb])
                P.op(P.act, lambda e: e.activation(out=tR[:], in_=PS[1][:, 0:128], func=AF.Exp), reads=[PB[1]], writes=[tRb])
                P.op(P.dve, lambda e: e.tensor_tensor(out=qp[:], in0=qbs[:, tc_], in1=tE[:], op=ALU.mult), reads=[qbsb, tEb], writes=[qpb])
                P.op(P.dve, lambda e: e.tensor_tensor(out=kp[:], in0=PS[2][:, 0:128], in1=tK[:], op=ALU.mult), reads=[PB[2], tKb], writes=[kpb])
                P.op(P.dve, lambda e: e.tensor_tensor(out=qpp[:], in0=qbs[:, tc_], in1=tG[:], op=ALU.mult), reads=[qbsb, tGb], writes=[qppb])
                P.op(P.dve, lambda e: e.tensor_tensor(out=kpp[:], in0=k_tm[:, t, :], in1=tR[:], op=ALU.mult), reads=[k_tmb, tRb], writes=[kppb])
                P.op(P.pe, lambda e: e.matmul(PS[3][:, 0:128], lhsT=kp[:], rhs=qp[:], start=True, stop=True),
                     reads=[kpb, qpb], writes=[PB[3]])
                P.op(P.dve, lambda e: e.tensor_tensor(out=ATm[:], in0=PS[3][:, 0:128], in1=k.BU64[:], op=ALU.mult),
                     reads=[PB[3], k.cstb], writes=[ATmb])
                P.op(P.pe, lambda e: e.matmul(PS[4][:, 0:128], lhsT=ATm[:], rhs=i_bf[:, t, :], start=True, stop=False),
                     reads=[ATmb, i_bfb], writes=[PB[4]], signal=False)
                for u in range(2):
                    r0, r1 = 64 * u, 64 * u + 64
                    if smp:
                        Su, Sub, Sbu, Sbub = ((S0, S0b, Sb0, Sb0b), (S1, S1b, Sb1, Sb1b))[u]
                    else:
                        Su, Sub, Sbu, Sbub = S0, S0b, Sb0, Sb0b
                    P.op(P.pe, lambda e: e.matmul(PS[4][r0:r1, 0:128], lhsT=qpp[:, r0:r1], rhs=Sbu[:], start=False, stop=(u == 1)),
                         reads=[qppb, Sbub], writes=[PB[4]], signal=(u == 1))
                    P.op(P.pe, lambda e: e.matmul(PS[5][:, 0:128], lhsT=kpp[r0:r1, :], rhs=i_bf[r0:r1, t, :], start=True, stop=True),
                         reads=[kppb, i_bfb], writes=[PB[5]])
                    P.op(P.dve, lambda e: e.scalar_tensor_tensor(out=Su[:], in0=Su[:], scalar=tG[:, r1 - 1:r1], in1=PS[5][:, 0:128],
                                                                 op0=ALU.mult, op1=ALU.add),
                         reads=[Sub, tGb, PB[5]], writes=[Sub])
                    if smp:
                        P.dma(P.sp, io.hgrn_s[j, 2 * c + u, h, :, :], Su[:], Sub, reads=[Sub])
                    else:
                        P.op(P.act, lambda e: e.activation(out=Sbu[:], in_=Su[:], func=AF.Copy), reads=[Sub], writes=[Sbub])
                ss = stat[:, 4:5]
                rs = stat[:, 5:6]
                P.op(P.act, lambda e: e.activation(out=on[:], in_=PS[4][:, 0:128], func=AF.Square, accum_out=ss),
                     reads=[PB[4]], writes=[onb, k.statb])
                P.op(P.dve, lambda e: e.tensor_scalar(out=rs, in0=ss, scalar1=1.0 / HD, scalar2=EPS, op0=ALU.mult, op1=ALU.add),
                     reads=[k.statb], writes=[k.statb])
                P.op(P.act, lambda e: e.activation(out=rs, in_=rs, func=AF.Sqrt), reads=[k.statb], writes=[k.statb])
                P.op(P.dve, lambda e: e.reciprocal(out=rs, in_=rs), reads=[k.statb], writes=[k.statb])
                P.op(P.dve, lambda e: e.scalar_tensor_tensor(out=on[:], in0=PS[4][:, 0:128], scalar=rs, in1=gnb[:], op0=ALU.mult, op1=ALU.mult),
                     reads=[PB[4], k.statb, gnbb], writes=[onb])
                P.op(P.dve, lambda e: e.tensor_tensor(out=on[:], in0=on[:], in1=gate[:, t, :], op=ALU.mult), reads=[onb, gateb], writes=[onb])
                P.op(P.pe, lambda e: e.transpose(out=PS[6][:, 0:128], in_=on[:], identity=k.ident[:]), reads=[onb, k.identb], writes=[PB[6]])
                P.op(P.act, lambda e: e.activation(out=obT[:, tc_], in_=PS[6][:, 0:128], func=AF.Copy), reads=[PB[6]], writes=[obTb])
                if t == 7:
                    if c < NCHUNK - 1:
                        P.dma(P.sp, io.sth[j, h, :, :], S0[:], sthb, reads=[S0b], writes=[sthb])
                    else:
                        P.dma(P.sp, io.hgrn_p[j, h, :, :], S0[:], S0b, reads=[S0b])
            dcnt = 0
            for hf in range(2 if DBG['outp'] else 0):
                cs = slice(hf * 1024, (hf + 1) * 1024)
                P.dma(P.pool, wo[:, 0, :], w_out[h * 128:(h + 1) * 128, cs], wob, writes=[wob])
                P.dma(P.pool, wo[:, 1, :], w_out[1024 + h * 128:1024 + (h + 1) * 128, cs], wob, writes=[wob])
                for t in range(NTL):
                    for q in range(2):
                        pd = dcnt % 4
                        dcnt += 1
                        xs_ = slice(hf * 1024 + q * 512, hf * 1024 + (q + 1) * 512)
                        P.op(P.pe, lambda e: e.matmul(PS[pd][:, :], lhsT=oaT[:, t * 128:(t + 1) * 128], rhs=wo[:, 0, q * 512:(q + 1) * 512],
                                                      start=True, stop=False),
                             reads=[oaTb, wob], writes=[PB[pd]], signal=False)
                        P.op(P.pe, lambda e: e.matmul(PS[pd][:, :], lhsT=obT[:, t * 128:(t + 1) * 128], rhs=wo[:, 1, q * 512:(q + 1) * 512],
                                                      start=False, stop=True),
                             reads=[obTb, wob], writes=[PB[pd]])
                        P.op(P.dve, lambda e: e.tensor_tensor(out=k.x[:, t, xs_], in0=PS[pd][:, :], in1=k.x[:, t, xs_], op=ALU.add),
                             reads=[PB[pd], k.xb[t]], writes=[k.xb[t]])
        P.barrier()


def final_norm_out(P, k, nc, gam_final, y_p, y_s, ch):
    with ExitStack() as es:
        gb = es.enter_context(nc.sbuf_tensor(U("f_gb"), [128, D], F32))
        gbb = P.buf("f_gb")
        yt = [es.enter_context(nc.sbuf_tensor(U(f"f_y{i}"), [128, D], F32)) for i in range(2)]
        ytb = [P.buf(f"f_y{i}") for i in range(2)]
        P.dma(P.sp, gb[:], gam_final.partition_broadcast(128), gbb, writes=[gbb])
        for t in range(ch.ntile):
            i = t % 2
            ss = k.small[:, 0:1]
            rs = k.small[:, 1:2]
            P.op(P.act, lambda e: e.activation(out=yt[i][:], in_=k.x[:, t, :], func=AF.Square, accum_out=ss),
                 reads=[k.xb[t]], writes=[ytb[i], k.ssb])
            P.op(P.dve, lambda e: e.tensor_scalar(out=rs, in0=ss, scalar1=1.0 / D, scalar2=EPS, op0=ALU.mult, op1=ALU.add),
                 reads=[k.ssb], writes=[k.rsb])
            P.op(P.act, lambda e: e.activation(out=rs, in_=rs, func=AF.Sqrt), reads=[k.rsb], writes=[k.rsb])
            P.op(P.dve, lambda e: e.reciprocal(out=rs, in_=rs), reads=[k.rsb], writes=[k.rsb])
            P.op(P.dve, lambda e: e.scalar_tensor_tensor(out=yt[i][:], in0=k.x[:, t, :], scalar=rs, in1=gb[:],
                                                         op0=ALU.mult, op1=ALU.mult),
                 reads=[k.xb[t], k.rsb, gbb], writes=[ytb[i]])
            if ch.has_sample and t == 8:
                dst = y_s[ch.c * 128:(ch.c + 1) * 128, :]
            else:
                dst = y_p[ch.p0 + t * 128:ch.p0 + (t + 1) * 128, :]
            P.dma(P.sp, dst, yt[i][:], ytb[i], reads=[ytb[i]])
        P.barrier()


def load_x(P, k, nc, x_p, x_s, ch):
    for t in range(ch.ntile):
        if ch.has_sample and t == 8:
            src = x_s[ch.c * 128:(ch.c + 1) * 128, :]
        else:
            src = x_p[ch.p0 + t * 128:ch.p0 + (t + 1) * 128, :]
        P.dma(P.sp, k.x[:, t, :], src, k.xb[t], writes=[k.xb[t]])


def load_consts(P, k, nc, gam):
    make_identity(P, k, nc)
    with nc.allow_non_contiguous_dma(reason="tiny gamma load"):
        P.dma(P.sp, k.gam[:], gam.rearrange("g (c p) -> p g c", p=128), k.gamb, writes=[k.gamb])


IN_SPECS = [
    ("x_p", [SEQ, D]), ("x_s", [256, D]),
    ("cache_k", [2, 4, H, SEQ, HD]), ("cache_v", [2, 4, H, SEQ, HD]), ("cache_logf", [2, 4, H, SEQ]),
    ("state_hgrn", [2, 4, H, HD, HD]),
    ("gam", [13, D]),
    ("ffn1_gate", [4, D, DFF]), ("ffn1_up", [4, D, DFF]), ("ffn1_down", [4, DFF, D]),
    ("ffn2_gate", [4, D, DFF]), ("ffn2_up", [4, D, DFF]), ("ffn2_down", [4, DFF, D]),
    ("ab_w_in", [2, D, AB_IN]), ("ab_b_f", [2, H]), ("hgrn_lb", [2, 1024]), ("hgrn_gnorm", [2, HD]),
    ("ab_w_out", [2, D, D]),
    ("c_w_in", [2, D, 2 * D]), ("c_ln_g", [2, D]), ("c_ln_b", [2, D]), ("c_w_s", [2, 8, 128, 128]),
    ("c_b_s", [2, 8, 128]), ("c_w_out", [2, D, D]),
]
OUT_SPECS = [
    ("y_p", [SEQ, D]), ("y_s", [256, D]),
    ("k_p", [2, H, SEQ, HD]), ("v_p", [2, H, SEQ, HD]), ("logf_p", [2, H, SEQ]), ("hgrn_p", [2, H, HD, HD]),
    ("k_s", [2, 4, H, 64, HD]), ("v_s", [2, 4, H, 64, HD]), ("logf_s", [2, 4, H, 64]), ("hgrn_s", [2, 4, H, HD, HD]),
    ("cv_s", [2, 256, D]),
]


def build(layers=(0, 1, 2, 3), do_ffn=True, chunks=(0, 1, 2, 3)):
    nc = bass.Bass("TRN2", target_bir_lowering=False)
    io = K()
    for n, shp in IN_SPECS:
        setattr(io, n, nc.dram_tensor(n, shp, F32, kind="ExternalInput").ap())
    for n, shp in OUT_SPECS:
        setattr(io, n, nc.dram_tensor(n, shp, F32, kind="ExternalOutput").ap())
    io.kth = nc.dram_tensor("kth", [2, H, HD, SEQ], BF16).ap()
    io.vth = nc.dram_tensor("vth", [2, H, SEQ, HD], BF16).ap()
    io.sth = nc.dram_tensor("sth", [2, H, HD, HD], F32).ap()
    with ExitStack() as es:
        P = Prog(nc, es)
        k = K()
        setup_persistent(P, k, es, nc)
        load_consts(P, k, nc, io.gam)
        for c in chunks:
            ch = Chunk(c)
            load_x(P, k, nc, io.x_p, io.x_s, ch)
            for l in layers:
                j = l // 2
                if do_ffn:
                    ffn_phase(P, k, nc, io.ffn1_gate[l], io.ffn1_up[l], io.ffn1_down[l], l, ch)
                if l % 2 == 0:
                    abmix(P, k, nc, j, ch, io, 4 + l)
                else:
                    cmix(P, k, nc, io.c_w_in[j], io.c_ln_g[j], io.c_ln_b[j], io.c_w_s[j], io.c_b_s[j], io.c_w_out[j],
                         io.cv_s[j], 4 + l, ch)
                if do_ffn:
                    ffn_phase(P, k, nc, io.ffn2_gate[l], io.ffn2_up[l], io.ffn2_down[l], 8 + l, ch)
            final_norm_out(P, k, nc, io.gam[12, :], io.y_p, io.y_s, ch)
        P.finish()
    return nc


def make_in_maps(inp):
    f = lambda a: np.ascontiguousarray(a, dtype=np.float32)
    gam = np.concatenate([inp["norm_ffn1"], inp["norm_mix"], inp["norm_ffn2"], inp["norm_final"][None, :]], axis=0)
    shared = {n: f(inp[n]) for n, _ in IN_SPECS if n in inp}
    shared["gam"] = f(gam)
    maps = []
    for core in range(NCORE):
        m = dict(shared)
        m["x_p"] = f(inp["x_prompt"][core % 2])
        sl = slice(4 * core, 4 * core + 4)
        m["x_s"] = f(inp["x_sample"][sl].reshape(256, D))
        m["cache_k"] = f(inp["cache_k"][:, sl])
        m["cache_v"] = f(inp["cache_v"][:, sl])
        m["cache_logf"] = f(inp["cache_logf"][:, sl])
        m["state_hgrn"] = f(inp["state_hgrn"][:, sl])
        maps.append(m)
    return maps


def gather_outputs(res):
    r = res
    cat = lambda n, ax: np.concatenate([r[c][n] for c in range(NCORE)], axis=ax)
    y_prompt = np.stack([r[0]["y_p"], r[1]["y_p"]])
    y_sample = np.concatenate([r[c]["y_s"].reshape(4, 64, D) for c in range(NCORE)], axis=0)
    stk = lambda n: np.stack([r[0][n], r[1][n]], axis=1)
    return (y_prompt, y_sample, stk("k_p"), stk("v_p"), stk("logf_p"), stk("hgrn_p"),
            cat("k_s", 1), cat("v_s", 1), cat("logf_s", 1), cat("hgrn_s", 1),
            np.concatenate([r[c]["cv_s"].reshape(2, 4, 64, D) for c in range(NCORE)], axis=1))


def kernel(**inputs):
    inp = {k_: np.asarray(v) for k_, v in inputs.items()}
    nc = build()
    maps = make_in_maps(inp)
    res = run_bass_kernel_spmd(nc, maps, core_ids=list(range(NCORE)))
    outs = gather_outputs(res.results)
    return tuple(np.ascontiguousarray(o, dtype=np.float32) for o in outs)
```

```python
import numpy as np
from contextlib import ExitStack
import concourse.bass as bass
import concourse.mybir as mybir
from concourse.bass_utils import run_bass_kernel_spmd

F32 = mybir.dt.float32
BF16 = mybir.dt.bfloat16
AF = mybir.ActivationFunctionType
ALU = mybir.AluOpType
AX = mybir.AxisListType

D = 2048
DFF = 5632
NCORE = 8
KC = D // 128
EPS = 1e-6
MAXT = 9
NTMAX = MAXT * 128
SEQ = 4096
NCHUNK = 4
H = 8
HD = 128
AB_IN = 7176
TINY = 1e-30
SCALE = HD ** -0.5


class Chunk:
    def __init__(self, c):
        self.c = c
        self.has_sample = c < 2
        self.ntile = 9 if self.has_sample else 8
        self.ntok = self.ntile * 128
        self.p0 = c * 1024
        self.gt0 = c * 8

    def groups(self):
        g = [(0, 512), (512, 512)]
        if self.has_sample:
            g.append((1024, 128))
        return g


_UID = [0]


def U(name):
    _UID[0] += 1
    return f"{name}_{_UID[0]}"


def sbuf_guard(nc):
    with nc.sbuf_tensor(U("guard"), [128, 4096 + 8], F32):
        pass


class Buf:
    __slots__ = ("name", "last_w", "readers", "sem", "dcnt")

    def __init__(self, name):
        self.name = name
        self.last_w = None
        self.readers = {}
        self.sem = None
        self.dcnt = 0


class Stream:
    def __init__(self, eng, name, sem, is_pe=False):
        self.eng = eng
        self.name = name
        self.sem = sem
        self.cnt = 0
        self.waited = {}
        self.is_pe = is_pe


class Prog:
    def __init__(self, nc, es):
        self.nc = nc
        self.es = es
        self.sems = {}
        mk = lambda n: es.enter_context(nc.semaphore(n))
        self.pe = Stream(nc.tensor, "pe", mk("s_pe"), is_pe=True)
        self.act = Stream(nc.scalar, "act", mk("s_act"))
        self.dve = Stream(nc.vector, "dve", mk("s_dve"))
        self.pool = Stream(nc.gpsimd, "pool", mk("s_pool"))
        self.sp = Stream(nc.sync, "sp", mk("s_sp"))
        self.streams = [self.pe, self.act, self.dve, self.pool, self.sp]
        self.semobj = {}
        for s in self.streams:
            self.semobj[id(s.sem)] = s
        self.dma_bufs = []
        self.bufs = {}

    def buf(self, name):
        b = self.bufs.get(name)
        if b is None:
            b = Buf(name)
            self.bufs[name] = b
        return b

    def _collect(self, reads, writes, extra=()):
        evs = {}

        def add(ev):
            if ev is None:
                return
            k = id(ev[0])
            if k not in evs or evs[k][1] < ev[1]:
                evs[k] = ev
        for b in reads:
            add(b.last_w)
        for b in writes:
            add(b.last_w)
            for ev in b.readers.values():
                add(ev)
        for ev in extra:
            add(ev)
        return evs

    def _wait(self, st, evs):
        for k, (sem, val) in evs.items():
            if sem is st.sem:
                if st.is_pe or val > st.cnt:
                    continue
            if st.waited.get(k, 0) >= val:
                continue
            st.eng.wait_ge(sem, val)
            st.waited[k] = val

    def _record(self, ev, reads, writes):
        k = id(ev[0])
        for b in reads:
            old = b.readers.get(k)
            if old is None or old[1] < ev[1]:
                b.readers[k] = ev
        for b in writes:
            b.last_w = ev
            b.readers = {}

    def op(self, st, fn, reads=(), writes=(), signal=True, extra=()):
        self._wait(st, self._collect(reads, writes, extra))
        ins = fn(st.eng)
        if signal:
            st.cnt += 1
            ins.then_inc(st.sem, 1)
            ev = (st.sem, st.cnt)
        else:
            ev = (st.sem, st.cnt + 1)
        self._record(ev, reads, writes)
        return ev

    def dma(self, st, out, in_, chan, reads=(), writes=(), extra=()):
        if chan.sem is None:
            chan.sem = self.es.enter_context(self.nc.semaphore("d_" + chan.name))
            self.dma_bufs.append(chan)
        self._wait(st, self._collect(reads, writes, extra))
        ins = st.eng.dma_start(out=out, in_=in_)
        chan.dcnt += 16
        ins.then_inc(chan.sem, 16)
        ev = (chan.sem, chan.dcnt)
        self._record(ev, reads, writes)
        return ev

    def barrier(self):
        evs = {}
        for s in self.streams:
            if s.cnt:
                evs[id(s.sem)] = (s.sem, s.cnt)
        for b in self.dma_bufs:
            evs[id(b.sem)] = (b.sem, b.dcnt)
        for s in self.streams:
            self._wait(s, dict(evs))

    def finish(self):
        evs = {}
        for b in self.dma_bufs:
            evs[id(b.sem)] = (b.sem, b.dcnt)
        for s in self.streams:
            if s.cnt:
                evs[id(s.sem)] = (s.sem, s.cnt)
        self._wait(self.sp, evs)


class K:
    pass


def setup_persistent(P, k, es, nc):
    sb = lambda name, shape, dt: es.enter_context(nc.sbuf_tensor(U(name), shape, dt))
    k.x = sb("x", [128, MAXT, D], F32)
    k.xb = [P.buf(f"x{t}") for t in range(MAXT)]
    k.hT = sb("hT", [128, KC, NTMAX], BF16)
    k.hTb = [P.buf(f"hT{t}") for t in range(MAXT)]
    k.xnb = P.buf("xn")
    k.ident = sb("ident", [128, 128], F32)
    k.identb = P.buf("ident")
    k.maskU = sb("maskU", [128, 128], F32)
    k.maskUb = P.buf("maskU")
    k.ones_f = sb("ones_f", [128, 128], F32)
    k.ones_b = sb("ones_b", [128, 128], BF16)
    k.BU64 = sb("BU64", [128, 128], F32)
    k.BR64 = sb("BR64", [128, 128], F32)
    k.cstb = P.buf("consts")
    k.Lhist = sb("Lhist", [128, 2, 32, H], F32)
    k.Lcarry = sb("Lcarry", [128, 2, H], F32)
    k.Lhb = [P.buf("Lhist0"), P.buf("Lhist1")]
    k.gam = sb("gam_sb", [128, 13, KC], F32)
    k.gamb = P.buf("gam")
    k.small = sb("small", [128, 64], F32)
    k.ssb = P.buf("ss")
    k.rsb = P.buf("rs")
    k.statb = P.buf("stat")
    k.psum = [es.enter_context(nc.psum_tensor(f"ps{i}", [128, 512], F32)) for i in range(8)]
    k.psb = [P.buf(f"ps{i}") for i in range(8)]


def make_identity(P, k, nc):
    P.op(P.pool, lambda e: e.memset(k.ident[:], 0.0), writes=[k.identb])
    P.op(P.pool, lambda e: e.affine_select(out=k.ident[:], in_=k.ident[:], pattern=[[-1, 128]],
                                           compare_op=ALU.not_equal, fill=1.0, base=0, channel_multiplier=1),
         reads=[k.identb], writes=[k.identb])
    P.op(P.pool, lambda e: e.memset(k.maskU[:], 1.0), writes=[k.maskUb])
    P.op(P.pool, lambda e: e.affine_select(out=k.maskU[:], in_=k.maskU[:], pattern=[[1, 128]],
                                           compare_op=ALU.is_ge, fill=0.0, base=0, channel_multiplier=-1),
         reads=[k.maskUb], writes=[k.maskUb])
    P.op(P.pool, lambda e: e.memset(k.ones_f[:], 1.0), writes=[k.cstb])
    P.op(P.pool, lambda e: e.memset(k.ones_b[:], 1.0), writes=[k.cstb])
    P.op(P.pool, lambda e: e.tensor_copy(out=k.BU64[:], in_=k.maskU[:]), reads=[k.maskUb], writes=[k.cstb])
    P.op(P.pool, lambda e: e.memset(k.BU64[0:64, 64:128], 0.0), writes=[k.cstb])
    P.op(P.pool, lambda e: e.memset(k.BR64[:], 1.0), writes=[k.cstb])
    P.op(P.pool, lambda e: e.affine_select(out=k.BR64[:], in_=k.BR64[:], pattern=[[-1, 128]],
                                           compare_op=ALU.is_gt, fill=0.0, base=0, channel_multiplier=1),
         reads=[k.cstb], writes=[k.cstb])
    P.op(P.pool, lambda e: e.memset(k.BR64[64:128, 0:64], 0.0), writes=[k.cstb])
    P.op(P.pool, lambda e: e.memset(k.Lcarry[:], 0.0), writes=k.Lhb)


def rmsnorm_to_hT(P, k, nc, gidx, ch, xn):
    hT, hTb, xnb, ssb_col = k.hT, k.hTb, k.xnb, 0
    for t in range(ch.ntile):
        ss = k.small[:, ssb_col + 0:ssb_col + 1]
        rs = k.small[:, ssb_col + 1:ssb_col + 2]
        P.op(P.act, lambda e: e.activation(out=xn[:], in_=k.x[:, t, :], func=AF.Square, accum_out=ss),
             reads=[k.xb[t]], writes=[xnb, k.ssb])
        P.op(P.dve, lambda e: e.tensor_scalar(out=rs, in0=ss, scalar1=1.0 / D, scalar2=EPS,
                                               op0=ALU.mult, op1=ALU.add),
             reads=[k.ssb], writes=[k.rsb])
        P.op(P.act, lambda e: e.activation(out=rs, in_=rs, func=AF.Sqrt), reads=[k.rsb], writes=[k.rsb])
        P.op(P.dve, lambda e: e.reciprocal(out=rs, in_=rs), reads=[k.rsb], writes=[k.rsb])
        P.op(P.dve, lambda e: e.tensor_scalar(out=xn[:], in0=k.x[:, t, :], scalar1=rs, scalar2=None,
                                               op0=ALU.mult),
             reads=[k.xb[t], k.rsb], writes=[xnb])
        for g4 in range(KC // 4):
            pi = g4 % 2
            ps = k.psum[pi]
            for j in range(4):
                c = g4 * 4 + j
                P.op(P.pe, lambda e: e.transpose(out=ps[:, j * 128:(j + 1) * 128],
                                                 in_=xn[:, c * 128:(c + 1) * 128], identity=k.ident[:]),
                     reads=[xnb, k.identb], writes=[k.psb[pi]], signal=(j == 3))
            for j in range(4):
                c = g4 * 4 + j
                eng = P.dve if j % 2 == 0 else P.act
                if eng is P.dve:
                    P.op(eng, lambda e: e.tensor_scalar(out=hT[:, c, t * 128:(t + 1) * 128],
                                                        in0=ps[:, j * 128:(j + 1) * 128],
                                                        scalar1=k.gam[:, gidx, c:c + 1], scalar2=None,
                                                        op0=ALU.mult),
                         reads=[k.psb[pi], k.gamb], writes=[hTb[t]])
                else:
                    P.op(eng, lambda e: e.activation(out=hT[:, c, t * 128:(t + 1) * 128],
                                                     in_=ps[:, j * 128:(j + 1) * 128],
                                                     func=AF.Copy, scale=k.gam[:, gidx, c:c + 1]),
                         reads=[k.psb[pi], k.gamb], writes=[hTb[t]])


FB = 256
NB = DFF // FB


def alloc_ffn_bufs(P, k, es, nc):
    sb = lambda name, shape, dt: es.enter_context(nc.sbuf_tensor(U(name), shape, dt))
    b = K()
    b.hT, b.hTb, b.xnb = k.hT, k.hTb, k.xnb
    b.xn = sb("xn", [128, D], F32)
    b.wgu = [sb(f"wgu{i}", [128, KC, FB], BF16) for i in range(4)]
    b.wgub = [P.buf(f"wgu{i}") for i in range(4)]
    b.wd = [sb(f"wd{i}", [128, FB // 128, D], BF16) for i in range(2)]
    b.wdb = [P.buf(f"wd{i}") for i in range(2)]
    b.actT = [sb(f"actT{i}", [128, FB // 128, NTMAX], BF16) for i in range(2)]
    b.actTb = [P.buf(f"actT{i}") for i in range(2)]
    b.stmp = [sb(f"stmp{i}", [128, 512], BF16) for i in range(2)]
    b.stmpb = [P.buf(f"stmp{i}") for i in range(2)]
    return b


def ffn_phase(P, k, nc, wg, wu, wd, gidx, ch):
    with ExitStack() as es:
        b = alloc_ffn_bufs(P, k, es, nc)
        sbuf_guard(nc)
        ffn(P, k, nc, wg, wu, wd, gidx, b, ch)
        P.barrier()


def ffn(P, k, nc, wg, wu, wd, gidx, b, ch):
    rmsnorm_to_hT(P, k, nc, gidx, ch, b.xn)
    wg_v = wg.rearrange("(c p) n -> p c n", p=128)
    wu_v = wu.rearrange("(c p) n -> p c n", p=128)
    wd_v = wd.rearrange("(c p) n -> p c n", p=128)

    def load(blk):
        sg = (2 * blk) % 4
        su = (2 * blk + 1) % 4
        sd = blk % 2
        P.dma(P.pool, b.wgu[sg][:], wg_v[:, :, blk * FB:(blk + 1) * FB], b.wgub[sg], writes=[b.wgub[sg]])
        P.dma(P.pool, b.wgu[su][:], wu_v[:, :, blk * FB:(blk + 1) * FB], b.wgub[su], writes=[b.wgub[su]])
        P.dma(P.pool, b.wd[sd][:], wd_v[:, blk * 2:blk * 2 + 2, :], b.wdb[sd], writes=[b.wdb[sd]])

    load(0)
    pcount = 0
    for blk in range(NB):
        if blk + 1 < NB:
            load(blk + 1)
        sg = (2 * blk) % 4
        su = (2 * blk + 1) % 4
        sd = blk % 2
        a = blk % 2
        for m in range(FB // 128):
            for (t0, tn) in ch.groups():
                pg = 0 + (pcount % 2)
                pu = 2 + (pcount % 2)
                st = pcount % 2
                pcount += 1
                tiles = [b.hTb[t] for t in range(t0 // 128, (t0 + tn) // 128)]
                for c in range(KC):
                    P.op(P.pe, lambda e: e.matmul(k.psum[pg][:, 0:tn], lhsT=b.wgu[sg][:, c, m * 128:(m + 1) * 128],
                                                  rhs=b.hT[:, c, t0:t0 + tn], start=(c == 0), stop=(c == KC - 1)),
                         reads=[b.wgub[sg]] + tiles, writes=[k.psb[pg]], signal=(c == KC - 1))
                for c in range(KC):
                    P.op(P.pe, lambda e: e.matmul(k.psum[pu][:, 0:tn], lhsT=b.wgu[su][:, c, m * 128:(m + 1) * 128],
                                                  rhs=b.hT[:, c, t0:t0 + tn], start=(c == 0), stop=(c == KC - 1)),
                         reads=[b.wgub[su]] + tiles, writes=[k.psb[pu]], signal=(c == KC - 1))
                P.op(P.act, lambda e: e.activation(out=b.stmp[st][:, 0:tn], in_=k.psum[pg][:, 0:tn], func=AF.Silu),
                     reads=[k.psb[pg]], writes=[b.stmpb[st]])
                P.op(P.dve, lambda e: e.tensor_tensor(out=b.actT[a][:, m, t0:t0 + tn], in0=k.psum[pu][:, 0:tn],
                                                      in1=b.stmp[st][:, 0:tn], op=ALU.mult),
                     reads=[k.psb[pu], b.stmpb[st]], writes=[b.actTb[a]])
        dcount = 0
        for t in range(ch.ntile):
            for q in range(4):
                pd = 4 + (dcount % 4)
                dcount += 1
                for m in range(FB // 128):
                    P.op(P.pe, lambda e: e.matmul(k.psum[pd][:, :], lhsT=b.actT[a][:, m, t * 128:(t + 1) * 128],
                                                  rhs=b.wd[sd][:, m, q * 512:(q + 1) * 512],
                                                  start=(m == 0), stop=(m == FB // 128 - 1)),
                         reads=[b.actTb[a], b.wdb[sd]], writes=[k.psb[pd]], signal=(m == FB // 128 - 1))
                P.op(P.dve, lambda e: e.scalar_tensor_tensor(out=k.x[:, t, q * 512:(q + 1) * 512], in0=k.psum[pd][:, :],
                                                             scalar=0.5, in1=k.x[:, t, q * 512:(q + 1) * 512],
                                                             op0=ALU.mult, op1=ALU.add),
                     reads=[k.psb[pd], k.xb[t]], writes=[k.xb[t]])


GELU_C = 1.5957691216057308


def gelu_from_psum(P, ps, psb, n, dst, dstb, tmp, tmpb):
    P.op(P.act, lambda e: e.activation(out=tmp[:, :n], in_=ps[:, :n], func=AF.Square), reads=[psb], writes=[tmpb])
    P.op(P.dve, lambda e: e.tensor_scalar(out=tmp[:, :n], in0=tmp[:, :n], scalar1=0.044715, scalar2=1.0,
                                           op0=ALU.mult, op1=ALU.add), reads=[tmpb], writes=[tmpb])
    P.op(P.dve, lambda e: e.tensor_tensor(out=tmp[:, :n], in0=ps[:, :n], in1=tmp[:, :n], op=ALU.mult),
         reads=[psb, tmpb], writes=[tmpb])
    P.op(P.act, lambda e: e.activation(out=tmp[:, :n], in_=tmp[:, :n], func=AF.Sigmoid, scale=GELU_C),
         reads=[tmpb], writes=[tmpb])
    P.op(P.dve, lambda e: e.tensor_tensor(out=dst, in0=ps[:, :n], in1=tmp[:, :n], op=ALU.mult),
         reads=[psb, tmpb], writes=[dstb])


def cmix(P, k, nc, w_in, ln_g, ln_b, w_s, b_s, w_out, cv_out, gidx, ch):
    NTILE = ch.ntile
    xnb = k.xnb
    is_s = lambda t: ch.has_sample and t == 8
    with ExitStack() as es:
        sb = lambda name, shape, dt: es.enter_context(nc.sbuf_tensor(U(name), shape, dt))
        xn = sb("c_xn", [128, D], F32)
        rmsnorm_to_hT(P, k, nc, gidx, ch, xn)
        P.barrier()
        vn = sb("c_vn", [128, MAXT, D], BF16)
        vnb = [P.buf(f"c_vn{t}") for t in range(MAXT)]
        wsT = [sb("c_wsTp", [128, 8, 128], BF16), sb("c_wsTs", [128, 8, 128], BF16)]
        wsTb = P.buf("c_wsT")
        bsb = [sb("c_bsp", [128, 8, 128], F32), sb("c_bss", [128, 8, 128], F32)]
        bsbb = P.buf("c_bs")
        gtmp = [xn[:, 0:512], xn[:, 512:1024]]
        gtmpb = [P.buf(f"c_gtmp{i}") for i in range(2)]
        stat = k.small
        with ExitStack() as es1:
            stg = [es1.enter_context(nc.sbuf_tensor(U(f"c_stg{i}"), [128, 128], F32)) for i in range(2)]
            stgb = [P.buf(f"c_stg{i}") for i in range(2)]
            P.op(P.pool, lambda e: e.memset(stg[1][:], 0.0), writes=[stgb[1]])
            for g in range(8):
                P.dma(P.sp, stg[0][:], w_s[g, :, :], stgb[0], writes=[stgb[0]])
                P.dma(P.sp, stg[1][0:64, 0:64], w_s[g, 0:64, 0:64], stgb[1], writes=[stgb[1]])
                P.dma(P.sp, stg[1][64:128, 64:128], w_s[g, 0:64, 0:64], stgb[1], writes=[stgb[1]])
                for v in range(2):
                    P.op(P.pe, lambda e: e.transpose(out=k.psum[v][:, 0:128], in_=stg[v][:], identity=k.ident[:]),
                         reads=[stgb[v], k.identb], writes=[k.psb[v]])
                    P.op(P.dve, lambda e: e.tensor_tensor(out=wsT[v][:, g, :], in0=k.psum[v][:, 0:128], in1=k.maskU[:],
                                                          op=ALU.mult),
                         reads=[k.psb[v], k.maskUb], writes=[wsTb])
            P.dma(P.sp, bsb[0][:], b_s.partition_broadcast(128), bsbb, writes=[bsbb])
            for hh in range(2):
                P.dma(P.sp, bsb[1][:, :, hh * 64:(hh + 1) * 64], b_s[:, 0:64].partition_broadcast(128), bsbb, writes=[bsbb])
            P.barrier()
        with ExitStack() as es2:
            sb2 = lambda name, shape, dt: es2.enter_context(nc.sbuf_tensor(U(name), shape, dt))
            ring = [sb2(f"c_ring{i}", [128, KC, 256], BF16) for i in range(2)]
            ringb = [P.buf(f"c_ring{i}") for i in range(2)]
            HD2 = D // 4
            lng = sb2("c_lng", [128, HD2], F32)
            lnb = sb2("c_lnb", [128, HD2], F32)
            lnbuf = P.buf("c_ln")
            sbuf_guard(nc)
            w_v = w_in.rearrange("(c p) n -> p c n", p=128)
            NVB = D // 256
            P.dma(P.pool, ring[0][:], w_v[:, :, D:D + 256], ringb[0], writes=[ringb[0]])
            cnt = 0
            for cb in range(NVB):
                if cb + 1 < NVB:
                    s1 = (cb + 1) % 2
                    P.dma(P.pool, ring[s1][:], w_v[:, :, D + (cb + 1) * 256:D + (cb + 2) * 256], ringb[s1], writes=[ringb[s1]])
                s = cb % 2
                for t in range(NTILE):
                    pi = cnt % 4
                    gi = cnt % 2
                    cnt += 1
                    for c in range(KC):
                        P.op(P.pe, lambda e: e.matmul(k.psum[pi][:, 0:256], lhsT=k.hT[:, c, t * 128:(t + 1) * 128],
                                                      rhs=ring[s][:, c, :], start=(c == 0), stop=(c == KC - 1)),
                             reads=[k.hTb[t], ringb[s]], writes=[k.psb[pi]], signal=(c == KC - 1))
                    gelu_from_psum(P, k.psum[pi], k.psb[pi], 256, vn[:, t, cb * 256:(cb + 1) * 256], vnb[t],
                                   gtmp[gi], gtmpb[gi])
            P.barrier()
            stat = k.small
            for t in range(NTILE):
                s1 = stat[:, 8:9]
                s2 = stat[:, 9:10]
                msq = stat[:, 11:12]
                mean = stat[:, 16 + t:17 + t]
                rstd = stat[:, 32 + t:33 + t]
                P.op(P.act, lambda e: e.activation(out=xn[:], in_=vn[:, t, :], func=AF.Copy, accum_out=s1),
                     reads=[vnb[t]], writes=[xnb, k.statb])
                P.op(P.act, lambda e: e.activation(out=xn[:], in_=vn[:, t, :], func=AF.Square, accum_out=s2),
                     reads=[vnb[t]], writes=[xnb, k.statb])
                P.op(P.dve, lambda e: e.tensor_scalar(out=mean, in0=s1, scalar1=1.0 / D, scalar2=None, op0=ALU.mult),
                     reads=[k.statb], writes=[k.statb])
                P.op(P.dve, lambda e: e.tensor_tensor(out=msq, in0=mean, in1=mean, op=ALU.mult),
                     reads=[k.statb], writes=[k.statb])
                P.op(P.dve, lambda e: e.scalar_tensor_tensor(out=rstd, in0=s2, scalar=1.0 / D, in1=msq,
                                                             op0=ALU.mult, op1=ALU.subtract),
                     reads=[k.statb], writes=[k.statb])
                P.op(P.dve, lambda e: e.tensor_scalar(out=rstd, in0=rstd, scalar1=EPS, scalar2=None, op0=ALU.add),
                     reads=[k.statb], writes=[k.statb])
                P.op(P.act, lambda e: e.activation(out=rstd, in_=rstd, func=AF.Sqrt), reads=[k.statb], writes=[k.statb])
                P.op(P.dve, lambda e: e.reciprocal(out=rstd, in_=rstd), reads=[k.statb], writes=[k.statb])
            for half in range(4):
                hs = slice(half * HD2, (half + 1) * HD2)
                P.dma(P.sp, lng[:], ln_g[hs].partition_broadcast(128), lnbuf, writes=[lnbuf])
                P.dma(P.sp, lnb[:], ln_b[hs].partition_broadcast(128), lnbuf, writes=[lnbuf])
                for t in range(NTILE):
                    mean = stat[:, 16 + t:17 + t]
                    rstd = stat[:, 32 + t:33 + t]
                    xh = xn[:, 0:HD2]
                    P.op(P.dve, lambda e: e.tensor_scalar(out=xh, in0=vn[:, t, hs], scalar1=mean, scalar2=rstd,
                                                          op0=ALU.subtract, op1=ALU.mult),
                         reads=[vnb[t], k.statb], writes=[xnb])
                    P.op(P.dve, lambda e: e.tensor_tensor(out=xh, in0=xh, in1=lng[:], op=ALU.mult),
                         reads=[xnb, lnbuf], writes=[xnb])
                    if not is_s(t):
                        P.op(P.dve, lambda e: e.tensor_tensor(out=vn[:, t, hs], in0=xh, in1=lnb[:], op=ALU.add),
                             reads=[xnb, lnbuf], writes=[vnb[t]])
                    else:
                        P.op(P.dve, lambda e: e.tensor_tensor(out=xh, in0=xh, in1=lnb[:], op=ALU.add),
                             reads=[xnb, lnbuf], writes=[xnb])
                        ts_ = ch.c
                        P.dma(P.sp, cv_out[ts_ * 128:(ts_ + 1) * 128, hs], xh, xnb, reads=[xnb])
                        P.op(P.act, lambda e: e.activation(out=vn[:, t, hs], in_=xh, func=AF.Copy),
                             reads=[xnb], writes=[vnb[t]])
            P.barrier()
        with ExitStack() as es3:
            sb3 = lambda name, shape, dt: es3.enter_context(nc.sbuf_tensor(U(name), shape, dt))
            ring = [sb3(f"c_uring{i}", [128, KC, 128], BF16) for i in range(2)]
            ringb = [P.buf(f"c_uring{i}") for i in range(2)]
            wo = [sb3(f"c_wo{i}", [128, D], BF16) for i in range(2)]
            wob = [P.buf(f"c_wo{i}") for i in range(2)]
            uT = [sb3(f"c_uT{i}", [128, NTMAX], BF16) for i in range(2)]
            uTb = [P.buf(f"c_uT{i}") for i in range(2)]
            pT, pTb = uT, uTb
            mt = [sb3(f"c_mt{i}", [128, 128], F32) for i in range(2)]
            mtb = [P.buf(f"c_mt{i}") for i in range(2)]
            sbuf_guard(nc)
            w_v = w_in.rearrange("(c p) n -> p c n", p=128)

            def load(cb):
                s = cb % 2
                P.dma(P.pool, ring[s][:], w_v[:, :, cb * 128:(cb + 1) * 128], ringb[s], writes=[ringb[s]])
                P.dma(P.pool, wo[s][:], w_out[cb * 128:(cb + 1) * 128, :], wob[s], writes=[wob[s]])
            load(0)
            cnt = 0
            mcnt = 0
            dcnt = 0
            for cb in range(KC):
                if cb + 1 < KC:
                    load(cb + 1)
                s = cb % 2
                g = cb // 2
                for (t0, tn) in ch.groups():
                    pi = cnt % 2
                    cnt += 1
                    tiles = [k.hTb[t] for t in range(t0 // 128, (t0 + tn) // 128)]
                    for c in range(KC):
                        P.op(P.pe, lambda e: e.matmul(k.psum[pi][:, 0:tn], lhsT=ring[s][:, c, :], rhs=k.hT[:, c, t0:t0 + tn],
                                                      start=(c == 0), stop=(c == KC - 1)),
                             reads=[ringb[s]] + tiles, writes=[k.psb[pi]], signal=(c == KC - 1))
                    gelu_from_psum(P, k.psum[pi], k.psb[pi], tn, uT[s][:, t0:t0 + tn], uTb[s], gtmp[pi], gtmpb[pi])
                for t in range(NTILE):
                    v = 1 if is_s(t) else 0
                    pm = 2 + (mcnt % 2)
                    mi = mcnt % 2
                    mcnt += 1
                    P.op(P.pe, lambda e: e.matmul(k.psum[pm][:, 0:128], lhsT=vn[:, t, cb * 128:(cb + 1) * 128],
                                                  rhs=wsT[v][:, g, :], start=True, stop=True),
                         reads=[vnb[t], wsTb], writes=[k.psb[pm]])
                    P.op(P.dve, lambda e: e.tensor_tensor(out=mt[mi][:], in0=k.psum[pm][:, 0:128], in1=bsb[v][:, g, :],
                                                          op=ALU.add),
                         reads=[k.psb[pm], bsbb], writes=[mtb[mi]])
                    P.op(P.dve, lambda e: e.tensor_tensor(out=pT[s][:, t * 128:(t + 1) * 128], in0=mt[mi][:],
                                                          in1=uT[s][:, t * 128:(t + 1) * 128], op=ALU.mult),
                         reads=[mtb[mi], uTb[s]], writes=[pTb[s]])
                for t in range(NTILE):
                    for q in range(4):
                        pd = 4 + (dcnt % 4)
                        dcnt += 1
                        P.op(P.pe, lambda e: e.matmul(k.psum[pd][:, :], lhsT=pT[s][:, t * 128:(t + 1) * 128],
                                                      rhs=wo[s][:, q * 512:(q + 1) * 512], start=True, stop=True),
                             reads=[pTb[s], wob[s]], writes=[k.psb[pd]])
                        P.op(P.dve, lambda e: e.tensor_tensor(out=k.x[:, t, q * 512:(q + 1) * 512], in0=k.psum[pd][:, :],
                                                              in1=k.x[:, t, q * 512:(q + 1) * 512], op=ALU.add),
                             reads=[k.psb[pd], k.xb[t]], writes=[k.xb[t]])
            P.barrier()


DBG = dict(fox=True, samp=True, hgrn=True, outp=True, heads=8, nproj=7, store=True)


def abmix(P, k, nc, j, ch, io, gidx):
    c, p0, gt0, NTL = ch.c, ch.p0, ch.gt0, ch.ntile
    HS = ch.has_sample
    w_in = io.ab_w_in[j].rearrange("(c p) n -> p c n", p=128)
    w_out = io.ab_w_out[j]
    with ExitStack() as es0:
        xn = es0.enter_context(nc.sbuf_tensor(U("a_xn"), [128, D], F32))
        rmsnorm_to_hT(P, k, nc, gidx, ch, xn)
        P.barrier()
    with ExitStack() as es:
        def sbt(name, shape, dt):
            return es.enter_context(nc.sbuf_tensor(U(name), shape, dt)), P.buf(name)
        NR = 3
        kTall, kTallb = sbt("a_kTall", [128, SEQ], BF16)
        wr = [sbt(f"a_wr{i}", [128, KC, 128], BF16) for i in range(NR)]
        wfa, wfab = sbt("a_wfa", [128, KC, H], BF16)
        wo, wob = sbt("a_wo", [128, 2, D // 2], BF16)
        qT, qTb = sbt("a_qT", [128, NTMAX], BF16)
        kT32, kT32b = sbt("a_kT32", [128, 512], F32)
        vall, vallb = sbt("a_vall", [128, 32, 128], BF16)
        kTs, kTsb = sbt("a_kTs", [128, 128], BF16)
        vs, vsb = sbt("a_vs", [128, 128], BF16)
        kc32 = [sbt(f"a_kc32_{i}", [128, 4, 128], F32) for i in range(2)]
        lf, lfb = sbt("a_lf", [128, MAXT, H], F32)
        lfT = [sbt(f"a_lfT{i}", [H, 128], F32) for i in range(2)]
        bfb, bfbb = sbt("a_bfb", [128, H], F32)
        Lref, Lrefb = sbt("a_Lref", [128, 8, H], F32)
        Bn, Bnb = sbt("a_Bn", [128, H], F32)
        Bq, Bqb = sbt("a_Bq", [128, 8, 32], F32)
        pT = [sbt(f"a_pT{i}", [128, 128], BF16) for i in range(4)]
        recs = [sbt(f"a_rec{i}", [128, 128], F32) for i in range(2)]
        rec, recb = recs[0]
        acc6 = [P.buf("acc6_0"), P.buf("acc6_1")]
        acc7 = [P.buf("acc7_0"), P.buf("acc7_1")]
        oaT, oaTb = sbt("a_oaT", [128, NTMAX], BF16)
        obT, obTb = sbt("a_obT", [128, NTMAX], BF16)
        st32 = [sbt(f"a_st{i}", [128, 128], F32) for i in range(2)]
        lfc_r, lfc_rb = sbt("a_lfcr", [32, 128], F32)
        lfc, lfcb = sbt("a_lfc", [128, 4, 32], F32)
        qbs, qbsb = sbt("a_qbs", [128, NTMAX], BF16)
        g_tm, g_tmb = sbt("a_gtm", [128, MAXT, 128], F32)
        k_tm, k_tmb = sbt("a_ktm", [128, MAXT, 128], BF16)
        i_bf, i_bfb = sbt("a_ibf", [128, MAXT, 128], BF16)
        gate, gateb = sbt("a_gate", [128, MAXT, 128], BF16)
        lbt, lbtb = sbt("a_lbt", [128, 2, 128], F32)
        lbo, lbob = sbt("a_lbo", [128, 2, 128], F32)
        gnb, gnbb = sbt("a_gnb", [128, 128], F32)
        tE, tEb = sbt("a_tE", [128, 128], F32)
        tK, tKb = sbt("a_tK", [128, 128], F32)
        tG, tGb = sbt("a_tG", [128, 128], F32)
        tR, tRb = sbt("a_tR", [128, 128], F32)
        rcol, rcolb = sbt("a_rcol", [128, 4], F32)
        qp, qpb = sbt("a_qp", [128, 128], BF16)
        kp, kpb = sbt("a_kp", [128, 128], BF16)
        qpp, qppb = sbt("a_qpp", [128, 128], BF16)
        kpp, kppb = sbt("a_kpp", [128, 128], BF16)
        ATm, ATmb = sbt("a_ATm", [128, 128], BF16)
        S32 = [sbt(f"a_S32_{i}", [128, 128], F32) for i in range(2)]
        Sbf = [sbt(f"a_Sbf_{i}", [128, 128], BF16) for i in range(2)]
        on, onb = sbt("a_on", [128, 128], F32)
        sbuf_guard(nc)
        PS, PB = k.psum, k.psb
        kthb, vthb, sthb = P.buf(f"kth{j}"), P.buf(f"vth{j}"), P.buf(f"sth{j}")
        stat = k.small

        P.dma(P.pool, wfa[:], w_in[:, :, 3072:3080], wfab, writes=[wfab])
        P.dma(P.sp, bfb[:], io.ab_b_f[j].partition_broadcast(128), bfbb, writes=[bfbb])
        P.dma(P.sp, gnb[:], io.hgrn_gnorm[j].partition_broadcast(128), gnbb, writes=[gnbb])
        for t in range(NTL):
            for kc in range(KC):
                P.op(P.pe, lambda e: e.matmul(PS[0][:, t * H:(t + 1) * H], lhsT=k.hT[:, kc, t * 128:(t + 1) * 128],
                                              rhs=wfa[:, kc, :], start=(kc == 0), stop=(kc == KC - 1)),
                     reads=[k.hTb[t], wfab], writes=[PB[0]], signal=(kc == KC - 1))
            P.op(P.dve, lambda e: e.tensor_tensor(out=lf[:, t, :], in0=PS[0][:, t * H:(t + 1) * H], in1=bfb[:], op=ALU.add),
                 reads=[PB[0], bfbb], writes=[lfb])
        lfl = lf[:, 0:NTL, :]
        P.op(P.act, lambda e: e.activation(out=lfl, in_=lfl, func=AF.Exp, scale=-1.0), reads=[lfb], writes=[lfb])
        P.op(P.dve, lambda e: e.tensor_scalar(out=lfl, in0=lfl, scalar1=1.0, scalar2=None, op0=ALU.add), reads=[lfb], writes=[lfb])
        P.op(P.act, lambda e: e.activation(out=lfl, in_=lfl, func=AF.Ln), reads=[lfb], writes=[lfb])
        P.op(P.dve, lambda e: e.tensor_scalar(out=lfl, in0=lfl, scalar1=-1.0, scalar2=None, op0=ALU.mult), reads=[lfb], writes=[lfb])
        for t in range(NTL):
            lt, ltb = lfT[t % 2]
            P.op(P.pe, lambda e: e.transpose(out=PS[1][0:H, 0:128], in_=lf[:, t, :], identity=k.ident[:]),
                 reads=[lfb, k.identb], writes=[PB[1]])
            P.op(P.act, lambda e: e.activation(out=lt[:], in_=PS[1][0:H, 0:128], func=AF.Copy),
                 reads=[PB[1]], writes=[ltb])
            if HS and t == 8:
                for s_ in range(2):
                    P.dma(P.sp, io.logf_s[j, 2 * c + s_, :, :], lt[:, 64 * s_:64 * s_ + 64], ltb, reads=[ltb])
            else:
                P.dma(P.sp, io.logf_p[j, :, p0 + t * 128:p0 + (t + 1) * 128], lt[:], ltb, reads=[ltb])
        lfp = lf[:, 0:8, :]
        P.op(P.pe, lambda e: e.matmul(PS[2][:, 0:64], lhsT=k.maskU[:], rhs=lfp, start=True, stop=True),
             reads=[lfb, k.maskUb], writes=[PB[2]])
        P.op(P.pe, lambda e: e.matmul(PS[3][:, 0:64], lhsT=k.ones_f[:], rhs=lfp, start=True, stop=True),
             reads=[lfb, k.cstb], writes=[PB[3]])
        for t in range(8):
            prev = k.Lcarry[:, j, :] if t == 0 else Lref[:, t - 1, :]
            P.op(P.dve, lambda e: e.tensor_tensor(out=k.Lhist[:, j, gt0 + t, :], in0=PS[2][:, t * H:(t + 1) * H], in1=prev, op=ALU.add),
                 reads=[PB[2], k.Lhb[j], Lrefb], writes=[k.Lhb[j]])
            P.op(P.dve, lambda e: e.tensor_tensor(out=Lref[:, t, :], in0=PS[3][:, t * H:(t + 1) * H], in1=prev, op=ALU.add),
                 reads=[PB[3], k.Lhb[j], Lrefb], writes=[Lrefb])
        P.op(P.dve, lambda e: e.tensor_copy(out=k.Lcarry[:, j, :], in_=Lref[:, 7, :]), reads=[Lrefb], writes=[k.Lhb[j]])
        if HS:
            P.op(P.pe, lambda e: e.matmul(PS[2][:, 64:64 + H], lhsT=k.BU64[:], rhs=lf[:, 8, :], start=True, stop=True),
                 reads=[lfb, k.cstb], writes=[PB[2]])
            P.op(P.dve, lambda e: e.tensor_scalar(out=Bn[:], in0=PS[2][:, 64:64 + H], scalar1=-1.0, scalar2=None, op0=ALU.mult),
                 reads=[PB[2]], writes=[Bnb])

        COLS = [0, 1024, 2048, 3080, 4104, 5128, 6152]
        nload = [0]

        def load_next():
            n = nload[0]
            if n >= 7 * H:
                return
            nload[0] += 1
            hh, bi = divmod(n, 7)
            c0 = COLS[bi] + hh * 128
            if DBG.get('dbgcol') and bi == 6:
                c0 = COLS[5]
            if True:
                for hf in range(2):
                    P.dma(P.pool, wr[n % NR][0][:, hf * 8:(hf + 1) * 8, :], w_in[:, hf * 8:(hf + 1) * 8, c0:c0 + 128], wr[n % NR][1], writes=[wr[n % NR][1]])
                return
            P.dma(P.pool, wr[n % NR][0][:], w_in[:, :, c0:c0 + 128], wr[n % NR][1], writes=[wr[n % NR][1]])

        def slot_of(h, bi):
            return wr[(h * 7 + bi) % NR]
        cntB = [0]
        cntA = [0]

        def projT(w, wb, evac):
            for (t0, tn) in ch.groups():
                pb = cntB[0] % 2
                cntB[0] += 1
                tiles = [k.hTb[t] for t in range(t0 // 128, (t0 + tn) // 128)]
                for kc in range(KC):
                    P.op(P.pe, lambda e: e.matmul(PS[pb][:, 0:tn], lhsT=w[:, kc, :], rhs=k.hT[:, kc, t0:t0 + tn],
                                                  start=(kc == 0), stop=(kc == KC - 1)),
                         reads=[wb] + tiles, writes=[PB[pb]], signal=(kc == KC - 1))
                evac(PS[pb], PB[pb], t0, tn)
            if not DBG.get('noload') and nload[0] < DBG.get('maxload', 999):
                load_next()

        def projA(w, wb, evac):
            for t in range(NTL):
                pb = 2 + cntA[0] % 2
                cntA[0] += 1
                for kc in range(KC):
                    P.op(P.pe, lambda e: e.matmul(PS[pb][:, 0:128], lhsT=k.hT[:, kc, t * 128:(t + 1) * 128], rhs=w[:, kc, :],
                                                  start=(kc == 0), stop=(kc == KC - 1)),
                         reads=[wb, k.hTb[t]], writes=[PB[pb]], signal=(kc == KC - 1))
                evac(PS[pb], PB[pb], t)
            load_next()

        for _ in range(NR):
            load_next()
        for h in range(DBG['heads']):
            P.dma(P.sp, lbt[:, 0, :], io.hgrn_lb[0, h * 128:(h + 1) * 128].partition_broadcast(128), lbtb, writes=[lbtb])
            P.dma(P.sp, lbt[:, 1, :], io.hgrn_lb[1, h * 128:(h + 1) * 128].partition_broadcast(128), lbtb, writes=[lbtb])
            P.op(P.dve, lambda e: e.tensor_tensor(out=lbo[:, 1, :], in0=lbt[:, 1, :], in1=lbt[:, 0, :], op=ALU.subtract),
                 reads=[lbtb], writes=[lbob])
            P.op(P.act, lambda e: e.activation(out=lbo[:, 1, :], in_=lbo[:, 1, :], func=AF.Sigmoid), reads=[lbob], writes=[lbob])
            if j == 0:
                P.op(P.dve, lambda e: e.tensor_scalar(out=lbo[:, 0, :], in0=lbo[:, 1, :], scalar1=0.0, scalar2=None, op0=ALU.mult),
                     reads=[lbob], writes=[lbob])
            else:
                P.op(P.dve, lambda e: e.tensor_copy(out=lbo[:, 0, :], in_=lbo[:, 1, :]), reads=[lbob], writes=[lbob])
            P.op(P.dve, lambda e: e.tensor_scalar(out=lbo[:, 1, :], in0=lbo[:, 0, :], scalar1=-1.0, scalar2=1.0,
                                                  op0=ALU.mult, op1=ALU.add), reads=[lbob], writes=[lbob])
            if c > 0:
                P.dma(P.sp, kTall[:, 0:p0], io.kth[j, h, :, 0:p0], kTallb, reads=[kthb], writes=[kTallb])
                P.dma(P.sp, vall[:, 0:gt0, :], io.vth[j, h, 0:p0, :].rearrange("(t p) d -> p t d", p=128), vallb,
                      reads=[vthb], writes=[vallb])
            def out_rows(dst_p, dst_s, src, srcb, t):
                if HS and t == 8:
                    for s_ in range(2):
                        P.dma(P.sp, dst_s[j, 2 * c + s_, h, :, :], src[64 * s_:64 * s_ + 64, :], srcb, reads=[srcb])
                else:
                    P.dma(P.sp, dst_p[j, h, p0 + t * 128:p0 + (t + 1) * 128, :], src[:], srcb, reads=[srcb])

            wq, wqb = slot_of(h, 0)

            def ev_q(ps, psb, t0, tn):
                P.op(P.act, lambda e: e.activation(out=qT[:, t0:t0 + tn], in_=ps[:, 0:tn], func=AF.Copy, scale=SCALE),
                     reads=[psb], writes=[qTb])
            if DBG['nproj'] > 0 and not DBG.get('skipq'):
                projT(wq, wqb, ev_q)
            wk, wkb = slot_of(h, 1)

            def ev_k(ps, psb, t0, tn):
                P.op(P.act, lambda e: e.activation(out=kT32[:, 0:tn], in_=ps[:, 0:tn], func=AF.Copy),
                     reads=[psb], writes=[kT32b])
                for tt in range(tn // 128 if DBG.get('kout', True) else 0):
                    t = t0 // 128 + tt
                    pb2 = 2 + cntA[0] % 2
                    cntA[0] += 1
                    sk, skb = st32[1]
                    P.op(P.pe, lambda e: e.transpose(out=PS[pb2][:, 0:128], in_=kT32[:, tt * 128:(tt + 1) * 128], identity=k.ident[:]),
                         reads=[kT32b, k.identb], writes=[PB[pb2]])
                    P.op(P.act, lambda e: e.activation(out=sk[:], in_=PS[pb2][:, 0:128], func=AF.Copy), reads=[PB[pb2]], writes=[skb])
                    out_rows(io.k_p, io.k_s, sk, skb, t)
                if not DBG.get('kdve', True):
                    pass
                elif t0 < 1024:
                    P.op(P.act, lambda e: e.activation(out=(oaT[:, t0:t0 + tn] if DBG.get('dst') else kTall[:, p0 + t0:p0 + t0 + tn]), in_=ps[:, 0:tn], func=AF.Copy),
                         reads=[psb], writes=[kTallb])
                else:
                    P.op(P.act, lambda e: e.activation(out=kTs[:], in_=ps[:, 0:tn], func=AF.Copy), reads=[psb], writes=[kTsb])
            if DBG['nproj'] > 1:
                projT(wk, wkb, ev_k)
            wv, wvb = slot_of(h, 2)

            def ev_v(ps, psb, t):
                sv, svb = st32[0]
                P.op(P.act, lambda e: e.activation(out=sv[:], in_=ps[:, 0:128], func=AF.Copy), reads=[psb], writes=[svb])
                if HS and t == 8:
                    P.op(P.act, lambda e: e.activation(out=vs[:], in_=ps[:, 0:128], func=AF.Copy), reads=[psb], writes=[vsb])
                else:
                    P.op(P.act, lambda e: e.activation(out=vall[:, gt0 + t, :], in_=ps[:, 0:128], func=AF.Copy), reads=[psb], writes=[vallb])
                out_rows(io.v_p, io.v_s, sv, svb, t)
            if DBG['nproj'] > 2:
                projA(wv, wvb, ev_v)
            wqb_, wqbb = slot_of(h, 3)

            def ev_qb(ps, psb, t0, tn):
                P.op(P.act, lambda e: e.activation(out=qbs[:, t0:t0 + tn], in_=ps[:, 0:tn], func=AF.Silu), reads=[psb], writes=[qbsb])
            if DBG['nproj'] > 3:
                projT(wqb_, wqbb, ev_qb)
            wf, wfb_ = slot_of(h, 4)

            def ev_fb(ps, psb, t):
                P.op(P.act, lambda e: e.activation(out=tE[:], in_=ps[:, 0:128], func=AF.Sigmoid), reads=[psb], writes=[tEb])
                P.op(P.dve, lambda e: e.tensor_tensor(out=tK[:], in0=tE[:], in1=lbo[:, 1, :], op=ALU.mult),
                     reads=[tEb, lbob], writes=[tKb])
                P.op(P.dve, lambda e: e.tensor_tensor(out=k_tm[:, t, :], in0=lbo[:, 1, :], in1=tK[:], op=ALU.subtract),
                     reads=[tKb, lbob], writes=[k_tmb])
                P.op(P.dve, lambda e: e.tensor_tensor(out=tE[:], in0=tK[:], in1=lbo[:, 0, :], op=ALU.add),
                     reads=[tKb, lbob], writes=[tEb])
                P.op(P.dve, lambda e: e.tensor_scalar(out=tE[:], in0=tE[:], scalar1=TINY, scalar2=None, op0=ALU.max),
                     reads=[tEb], writes=[tEb])
                P.op(P.act, lambda e: e.activation(out=g_tm[:, t, :], in_=tE[:], func=AF.Ln), reads=[tEb], writes=[g_tmb])
            if DBG['nproj'] > 4:
                projA(wf, wfb_, ev_fb)
            wi, wib = slot_of(h, 5)

            def ev_ib(ps, psb, t):
                P.op(P.act, lambda e: e.activation(out=i_bf[:, t, :], in_=ps[:, 0:128], func=AF.Copy), reads=[psb], writes=[i_bfb])
            if DBG['nproj'] > 5:
                projA(wi, wib, ev_ib)
            wg_, wgb_ = slot_of(h, 6)

            def ev_gb(ps, psb, t):
                P.op(P.act, lambda e: e.activation(out=gate[:, t, :], in_=ps[:, 0:128], func=AF.Silu), reads=[psb], writes=[gateb])
            if DBG['nproj'] > 6:
                projA(wg_, wgb_, ev_gb)
            if c < NCHUNK - 1 and DBG['store']:
                P.dma(P.sp, io.kth[j, h, :, p0:p0 + 1024], kTall[:, p0:p0 + 1024], kthb, reads=[kTallb], writes=[kthb])
                P.dma(P.sp, io.vth[j, h, p0:p0 + 1024, :].rearrange("(t p) d -> p t d", p=128), vall[:, gt0:gt0 + 8, :], vthb,
                      reads=[vallb], writes=[vthb])

            SBK = [0, 1, 4, 5] if DBG.get('sbk4', True) else [4, 5, 4, 5]
            DEPTH = DBG.get('depth', 3)
            nq = 8 if DBG['fox'] else 0
            for iq in range(nq):
                gi = gt0 + iq
                P.op(P.dve, lambda e: e.tensor_scalar(out=Bq[:, iq, 0:gi + 1], in0=k.Lhist[:, j, 0:gi + 1, h], scalar1=Lref[:, iq, h:h + 1],
                                                      scalar2=-1.0, op0=ALU.subtract, op1=ALU.mult),
                     reads=[k.Lhb[j], Lrefb], writes=[Bqb])
            pairs = [(iq, jj) for iq in range(nq) for jj in range(gt0 + iq + 1)]

            def st1(n):
                iq, jj = pairs[n]
                pb = SBK[n % 4]
                pt, ptb = pT[n % 4]
                P.op(P.pe, lambda e: e.matmul(PS[pb][:, 0:128], lhsT=kTall[:, jj * 128:(jj + 1) * 128],
                                              rhs=qT[:, iq * 128:(iq + 1) * 128], start=True, stop=True),
                     reads=[kTallb, qTb], writes=[PB[pb]])
                P.op(P.act, lambda e: e.activation(out=pt[:], in_=PS[pb][:, 0:128], func=AF.Exp, bias=Bq[:, iq, jj:jj + 1]),
                     reads=[PB[pb], Bqb], writes=[ptb])
                if jj == gt0 + iq:
                    P.op(P.dve, lambda e: e.tensor_tensor(out=pt[:], in0=pt[:], in1=k.maskU[:], op=ALU.mult),
                         reads=[ptb, k.maskUb], writes=[ptb])

            def st2(n):
                iq, jj = pairs[n]
                gi = gt0 + iq
                a_ = (iq % 2) if DBG.get('halves', False) else 0
                ca = slice(a_ * 128, (a_ + 1) * 128)
                pt, ptb = pT[n % 4]
                P.op(P.pe, lambda e: e.matmul(PS[6][:, ca], lhsT=vall[:, jj, :], rhs=pt[:], start=(jj == 0), stop=(jj == gi)),
                     reads=[vallb, ptb], writes=[acc6[a_], PB[6]], signal=False)
                P.op(P.pe, lambda e: e.matmul(PS[7][:, ca], lhsT=k.ones_b[:], rhs=pt[:], start=(jj == 0), stop=(jj == gi)),
                     reads=[k.cstb, ptb], writes=[acc7[a_], PB[7]])
                if jj == gi:
                    rc, rcb = recs[a_]
                    last = [PB[6], PB[7]] if iq == nq - 1 else []
                    P.op(P.dve, lambda e: e.reciprocal(out=rc[:], in_=PS[7][:, ca]), reads=[acc7[a_]] + last, writes=[rcb])
                    P.op(P.dve, lambda e: e.tensor_tensor(out=oaT[:, iq * 128:(iq + 1) * 128], in0=PS[6][:, ca], in1=rc[:], op=ALU.mult),
                         reads=[acc6[a_], rcb] + last, writes=[oaTb])
            for n in range(len(pairs) + DEPTH):
                if n < len(pairs):
                    st1(n)
                if n >= DEPTH:
                    st2(n - DEPTH)

            if HS and DBG['samp']:
                for s_ in range(2):
                    b = 2 * c + s_
                    qs = qT[:, 1024 + 64 * s_:1024 + 64 * s_ + 64]
                    P.dma(P.pool, vall[:], io.cache_v[j, b, h, :, :].rearrange("(t p) d -> p t d", p=128), vallb, writes=[vallb])
                    for r in range(8):
                        kc_, kcb = kc32[r % 2]
                        P.dma(P.sp, kc_[:], io.cache_k[j, b, h, r * 512:(r + 1) * 512, :].rearrange("(t p) d -> p t d", p=128),
                              kcb, writes=[kcb])
                        pb = r % 2
                        for q4 in range(4):
                            P.op(P.pe, lambda e: e.transpose(out=PS[pb][:, q4 * 128:(q4 + 1) * 128], in_=kc_[:, q4, :], identity=k.ident[:]),
                                 reads=[kcb, k.identb], writes=[PB[pb]], signal=(q4 == 3))
                        P.op(P.act, lambda e: e.activation(out=kTall[:, r * 512:(r + 1) * 512], in_=PS[pb][:, :], func=AF.Copy),
                             reads=[PB[pb]], writes=[kTallb])
                    P.dma(P.sp, lfc_r[:], io.cache_logf[j, b, h, :].rearrange("(t p) -> t p", p=128), lfc_rb, writes=[lfc_rb])
                    P.op(P.pe, lambda e: e.transpose(out=PS[2][:, 0:32], in_=lfc_r[:], identity=k.ident[0:32, 0:32]),
                         reads=[lfc_rb, k.identb], writes=[PB[2]])
                    P.op(P.act, lambda e: e.activation(out=lfc[:, 3, :], in_=PS[2][:, 0:32], func=AF.Copy), reads=[PB[2]], writes=[lfcb])
                    P.op(P.pe, lambda e: e.matmul(PS[2][:, 0:32], lhsT=k.maskU[:], rhs=lfc[:, 3, :], start=True, stop=True),
                         reads=[lfcb, k.maskUb], writes=[PB[2]])
                    P.op(P.pe, lambda e: e.matmul(PS[3][:, 0:32], lhsT=k.ones_f[:], rhs=lfc[:, 3, :], start=True, stop=True),
                         reads=[lfcb, k.cstb], writes=[PB[3]])
                    P.op(P.act, lambda e: e.activation(out=lfc[:, 1, :], in_=PS[3][:, 0:32], func=AF.Copy), reads=[PB[3]], writes=[lfcb])
                    P.op(P.dve, lambda e: e.tensor_tensor_scan(out=lfc[:, 2, :], data0=k.ones_f[:, 0:32], data1=lfc[:, 1, :], initial=0.0,
                                                               op0=ALU.mult, op1=ALU.add),
                         reads=[lfcb, k.cstb], writes=[lfcb])
                    P.op(P.dve, lambda e: e.tensor_tensor(out=lfc[:, 0, :], in0=lfc[:, 2, :], in1=lfc[:, 1, :], op=ALU.subtract),
                         reads=[lfcb], writes=[lfcb])
                    P.op(P.dve, lambda e: e.tensor_tensor(out=lfc[:, 0, :], in0=PS[2][:, 0:32], in1=lfc[:, 0, :], op=ALU.add),
                         reads=[lfcb, PB[2]], writes=[lfcb])
                    P.op(P.dve, lambda e: e.tensor_scalar(out=lfc[:, 3, :], in0=lfc[:, 0, :], scalar1=lfc[:, 2, 31:32], scalar2=-1.0,
                                                          op0=ALU.subtract, op1=ALU.mult),
                         reads=[lfcb], writes=[lfcb])
                    r0, r1 = 64 * s_, 64 * s_ + 64

                    def sm1(jj):
                        pb = SBK[jj % 4]
                        pt, ptb = pT[jj % 4]
                        if jj < 32:
                            P.op(P.pe, lambda e: e.matmul(PS[pb][:, 0:64], lhsT=kTall[:, jj * 128:(jj + 1) * 128], rhs=qs, start=True, stop=True),
                                 reads=[kTallb, qTb], writes=[PB[pb]])
                            P.op(P.act, lambda e: e.activation(out=pt[:, 0:64], in_=PS[pb][:, 0:64], func=AF.Exp, bias=lfc[:, 3, jj:jj + 1]),
                                 reads=[PB[pb], lfcb], writes=[ptb])
                        else:
                            P.op(P.pe, lambda e: e.matmul(PS[pb][r0:r1, 0:64], lhsT=kTs[:, r0:r1], rhs=qs, start=True, stop=True),
                                 reads=[kTsb, qTb], writes=[PB[pb]])
                            P.op(P.act, lambda e: e.activation(out=pt[r0:r1, 0:64], in_=PS[pb][r0:r1, 0:64], func=AF.Exp, bias=Bn[r0:r1, h:h + 1]),
                                 reads=[PB[pb], Bnb], writes=[ptb])
                            P.op(P.dve, lambda e: e.tensor_tensor(out=pt[r0:r1, 0:64], in0=pt[r0:r1, 0:64], in1=k.maskU[r0:r1, r0:r1], op=ALU.mult),
                                 reads=[ptb, k.maskUb], writes=[ptb])

                    def sm2(jj):
                        pt, ptb = pT[jj % 4]
                        if jj < 32:
                            P.op(P.pe, lambda e: e.matmul(PS[6][:, 0:64], lhsT=vall[:, jj, :], rhs=pt[:, 0:64], start=(jj == 0), stop=False),
                                 reads=[vallb, ptb], writes=[PB[6]], signal=False)
                            P.op(P.pe, lambda e: e.matmul(PS[7][:, 0:64], lhsT=k.ones_b[:], rhs=pt[:, 0:64], start=(jj == 0), stop=False),
                                 reads=[k.cstb, ptb], writes=[PB[7]])
                        else:
                            P.op(P.pe, lambda e: e.matmul(PS[6][:, 0:64], lhsT=vs[r0:r1, :], rhs=pt[r0:r1, 0:64], start=False, stop=True),
                                 reads=[vsb, ptb], writes=[PB[6]], signal=False)
                            P.op(P.pe, lambda e: e.matmul(PS[7][:, 0:64], lhsT=k.ones_b[r0:r1, :], rhs=pt[r0:r1, 0:64], start=False, stop=True),
                                 reads=[k.cstb, ptb], writes=[PB[7]])
                    for n in range(33 + DEPTH):
                        if n < 33:
                            sm1(n)
                        if n >= DEPTH:
                            sm2(n - DEPTH)
                    P.op(P.dve, lambda e: e.reciprocal(out=rec[:, 0:64], in_=PS[7][:, 0:64]), reads=[PB[7]], writes=[recb])
                    P.op(P.dve, lambda e: e.tensor_tensor(out=oaT[:, 1024 + 64 * s_:1024 + 64 * s_ + 64], in0=PS[6][:, 0:64], in1=rec[:, 0:64], op=ALU.mult),
                         reads=[PB[6], recb], writes=[oaTb])

            (S0, S0b), (S1, S1b) = S32
            (Sb0, Sb0b), (Sb1, Sb1b) = Sbf
            if c == 0:
                P.op(P.pool, lambda e: e.memset(S0[:], 0.0), writes=[S0b])
            else:
                P.dma(P.sp, S0[:], io.sth[j, h, :, :], S0b, reads=[sthb], writes=[S0b])
            P.op(P.act, lambda e: e.activation(out=Sb0[:], in_=S0[:], func=AF.Copy), reads=[S0b], writes=[Sb0b])
            for t in range(NTL if DBG['hgrn'] else 0):
                smp = HS and t == 8
                tc_ = slice(t * 128, (t + 1) * 128)
                if smp:
                    for u, (Su, Sub, Sbu, Sbub) in enumerate(((S0, S0b, Sb0, Sb0b), (S1, S1b, Sb1, Sb1b))):
                        P.dma(P.sp, Su[:], io.state_hgrn[j, 2 * c + u, h, :, :], Sub, writes=[Sub])
                        P.op(P.act, lambda e: e.activation(out=Sbu[:], in_=Su[:], func=AF.Copy), reads=[Sub], writes=[Sbub])
                P.op(P.pe, lambda e: e.matmul(PS[0][:, 0:128], lhsT=g_tm[:, t, :], rhs=k.BU64[:], start=True, stop=True),
                     reads=[g_tmb, k.cstb], writes=[PB[0]])
                P.op(P.pe, lambda e: e.matmul(PS[1][:, 0:128], lhsT=k.BR64[:], rhs=g_tm[:, t, :], start=True, stop=True),
                     reads=[g_tmb, k.cstb], writes=[PB[1]])
                P.op(P.dve, lambda e: e.tensor_copy(out=tR[:], in_=k_tm[:, t, :]), reads=[k_tmb], writes=[tRb])
                P.op(P.pe, lambda e: e.transpose(out=PS[2][:, 0:128], in_=tR[:], identity=k.ident[:]),
                     reads=[tRb, k.identb], writes=[PB[2]])
                for u in range(2):
                    cu = slice(64 * u, 64 * u + 64)
                    mid = 64 * u + 31
                    P.op(P.dve, lambda e: e.tensor_scalar(out=rcol[:, 2 * u:2 * u + 1], in0=PS[0][:, mid:mid + 1], scalar1=1.0, scalar2=None, op0=ALU.mult), reads=[PB[0]], writes=[rcolb])
                    P.op(P.dve, lambda e: e.tensor_scalar(out=rcol[:, 2 * u + 1:2 * u + 2], in0=PS[0][:, mid:mid + 1], scalar1=-1.0, scalar2=None,
                                                          op0=ALU.mult), reads=[PB[0]], writes=[rcolb])
                    P.op(P.act, lambda e: e.activation(out=tE[:, cu], in_=PS[0][:, cu], func=AF.Exp, bias=rcol[:, 2 * u + 1:2 * u + 2]),
                         reads=[PB[0], rcolb], writes=[tEb])
                    P.op(P.act, lambda e: e.activation(out=tK[:, cu], in_=PS[0][:, cu], func=AF.Exp, bias=rcol[:, 2 * u:2 * u + 1], scale=-1.0),
                         reads=[PB[0], rcolb], writes=[tKb])
                P.op(P.act, lambda e: e.activation(out=tG[:], in_=PS[0][:, 0:128], func=AF.Exp), reads=[PB[0]], writes=[tGb])
                P.op(P.act, lambda e: e.activation(out=tR[:], in_=PS[1][:, 0:128], func=AF.Exp), reads=[PB[1]], writes=[tRb])
                P.op(P.dve, lambda e: e.tensor_tensor(out=qp[:], in0=qbs[:, tc_], in1=tE[:], op=ALU.mult), reads=[qbsb, tEb], writes=[qpb])
                P.op(P.dve, lambda e: e.tensor_tensor(out=kp[:], in0=PS[2][:, 0:128], in1=tK[:], op=ALU.mult), reads=[PB[2], tKb], writes=[kpb])
                P.op(P.dve, lambda e: e.tensor_tensor(out=qpp[:], in0=qbs[:, tc_], in1=tG[:], op=ALU.mult), reads=[qbsb, tGb], writes=[qppb])
                P.op(P.dve, lambda e: e.tensor_tensor(out=kpp[:], in0=k_tm[:, t, :], in1=tR[:], op=ALU.mult), reads=[k_tmb, tRb], writes=[kppb])
                P.op(P.pe, lambda e: e.matmul(PS[3][:, 0:128], lhsT=kp[:], rhs=qp[:], start=True, stop=True),
                     reads=[kpb, qpb], writes=[PB[3]])
                P.op(P.dve, lambda e: e.tensor_tensor(out=ATm[:], in0=PS[3][:, 0:128], in1=k.BU64[:], op=ALU.mult),
                     reads=[PB[3], k.cstb], writes=[ATmb])
                P.op(P.pe, lambda e: e.matmul(PS[4][:, 0:128], lhsT=ATm[:], rhs=i_bf[:, t, :], start=True, stop=False),
                     reads=[ATmb, i_bfb], writes=[PB[4]], signal=False)
                for u in range(2):
                    r0, r1 = 64 * u, 64 * u + 64
                    if smp:
                        Su, Sub, Sbu, Sbub = ((S0, S0b, Sb0, Sb0b), (S1, S1b, Sb1, Sb1b))[u]
                    else:
                        Su, Sub, Sbu, Sbub = S0, S0b, Sb0, Sb0b
                    P.op(P.pe, lambda e: e.matmul(PS[4][r0:r1, 0:128], lhsT=qpp[:, r0:r1], rhs=Sbu[:], start=False, stop=(u == 1)),
                         reads=[qppb, Sbub], writes=[PB[4]], signal=(u == 1))
                    P.op(P.pe, lambda e: e.matmul(PS[5][:, 0:128], lhsT=kpp[r0:r1, :], rhs=i_bf[r0:r1, t, :], start=True, stop=True),
                         reads=[kppb, i_bfb], writes=[PB[5]])
                    P.op(P.dve, lambda e: e.scalar_tensor_tensor(out=Su[:], in0=Su[:], scalar=tG[:, r1 - 1:r1], in1=PS[5][:, 0:128],
                                                                 op0=ALU.mult, op1=ALU.add),
                         reads=[Sub, tGb, PB[5]], writes=[Sub])
                    if smp:
                        P.dma(P.sp, io.hgrn_s[j, 2 * c + u, h, :, :], Su[:], Sub, reads=[Sub])
                    else:
                        P.op(P.act, lambda e: e.activation(out=Sbu[:], in_=Su[:], func=AF.Copy), reads=[Sub], writes=[Sbub])
                ss = stat[:, 4:5]
                rs = stat[:, 5:6]
                P.op(P.act, lambda e: e.activation(out=on[:], in_=PS[4][:, 0:128], func=AF.Square, accum_out=ss),
                     reads=[PB[4]], writes=[onb, k.statb])
                P.op(P.dve, lambda e: e.tensor_scalar(out=rs, in0=ss, scalar1=1.0 / HD, scalar2=EPS, op0=ALU.mult, op1=ALU.add),
                     reads=[k.statb], writes=[k.statb])
                P.op(P.act, lambda e: e.activation(out=rs, in_=rs, func=AF.Sqrt), reads=[k.statb], writes=[k.statb])
                P.op(P.dve, lambda e: e.reciprocal(out=rs, in_=rs), reads=[k.statb], writes=[k.statb])
                P.op(P.dve, lambda e: e.scalar_tensor_tensor(out=on[:], in0=PS[4][:, 0:128], scalar=rs, in1=gnb[:], op0=ALU.mult, op1=ALU.mult),
                     reads=[PB[4], k.statb, gnbb], writes=[onb])
                P.op(P.dve, lambda e: e.tensor_tensor(out=on[:], in0=on[:], in1=gate[:, t, :], op=ALU.mult), reads=[onb, gateb], writes=[onb])
                P.op(P.pe, lambda e: e.transpose(out=PS[6][:, 0:128], in_=on[:], identity=k.ident[:]), reads=[onb, k.identb], writes=[PB[6]])
                P.op(P.act, lambda e: e.activation(out=obT[:, tc_], in_=PS[6][:, 0:128], func=AF.Copy), reads=[PB[6]], writes=[obTb])
                if t == 7:
                    if c < NCHUNK - 1:
                        P.dma(P.sp, io.sth[j, h, :, :], S0[:], sthb, reads=[S0b], writes=[sthb])
                    else:
                        P.dma(P.sp, io.hgrn_p[j, h, :, :], S0[:], S0b, reads=[S0b])
            dcnt = 0
            for hf in range(2 if DBG['outp'] else 0):
                cs = slice(hf * 1024, (hf + 1) * 1024)
                P.dma(P.pool, wo[:, 0, :], w_out[h * 128:(h + 1) * 128, cs], wob, writes=[wob])
                P.dma(P.pool, wo[:, 1, :], w_out[1024 + h * 128:1024 + (h + 1) * 128, cs], wob, writes=[wob])
                for t in range(NTL):
                    for q in range(2):
                        pd = dcnt % 4
                        dcnt += 1
                        xs_ = slice(hf * 1024 + q * 512, hf * 1024 + (q + 1) * 512)
                        P.op(P.pe, lambda e: e.matmul(PS[pd][:, :], lhsT=oaT[:, t * 128:(t + 1) * 128], rhs=wo[:, 0, q * 512:(q + 1) * 512],
                                                      start=True, stop=False),
                             reads=[oaTb, wob], writes=[PB[pd]], signal=False)
                        P.op(P.pe, lambda e: e.matmul(PS[pd][:, :], lhsT=obT[:, t * 128:(t + 1) * 128], rhs=wo[:, 1, q * 512:(q + 1) * 512],
                                                      start=False, stop=True),
                             reads=[obTb, wob], writes=[PB[pd]])
                        P.op(P.dve, lambda e: e.tensor_tensor(out=k.x[:, t, xs_], in0=PS[pd][:, :], in1=k.x[:, t, xs_], op=ALU.add),
                             reads=[PB[pd], k.xb[t]], writes=[k.xb[t]])
        P.barrier()


def final_norm_out(P, k, nc, gam_final, y_p, y_s, ch):
    with ExitStack() as es:
        gb = es.enter_context(nc.sbuf_tensor(U("f_gb"), [128, D], F32))
        gbb = P.buf("f_gb")
        yt = [es.enter_context(nc.sbuf_tensor(U(f"f_y{i}"), [128, D], F32)) for i in range(2)]
        ytb = [P.buf(f"f_y{i}") for i in range(2)]
        P.dma(P.sp, gb[:], gam_final.partition_broadcast(128), gbb, writes=[gbb])
        for t in range(ch.ntile):
            i = t % 2
            ss = k.small[:, 0:1]
            rs = k.small[:, 1:2]
            P.op(P.act, lambda e: e.activation(out=yt[i][:], in_=k.x[:, t, :], func=AF.Square, accum_out=ss),
                 reads=[k.xb[t]], writes=[ytb[i], k.ssb])
            P.op(P.dve, lambda e: e.tensor_scalar(out=rs, in0=ss, scalar1=1.0 / D, scalar2=EPS, op0=ALU.mult, op1=ALU.add),
                 reads=[k.ssb], writes=[k.rsb])
            P.op(P.act, lambda e: e.activation(out=rs, in_=rs, func=AF.Sqrt), reads=[k.rsb], writes=[k.rsb])
            P.op(P.dve, lambda e: e.reciprocal(out=rs, in_=rs), reads=[k.rsb], writes=[k.rsb])
            P.op(P.dve, lambda e: e.scalar_tensor_tensor(out=yt[i][:], in0=k.x[:, t, :], scalar=rs, in1=gb[:],
                                                         op0=ALU.mult, op1=ALU.mult),
                 reads=[k.xb[t], k.rsb, gbb], writes=[ytb[i]])
            if ch.has_sample and t == 8:
                dst = y_s[ch.c * 128:(ch.c + 1) * 128, :]
            else:
                dst = y_p[ch.p0 + t * 128:ch.p0 + (t + 1) * 128, :]
            P.dma(P.sp, dst, yt[i][:], ytb[i], reads=[ytb[i]])
        P.barrier()


def load_x(P, k, nc, x_p, x_s, ch):
    for t in range(ch.ntile):
        if ch.has_sample and t == 8:
            src = x_s[ch.c * 128:(ch.c + 1) * 128, :]
        else:
            src = x_p[ch.p0 + t * 128:ch.p0 + (t + 1) * 128, :]
        P.dma(P.sp, k.x[:, t, :], src, k.xb[t], writes=[k.xb[t]])


def load_consts(P, k, nc, gam):
    make_identity(P, k, nc)
    with nc.allow_non_contiguous_dma(reason="tiny gamma load"):
        P.dma(P.sp, k.gam[:], gam.rearrange("g (c p) -> p g c", p=128), k.gamb, writes=[k.gamb])


IN_SPECS = [
    ("x_p", [SEQ, D]), ("x_s", [256, D]),
    ("cache_k", [2, 4, H, SEQ, HD]), ("cache_v", [2, 4, H, SEQ, HD]), ("cache_logf", [2, 4, H, SEQ]),
    ("state_hgrn", [2, 4, H, HD, HD]),
    ("gam", [13, D]),
    ("ffn1_gate", [4, D, DFF]), ("ffn1_up", [4, D, DFF]), ("ffn1_down", [4, DFF, D]),
    ("ffn2_gate", [4, D, DFF]), ("ffn2_up", [4, D, DFF]), ("ffn2_down", [4, DFF, D]),
    ("ab_w_in", [2, D, AB_IN]), ("ab_b_f", [2, H]), ("hgrn_lb", [2, 1024]), ("hgrn_gnorm", [2, HD]),
    ("ab_w_out", [2, D, D]),
    ("c_w_in", [2, D, 2 * D]), ("c_ln_g", [2, D]), ("c_ln_b", [2, D]), ("c_w_s", [2, 8, 128, 128]),
    ("c_b_s", [2, 8, 128]), ("c_w_out", [2, D, D]),
]
OUT_SPECS = [
    ("y_p", [SEQ, D]), ("y_s", [256, D]),
    ("k_p", [2, H, SEQ, HD]), ("v_p", [2, H, SEQ, HD]), ("logf_p", [2, H, SEQ]), ("hgrn_p", [2, H, HD, HD]),
    ("k_s", [2, 4, H, 64, HD]), ("v_s", [2, 4, H, 64, HD]), ("logf_s", [2, 4, H, 64]), ("hgrn_s", [2, 4, H, HD, HD]),
    ("cv_s", [2, 256, D]),
]


def build(layers=(0, 1, 2, 3), do_ffn=True, chunks=(0, 1, 2, 3)):
    nc = bass.Bass("TRN2", target_bir_lowering=False)
    io = K()
    for n, shp in IN_SPECS:
        setattr(io, n, nc.dram_tensor(n, shp, F32, kind="ExternalInput").ap())
    for n, shp in OUT_SPECS:
        setattr(io, n, nc.dram_tensor(n, shp, F32, kind="ExternalOutput").ap())
    io.kth = nc.dram_tensor("kth", [2, H, HD, SEQ], BF16).ap()
    io.vth = nc.dram_tensor("vth", [2, H, SEQ, HD], BF16).ap()
    io.sth = nc.dram_tensor("sth", [2, H, HD, HD], F32).ap()
    with ExitStack() as es:
        P = Prog(nc, es)
        k = K()
        setup_persistent(P, k, es, nc)
        load_consts(P, k, nc, io.gam)
        for c in chunks:
            ch = Chunk(c)
            load_x(P, k, nc, io.x_p, io.x_s, ch)
            for l in layers:
                j = l // 2
                if do_ffn:
                    ffn_phase(P, k, nc, io.ffn1_gate[l], io.ffn1_up[l], io.ffn1_down[l], l, ch)
                if l % 2 == 0:
                    abmix(P, k, nc, j, ch, io, 4 + l)
                else:
                    cmix(P, k, nc, io.c_w_in[j], io.c_ln_g[j], io.c_ln_b[j], io.c_w_s[j], io.c_b_s[j], io.c_w_out[j],
                         io.cv_s[j], 4 + l, ch)
                if do_ffn:
                    ffn_phase(P, k, nc, io.ffn2_gate[l], io.ffn2_up[l], io.ffn2_down[l], 8 + l, ch)
            final_norm_out(P, k, nc, io.gam[12, :], io.y_p, io.y_s, ch)
        P.finish()
    return nc


def make_in_maps(inp):
    f = lambda a: np.ascontiguousarray(a, dtype=np.float32)
    gam = np.concatenate([inp["norm_ffn1"], inp["norm_mix"], inp["norm_ffn2"], inp["norm_final"][None, :]], axis=0)
    shared = {n: f(inp[n]) for n, _ in IN_SPECS if n in inp}
    shared["gam"] = f(gam)
    maps = []
    for core in range(NCORE):
        m = dict(shared)
        m["x_p"] = f(inp["x_prompt"][core % 2])
        sl = slice(4 * core, 4 * core + 4)
        m["x_s"] = f(inp["x_sample"][sl].reshape(256, D))
        m["cache_k"] = f(inp["cache_k"][:, sl])
        m["cache_v"] = f(inp["cache_v"][:, sl])
        m["cache_logf"] = f(inp["cache_logf"][:, sl])
        m["state_hgrn"] = f(inp["state_hgrn"][:, sl])
        maps.append(m)
    return maps


def gather_outputs(res):
    r = res
    cat = lambda n, ax: np.concatenate([r[c][n] for c in range(NCORE)], axis=ax)
    y_prompt = np.stack([r[0]["y_p"], r[1]["y_p"]])
    y_sample = np.concatenate([r[c]["y_s"].reshape(4, 64, D) for c in range(NCORE)], axis=0)
    stk = lambda n: np.stack([r[0][n], r[1][n]], axis=1)
    return (y_prompt, y_sample, stk("k_p"), stk("v_p"), stk("logf_p"), stk("hgrn_p"),
            cat("k_s", 1), cat("v_s", 1), cat("logf_s", 1), cat("hgrn_s", 1),
            np.concatenate([r[c]["cv_s"].reshape(2, 4, 64, D) for c in range(NCORE)], axis=1))


def kernel(**inputs):
    inp = {k_: np.asarray(v) for k_, v in inputs.items()}
    nc = build()
    maps = make_in_maps(inp)
    res = run_bass_kernel_spmd(nc, maps, core_ids=list(range(NCORE)))
    outs = gather_outputs(res.results)
    return tuple(np.ascontiguousarray(o, dtype=np.float32) for o in outs)
```
